# Optimizing a Trainium2 kernel written in Bass

```python
import math
import jax
import jax.numpy as jnp
from jax import lax
import numpy as np

D_MODEL = 2048
BATCH = 4
SEQ = 8192
DEPTH = 4

GRID_W = 64
CTX_LEN = 256
N_MIXERS = 3
N_MOD = 9
RMS_EPS = 1e-6
D_FF = 5632

SSM_INNER = 2 * D_MODEL
SSM_HEAD_DIM = 64
SSM_HEADS = SSM_INNER // SSM_HEAD_DIM
SSM_GROUPS = 8
SSM_STATE = 128
SSM_CONV_W = 7
SSM_CHUNK = 128
SSM_GN = SSM_GROUPS * SSM_STATE
SSM_CONV_DIM = SSM_INNER + 2 * SSM_GN
SSM_PROJ = SSM_INNER + SSM_CONV_DIM + 2 * SSM_HEADS

POOL_WINDOWS = (2, 4, 8, 16)
POOL_GROUPS = 4
POOL_GROUP_DIM = D_MODEL // POOL_GROUPS

HEAD_DIM = 128
N_HEADS = D_MODEL // HEAD_DIM
N_KV_HEADS = 4
Q_PER_KV = N_HEADS // N_KV_HEADS
Q_DIM = N_HEADS * HEAD_DIM
KV_DIM = N_KV_HEADS * HEAD_DIM
WINDOW = 128
ATT_BLOCK = 128
ATT_SCALE = HEAD_DIM ** -0.5
ROPE_BASE = 10000.0
ROPE_AXIS_DIM = HEAD_DIM // 2

N_SSM_LAYERS = (DEPTH + 2) // 3
N_POOL_LAYERS = (DEPTH + 1) // 3
N_ATTN_LAYERS = DEPTH // 3

kernel_name = 'hybrid_ssd_pool_swa_diffusion_trunk'


def rmsnorm(h, g):
    hf = h.astype(jnp.float32)
    hf = hf * lax.rsqrt(jnp.mean(hf * hf, axis=-1, keepdims=True) + RMS_EPS)
    return (hf * g.astype(jnp.float32)).astype(h.dtype)


def adaln_params(cond, w, b):
    m = jnp.einsum('...d,de->...e', jax.nn.silu(cond), w) + b
    return m.reshape(m.shape[:-1] + (N_MOD, D_MODEL))


def modulate(h, g, m, k):
    return rmsnorm(h, g) * (1.0 + m[..., 3 * k + 1, :]) + m[..., 3 * k, :]


def swiglu(a, w_in, w_out):
    gu = jnp.einsum('bld,df->blf', a, w_in)
    return jnp.einsum('blf,fd->bld', jax.nn.silu(gu[..., :D_FF]) * gu[..., D_FF:], w_out)


def depthwise_conv(t, w, b):
    out = lax.conv_general_dilated(
        t, w[:, None, :].astype(t.dtype), window_strides=(1,),
        padding=[(SSM_CONV_W // 2, SSM_CONV_W // 2)],
        dimension_numbers=('NWC', 'WIO', 'NWC'), feature_group_count=t.shape[-1])
    return out + b.astype(t.dtype)


def ssd_chunked(xs, dt, a, bs, cs, init_state):
    bsz, n = xs.shape[:2]
    nc = n // SSM_CHUNK
    hg = SSM_HEADS // SSM_GROUPS
    shp = (bsz, nc, SSM_CHUNK, SSM_GROUPS, hg)
    xd = (xs.astype(jnp.float32) * dt[..., None]).reshape(shp + (SSM_HEAD_DIM,))
    acs = jnp.cumsum((dt * a).reshape(shp), axis=2)
    bq = bs.reshape(bsz, nc, SSM_CHUNK, SSM_GROUPS, SSM_STATE)
    cq = cs.reshape(bsz, nc, SSM_CHUNK, SSM_GROUPS, SSM_STATE)
    lower = jnp.tril(jnp.ones((SSM_CHUNK, SSM_CHUNK), dtype=bool))[None, None, :, :, None, None]
    seg = acs[:, :, :, None] - acs[:, :, None, :]
    decay = jnp.exp(jnp.where(lower, seg, -jnp.inf))
    cb = jnp.einsum('bclgn,bcsgn->bclsg', cq, bq)
    y_diag = jnp.einsum('bclsgh,bcsghp->bclghp', cb[..., None] * decay, xd)
    decay_to_end = jnp.exp(acs[:, :, -1:] - acs)
    chunk_states = jnp.einsum('bclgn,bclghp->bcghpn', bq, xd * decay_to_end[..., None])
    chunk_decay = jnp.exp(acs[:, :, -1])

    def step(state, inp):
        dec, st = inp
        return state * dec[..., None, None] + st, state

    final, states_in = lax.scan(
        step, init_state.reshape(bsz, SSM_GROUPS, hg, SSM_HEAD_DIM, SSM_STATE),
        (jnp.moveaxis(chunk_decay, 1, 0), jnp.moveaxis(chunk_states, 1, 0)))
    states_in = jnp.moveaxis(states_in, 0, 1)
    y_off = jnp.einsum('bclgn,bcghpn->bclghp', cq, states_in) * jnp.exp(acs)[..., None]
    y = (y_diag + y_off).reshape(bsz, n, SSM_HEADS, SSM_HEAD_DIM)
    return y, final.reshape(bsz, SSM_HEADS, SSM_HEAD_DIM, SSM_STATE)


def gated_group_rmsnorm(y, z, g):
    u = y * jax.nn.silu(z.astype(jnp.float32))
    shp = u.shape
    u = u.reshape(shp[:-1] + (SSM_GROUPS, SSM_INNER // SSM_GROUPS))
    u = u * lax.rsqrt(jnp.mean(u * u, axis=-1, keepdims=True) + RMS_EPS)
    return u.reshape(shp) * g.astype(jnp.float32)


def mamba_mixer(a_lat, a_ctx, w_in, conv_w, conv_b, dt_bias, a_log, d_skip, gn_g, w_out, need_ctx):
    def project(a):
        bsz, n = a.shape[:2]
        zxbcdt = jnp.einsum('bld,de->ble', a, w_in)
        z = zxbcdt[..., :SSM_INNER]
        xbc = jax.nn.silu(depthwise_conv(zxbcdt[..., SSM_INNER:SSM_INNER + SSM_CONV_DIM], conv_w, conv_b))
        dtr = zxbcdt[..., SSM_INNER + SSM_CONV_DIM:]
        xs = xbc[..., :SSM_INNER].reshape(bsz, n, SSM_HEADS, SSM_HEAD_DIM)
        bs = xbc[..., SSM_INNER:SSM_INNER + SSM_GN].reshape(bsz, n, SSM_GROUPS, SSM_STATE)
        cs = xbc[..., SSM_INNER + SSM_GN:].reshape(bsz, n, SSM_GROUPS, SSM_STATE)
        dt = jax.nn.softplus(dtr.reshape(bsz, n, 2, SSM_HEADS).astype(jnp.float32)
                             + dt_bias.astype(jnp.float32))
        return z, xs, bs, cs, dt

    zc, xc, bc, cc, dtc = project(a_ctx)
    zl, xl, bl, cl, dtl = project(a_lat)
    a = -jnp.exp(a_log.astype(jnp.float32))
    flip = lambda t: jnp.flip(t, axis=1)
    zero = jnp.zeros((a_lat.shape[0], SSM_HEADS, SSM_HEAD_DIM, SSM_STATE), jnp.float32)
    yc_f, sc_f = ssd_chunked(xc, dtc[:, :, 0], a[0], bc, cc, zero)
    yl_f, _ = ssd_chunked(xl, dtl[:, :, 0], a[0], bl, cl, sc_f)
    yc_b, sc_b = ssd_chunked(flip(xc), flip(dtc[:, :, 1]), a[1], flip(bc), flip(cc), zero)
    yl_b, _ = ssd_chunked(flip(xl), flip(dtl[:, :, 1]), a[1], flip(bl), flip(cl), sc_b)

    def finish(y_f, y_b_rev, xs, z):
        bsz, n = xs.shape[:2]
        y = y_f + flip(y_b_rev) + d_skip.astype(jnp.float32)[:, None] * xs.astype(jnp.float32)
        y = gated_group_rmsnorm(y.reshape(bsz, n, SSM_INNER), z, gn_g)
        return jnp.einsum('ble,ed->bld', y.astype(w_out.dtype), w_out)

    out_lat = finish(yl_f, yl_b, xl, zl).astype(a_lat.dtype)
    out_ctx = finish(yc_f, yc_b, xc, zc).astype(a_ctx.dtype) if need_ctx else None
    return out_lat, out_ctx


def multiscale_pool(a, w_grp, scale):
    bsz, n, _ = a.shape
    af = a.astype(jnp.float32)
    cs = jnp.concatenate([jnp.zeros((bsz, 1, D_MODEL), jnp.float32), jnp.cumsum(af, axis=1)], axis=1)
    t = jnp.arange(n)
    outs = []
    for g, w in enumerate(POOL_WINDOWS):
        lo = jnp.clip(t - w // 2, 0, n)
        hi = jnp.clip(t - w // 2 + w, 0, n)
        sl = slice(g * POOL_GROUP_DIM, (g + 1) * POOL_GROUP_DIM)
        win_sum = cs[:, hi, sl] - cs[:, lo, sl]
        outs.append(win_sum / (hi - lo).astype(jnp.float32)[:, None] - af[:, :, sl])
    pooled = jnp.stack(outs, axis=2)
    y = jnp.einsum('blgc,gce->blge', pooled, w_grp.astype(jnp.float32)).reshape(bsz, n, D_MODEL)
    return (y * scale.astype(jnp.float32)).astype(a.dtype)


def axial_rope_tables(n_tokens):
    rows = n_tokens // GRID_W
    row = jnp.repeat(jnp.arange(rows, dtype=jnp.int32), GRID_W)
    col = jnp.tile(jnp.arange(GRID_W, dtype=jnp.int32), rows)
    inv_freq = ROPE_BASE ** (-jnp.arange(0, ROPE_AXIS_DIM, 2, dtype=jnp.float32) / ROPE_AXIS_DIM)
    ang = jnp.concatenate([row.astype(jnp.float32)[:, None] * inv_freq,
                           col.astype(jnp.float32)[:, None] * inv_freq], axis=-1)
    return jnp.cos(ang), jnp.sin(ang)


def apply_rope(x, cos, sin):
    xp = x.astype(jnp.float32).reshape(x.shape[:-1] + (HEAD_DIM // 2, 2))
    x0, x1 = xp[..., 0], xp[..., 1]
    bshape = (x.shape[1],) + (1,) * (x.ndim - 3) + (HEAD_DIM // 2,)
    c, s = cos.reshape(bshape), sin.reshape(bshape)
    out = jnp.stack([x0 * c - x1 * s, x0 * s + x1 * c], axis=-1).reshape(x.shape)
    return out.astype(x.dtype)


def banded_gqa(q, k, v, k_ctx, v_ctx, sink):
    bsz, n = q.shape[:2]
    nb = n // ATT_BLOCK
    qb = q.reshape(bsz, nb, ATT_BLOCK, N_KV_HEADS, Q_PER_KV, HEAD_DIM)

    def band(t):
        tp = jnp.pad(t, ((0, 0), (ATT_BLOCK, ATT_BLOCK), (0, 0), (0, 0)))
        tp = tp.reshape(bsz, nb + 2, ATT_BLOCK, N_KV_HEADS, HEAD_DIM)
        return jnp.concatenate([tp[:, :-2], tp[:, 1:-1], tp[:, 2:]], axis=2)

    kb, vb = band(k), band(v)
    s_loc = jnp.einsum('bnqkgd,bnskd->bnkgqs', qb, kb).astype(jnp.float32) * ATT_SCALE
    blk = jnp.arange(nb)[:, None]
    q_pos = blk * ATT_BLOCK + jnp.arange(ATT_BLOCK)[None]
    k_pos = (blk - 1) * ATT_BLOCK + jnp.arange(3 * ATT_BLOCK)[None]
    rel = k_pos[:, None, :] - q_pos[:, :, None]
    valid = (jnp.abs(rel) <= WINDOW) & (k_pos[:, None, :] >= 0) & (k_pos[:, None, :] < n)
    s_loc = jnp.where(valid[None, :, None, None], s_loc, -jnp.inf)
    s_ctx = jnp.einsum('bnqkgd,bckd->bnkgqc', qb, k_ctx).astype(jnp.float32) * ATT_SCALE
    s_sink = jnp.broadcast_to(sink.astype(jnp.float32)[None, None, :, :, None, None], s_loc.shape[:-1] + (1,))
    p = jax.nn.softmax(jnp.concatenate([s_sink, s_ctx, s_loc], axis=-1), axis=-1)
    n_ctx = k_ctx.shape[1]
    p_ctx = p[..., 1:1 + n_ctx].astype(v.dtype)
    p_loc = p[..., 1 + n_ctx:].astype(v.dtype)
    o = (jnp.einsum('bnkgqc,bckd->bnqkgd', p_ctx, v_ctx)
         + jnp.einsum('bnkgqs,bnskd->bnqkgd', p_loc, vb))
    return o.reshape(bsz, n, Q_DIM)


def context_gqa(q, k, v, sink):
    s = jnp.einsum('bqkgd,bckd->bkgqc', q, k).astype(jnp.float32) * ATT_SCALE
    s0 = jnp.broadcast_to(sink.astype(jnp.float32)[None, :, :, None, None], s.shape[:-1] + (1,))
    p = jax.nn.softmax(jnp.concatenate([s0, s], axis=-1), axis=-1)[..., 1:]
    o = jnp.einsum('bkgqc,bckd->bqkgd', p.astype(v.dtype), v)
    return o.reshape(o.shape[0], o.shape[1], Q_DIM)


def window_attention_mixer(a_lat, a_ctx, w_qkv, sink, w_o, cos, sin, need_ctx):
    def project(a):
        bsz, n = a.shape[:2]
        qkv = jnp.einsum('bld,de->ble', a, w_qkv)
        q = qkv[..., :Q_DIM].reshape(bsz, n, N_KV_HEADS, Q_PER_KV, HEAD_DIM)
        k = qkv[..., Q_DIM:Q_DIM + KV_DIM].reshape(bsz, n, N_KV_HEADS, HEAD_DIM)
        v = qkv[..., Q_DIM + KV_DIM:].reshape(bsz, n, N_KV_HEADS, HEAD_DIM)
        return q, k, v

    q_l, k_l, v_l = project(a_lat)
    q_c, k_c, v_c = project(a_ctx)
    q_l = apply_rope(q_l, cos, sin)
    k_l = apply_rope(k_l, cos, sin)
    sink_g = sink.reshape(N_KV_HEADS, Q_PER_KV)
    o_l = banded_gqa(q_l, k_l, v_l, k_c, v_c, sink_g)
    y_l = jnp.einsum('ble,ed->bld', o_l, w_o).astype(a_lat.dtype)
    y_c = None
    if need_ctx:
        y_c = jnp.einsum('ble,ed->bld', context_gqa(q_c, k_c, v_c, sink_g), w_o).astype(a_ctx.dtype)
    return y_l, y_c


def setup_inputs(seed: int = 0) -> dict:
    key = jax.random.key(seed)
    ks = jax.random.split(key, 24)
    f32 = jnp.float32

    def nrm(k, shape, s):
        return jax.random.normal(k, shape, f32) * s

    x = nrm(ks[0], (BATCH, SEQ, D_MODEL), 1.0)
    c = nrm(ks[1], (BATCH, D_MODEL), 1.0)
    ctx = nrm(ks[2], (BATCH, CTX_LEN, D_MODEL), 1.0)
    c_ctx = nrm(ks[3], (D_MODEL,), 1.0)
    ada_w = nrm(ks[4], (DEPTH, D_MODEL, N_MOD * D_MODEL), 0.5 * D_MODEL ** -0.5)
    ada_b = nrm(ks[5], (DEPTH, N_MOD * D_MODEL), 0.02)
    norm_g = 1.0 + nrm(ks[6], (DEPTH, 3, D_MODEL), 0.1)
    ffn_w_in = nrm(ks[7], (DEPTH, 2, D_MODEL, 2 * D_FF), D_MODEL ** -0.5)
    ffn_w_out = nrm(ks[8], (DEPTH, 2, D_FF, D_MODEL), D_FF ** -0.5)
    ssm_w_in = nrm(ks[9], (N_SSM_LAYERS, D_MODEL, SSM_PROJ), D_MODEL ** -0.5)
    ssm_conv_w = nrm(ks[10], (N_SSM_LAYERS, SSM_CONV_W, SSM_CONV_DIM), SSM_CONV_W ** -0.5)
    ssm_conv_b = nrm(ks[11], (N_SSM_LAYERS, SSM_CONV_DIM), 0.02)
    dt0 = jnp.exp(jax.random.uniform(ks[12], (N_SSM_LAYERS, 2, SSM_HEADS), f32,
                                     math.log(1e-3), math.log(1e-1)))
    ssm_dt_bias = dt0 + jnp.log(-jnp.expm1(-dt0))
    ssm_a_log = jnp.log(jax.random.uniform(ks[13], (N_SSM_LAYERS, 2, SSM_HEADS), f32, 1.0, 16.0))
    ssm_d = 1.0 + nrm(ks[14], (N_SSM_LAYERS, SSM_HEADS), 0.1)
    ssm_norm_g = 1.0 + nrm(ks[15], (N_SSM_LAYERS, SSM_INNER), 0.1)
    ssm_w_out = nrm(ks[16], (N_SSM_LAYERS, SSM_INNER, D_MODEL), SSM_INNER ** -0.5)
    pool_w = nrm(ks[17], (N_POOL_LAYERS, POOL_GROUPS, POOL_GROUP_DIM, POOL_GROUP_DIM), POOL_GROUP_DIM ** -0.5)
    pool_scale = 1.0 + nrm(ks[18], (N_POOL_LAYERS, D_MODEL), 0.1)
    attn_w_qkv = nrm(ks[19], (N_ATTN_LAYERS, D_MODEL, Q_DIM + 2 * KV_DIM), D_MODEL ** -0.5)
    attn_sink = nrm(ks[20], (N_ATTN_LAYERS, N_HEADS), 0.5)
    attn_w_o = nrm(ks[21], (N_ATTN_LAYERS, Q_DIM, D_MODEL), Q_DIM ** -0.5)
    final_g = 1.0 + nrm(ks[22], (D_MODEL,), 0.1)
    return {'x': x, 'c': c, 'ctx': ctx, 'c_ctx': c_ctx, 'ada_w': ada_w, 'ada_b': ada_b,
            'norm_g': norm_g, 'ffn_w_in': ffn_w_in, 'ffn_w_out': ffn_w_out,
            'ssm_w_in': ssm_w_in, 'ssm_conv_w': ssm_conv_w, 'ssm_conv_b': ssm_conv_b,
            'ssm_dt_bias': ssm_dt_bias, 'ssm_a_log': ssm_a_log, 'ssm_d': ssm_d,
            'ssm_norm_g': ssm_norm_g, 'ssm_w_out': ssm_w_out, 'pool_w': pool_w,
            'pool_scale': pool_scale, 'attn_w_qkv': attn_w_qkv, 'attn_sink': attn_sink,
            'attn_w_o': attn_w_o, 'final_g': final_g}


def reference(x, c, ctx, c_ctx, ada_w, ada_b, norm_g, ffn_w_in, ffn_w_out,
              ssm_w_in, ssm_conv_w, ssm_conv_b, ssm_dt_bias, ssm_a_log, ssm_d,
              ssm_norm_g, ssm_w_out, pool_w, pool_scale, attn_w_qkv, attn_sink,
              attn_w_o, final_g):
    n_lat = x.shape[1]
    cos, sin = axial_rope_tables(n_lat)
    h, hc = x, ctx
    for i in range(DEPTH):
        kind, j = i % N_MIXERS, i // N_MIXERS
        last = i == DEPTH - 1
        ctx_live = (not last) or kind != 1
        ml = adaln_params(c, ada_w[i], ada_b[i])[:, None]
        mc = adaln_params(c_ctx, ada_w[i], ada_b[i])
        h = h + 0.5 * ml[..., 2, :] * swiglu(modulate(h, norm_g[i, 0], ml, 0), ffn_w_in[i, 0], ffn_w_out[i, 0])
        if ctx_live:
            hc = hc + 0.5 * mc[..., 2, :] * swiglu(modulate(hc, norm_g[i, 0], mc, 0), ffn_w_in[i, 0], ffn_w_out[i, 0])
        a_l = modulate(h, norm_g[i, 1], ml, 1)
        if kind == 0:
            a_c = modulate(hc, norm_g[i, 1], mc, 1)
            y_l, y_c = mamba_mixer(a_l, a_c, ssm_w_in[j], ssm_conv_w[j], ssm_conv_b[j], ssm_dt_bias[j],
                                   ssm_a_log[j], ssm_d[j], ssm_norm_g[j], ssm_w_out[j], not last)
        elif kind == 1:
            y_l = multiscale_pool(a_l, pool_w[j], pool_scale[j])
            y_c = None if last else multiscale_pool(modulate(hc, norm_g[i, 1], mc, 1), pool_w[j], pool_scale[j])
        else:
            a_c = modulate(hc, norm_g[i, 1], mc, 1)
            y_l, y_c = window_attention_mixer(a_l, a_c, attn_w_qkv[j], attn_sink[j], attn_w_o[j], cos, sin, not last)
        h = h + ml[..., 5, :] * y_l
        h = h + 0.5 * ml[..., 8, :] * swiglu(modulate(h, norm_g[i, 2], ml, 2), ffn_w_in[i, 1], ffn_w_out[i, 1])
        if not last:
            hc = hc + mc[..., 5, :] * y_c
            hc = hc + 0.5 * mc[..., 8, :] * swiglu(modulate(hc, norm_g[i, 2], mc, 2), ffn_w_in[i, 1], ffn_w_out[i, 1])
    return rmsnorm(h, final_g)
```

```python
import numpy as np
from contextlib import ExitStack
import concourse.bass as bass
import concourse.mybir as mybir
from concourse.bass_utils import run_bass_kernel_spmd

F32 = mybir.dt.float32
BF16 = mybir.dt.bfloat16
AF = mybir.ActivationFunctionType
ALU = mybir.AluOpType
AX = mybir.AxisListType

D = 2048
NDC = 16
DFF = 5632
NFC = 44
CTX = 256
NMOD = 9
EPS = 1e-6
GRID_W = 64
SI = 4096
SH = 64
SP_ = 64
SG = 8
SN = 128
SCONV = 6144
SPROJ = 10368
HD = 128
NH = 16
NKV = 4
ATT_SCALE = HD ** -0.5
NEG = -30000.0


class Buf:
    __slots__ = ("name", "w", "rs")

    def __init__(self, name):
        self.name = name
        self.w = None
        self.rs = {}


class Prog:
    def __init__(self, nc, es):
        self.nc = nc
        self.es = es
        self.eng = {"pe": nc.tensor, "act": nc.scalar, "dve": nc.vector, "pool": nc.gpsimd, "sp": nc.sync}
        self.sem = {}
        self.cnt = {}
        for e in ("pe", "act", "dve", "pool"):
            self.sem[e] = es.enter_context(nc.semaphore("s_" + e))
            self.cnt[e] = 0
        self.waited = {e: {} for e in self.eng}
        self.dq = {}
        for q in ("sp", "pool", "act"):
            sems = [es.enter_context(nc.semaphore("d_%s%d" % (q, i))) for i in range(8)]
            self.dq[q] = {"sems": sems, "cnt": [0] * 8, "i": 0}
        self.nbuf = 0

    def uid(self):
        self.nbuf += 1
        return self.nbuf

    def buf(self, name=None):
        self.nbuf += 1
        return Buf(name or ("b%d" % self.nbuf))

    def _wait(self, e, tok):
        if tok is None:
            return
        sem, val, owner = tok
        if owner == e and e == "pe":
            return
        key = id(sem)
        if self.waited[e].get(key, 0) >= val:
            return
        self.waited[e][key] = val
        self.eng[e].wait_ge(sem, val)

    def _deps(self, e, reads, writes):
        for b in reads:
            self._wait(e, b.w)
        for b in writes:
            self._wait(e, b.w)
            for t in list(b.rs.values()):
                self._wait(e, t)

    def _commit(self, tok, reads, writes):
        for b in writes:
            b.w = tok
            b.rs = {}
        for b in reads:
            key = id(tok[0])
            old = b.rs.get(key)
            if old is None or old[1] < tok[1]:
                b.rs[key] = tok

    def op(self, e, fn, reads=(), writes=(), mark=True):
        self._deps(e, reads, writes)
        ins = fn()
        if mark:
            self.cnt[e] += 1
            ins.then_inc(self.sem[e], 1)
            tok = (self.sem[e], self.cnt[e], e)
            self._commit(tok, reads, writes)
        return ins

    def mm_group(self, mms, reads, writes):
        self._deps("pe", reads, writes)
        ins = None
        for f in mms:
            ins = f()
        self.cnt["pe"] += 1
        ins.then_inc(self.sem["pe"], 1)
        tok = (self.sem["pe"], self.cnt["pe"], "pe")
        self._commit(tok, reads, writes)

    def dma(self, q, out, in_, reads=(), writes=(), **kw):
        self._deps(q, reads, writes)
        d = self.dq[q]
        i = d["i"]
        d["i"] = (i + 1) % 8
        sem = d["sems"][i]
        if d["cnt"][i] > 0:
            self._wait(q, (sem, d["cnt"][i], "dma"))
        d["cnt"][i] += 16
        self.eng[q].dma_start(out=out, in_=in_, **kw).then_inc(sem, 16)
        tok = (sem, d["cnt"][i], "dma")
        self._commit(tok, reads, writes)
        return tok

    def barrier(self):
        toks = [(self.sem[e], self.cnt[e], e) for e in self.cnt if self.cnt[e] > 0]
        for q, d in self.dq.items():
            for sem, c in zip(d["sems"], d["cnt"]):
                if c > 0:
                    toks.append((sem, c, "dma"))
        for e in self.eng:
            for t in toks:
                self._wait(e, t)

    def finish(self, bufs):
        for b in bufs:
            self._wait("sp", b.w)


class Ring:
    def __init__(self, P, tiles):
        self.tiles = tiles
        self.bufs = [P.buf() for _ in tiles]
        self.i = 0

    def next(self):
        i = self.i
        self.i = (i + 1) % len(self.tiles)
        return self.tiles[i], self.bufs[i]


def build_program(L, layers, debug=False):
    nc = bass.Bass("TRN2", target_bir_lowering=False)
    NT = L + CTX
    NL = len(layers)

    def din(name, shape, dt=F32):
        return nc.dram_tensor(name, list(shape), dt, kind="ExternalInput").ap()

    x_d = din("x", [L, D])
    ctx_d = din("ctx", [CTX, D])
    cvec_d = din("cvec", [128, NDC, 2])
    fing_d = din("final_g", [128, NDC])
    has_attn = any(l["kind"] == 2 for l in layers)
    if has_attn:
        cosT_d = din("cosT", [128, L])
        sinT_d = din("sinT", [128, L])
        rot_d = din("rotT", [128, 128])
        qT_d = nc.dram_tensor("qT_s", [128, NH, NT], BF16, kind="Internal").ap()
        kT_d = nc.dram_tensor("kT_s", [128, NKV, NT], BF16, kind="Internal").ap()
        v_d = nc.dram_tensor("v_s", [NT, NKV * HD], BF16, kind="Internal").ap()
    has_ssm = any(l["kind"] == 0 for l in layers)
    if has_ssm:
        def dscr(name, shape, dt):
            return nc.dram_tensor(name, list(shape), dt, kind="Internal").ap()
        xbc_d = dscr("m_xbc", [128, 48, NT], F32)
        dtr_d = dscr("m_dtr", [128, NT], F32)
        z_d = dscr("m_z", [NT, SI], F32)
        xs_d = dscr("m_xs", [NT, SI], BF16)
        BT_d = dscr("m_BT", [128, SG, NT], BF16)
        CT_d = dscr("m_CT", [128, SG, NT], BF16)
        Btm_d = dscr("m_Btm", [NT, SG * SN], BF16)
        dttm_d = dscr("m_dttm", [NT, 128], F32)
        dtAtm_d = dscr("m_dtAtm", [NT, 128], F32)
        acsfm_d = dscr("m_acsfm", [128, NT], F32)
        yf_d = dscr("m_yf", [NT, SI], F32)
        uT_d = dscr("m_uT", [128, 32, NT], BF16)
    out_d = nc.dram_tensor("out", [L, D], F32, kind="ExternalOutput").ap()
    hT_d = nc.dram_tensor("hT", [D, NT], F32, kind=("ExternalOutput" if debug else "Internal")).ap()
    hT_v = hT_d.rearrange("(dc p) t -> p dc t", p=128)
    W = []
    for li, lay in enumerate(layers):
        w = {}
        w["ada_w"] = din("ada_w%d" % li, [D, NMOD * D])
        w["ada_b"] = din("ada_b%d" % li, [128, NMOD * NDC])
        w["norm_g"] = din("norm_g%d" % li, [128, 3, NDC])
        w["ffn_w_in"] = [din("ffn_w_in%d_%d" % (li, k), [D, 2 * DFF]) for k in range(2)]
        w["ffn_w_out"] = [din("ffn_w_out%d_%d" % (li, k), [DFF, D]) for k in range(2)]
        if lay["kind"] == 0:
            w["s_w_in"] = din("ssm_w_in%d" % li, [D, SPROJ])
            w["s_cw"] = din("ssm_cw%d" % li, [128, 48, 7])
            w["s_cb"] = din("ssm_cb%d" % li, [128, 48])
            w["s_dtb"] = din("ssm_dtb%d" % li, [128, 1])
            w["s_alog"] = din("ssm_alog%d" % li, [128, 1])
            w["s_dvec"] = din("ssm_dvec%d" % li, [128, SH])
            w["s_gn"] = din("ssm_gn%d" % li, [128, SI])
            w["s_w_out"] = din("ssm_w_out%d" % li, [SI, D])
        if lay["kind"] == 2:
            w["w_qkv"] = din("attn_w_qkv%d" % li, [D, 3072])
            w["w_o"] = din("attn_w_o%d" % li, [D, D])
            w["sinkb"] = din("attn_sinkb%d" % li, [128, NH])
        if lay["kind"] == 1:
            w["pool_w"] = din("pool_w%d" % li, [4, 512, 512])
            w["pool_scale"] = din("pool_scale%d" % li, [128, NDC])
        W.append(w)

    lay_dbg_k = 1 if layers[0].get("ffn2only") else 0
    lat_groups = [(g * 512, 512, 0, False) for g in range(L // 512)]
    ctx_group = (L, CTX, 1, True)

    es = ExitStack()
    with es:
        P = Prog(nc, es)
        E = es.enter_context

        def sb(name, shape, dt):
            return E(nc.sbuf_tensor("sb_" + name, list(shape), dt))

        ones_bf = sb("ones_bf", [128, 128], BF16)
        ident_f = sb("ident_f", [128, 128], F32)
        b_const = P.buf("const")
        P.op("pool", lambda: nc.gpsimd.memset(ones_bf[:], 1.0), writes=[b_const])
        P.op("pool", lambda: nc.gpsimd.memset(ident_f[:], 0.0), writes=[b_const])
        P.op("pool", lambda: nc.gpsimd.affine_select(out=ident_f[:], in_=ident_f[:], compare_op=ALU.not_equal, fill=1.0,
                                                     base=0, pattern=[[-1, 128]], channel_multiplier=1), writes=[b_const])

        ident_b = sb("ident_b", [128, 128], BF16)
        P.op("pool", lambda: nc.gpsimd.tensor_copy(out=ident_b[:], in_=ident_f[:]), reads=[b_const], writes=[b_const])
        if has_ssm:
            triF = sb("triF", [128, 128], F32)
            triB = sb("triB", [128, 128], F32)
            mbF = sb("mbF", [128, 128], F32)
            mbB = sb("mbB", [128, 128], F32)
            ones_f = sb("ones_f", [128, 128], F32)
            P.op("pool", lambda: nc.gpsimd.memset(ones_f[:], 1.0), writes=[b_const])
            for (t_, v_) in ((triF, 1.0), (triB, 1.0), (mbF, 0.0), (mbB, 0.0)):
                P.op("pool", lambda t_=t_, v_=v_: nc.gpsimd.memset(t_[:], v_), writes=[b_const])
            P.op("pool", lambda: nc.gpsimd.affine_select(out=triF[:], in_=triF[:], compare_op=ALU.is_ge, fill=0.0, base=0,
                                                         pattern=[[1, 128]], channel_multiplier=-1), writes=[b_const])
            P.op("pool", lambda: nc.gpsimd.affine_select(out=triB[:], in_=triB[:], compare_op=ALU.is_ge, fill=0.0, base=0,
                                                         pattern=[[-1, 128]], channel_multiplier=1), writes=[b_const])
            P.op("pool", lambda: nc.gpsimd.affine_select(out=mbF[:], in_=mbF[:], compare_op=ALU.is_ge, fill=NEG, base=0,
                                                         pattern=[[1, 128]], channel_multiplier=-1), writes=[b_const])
            P.op("pool", lambda: nc.gpsimd.affine_select(out=mbB[:], in_=mbB[:], compare_op=ALU.is_ge, fill=NEG, base=0,
                                                         pattern=[[-1, 128]], channel_multiplier=1), writes=[b_const])
        if has_attn:
            maskA = sb("maskA", [128, 2, 512], F32)
            maskB = sb("maskB", [128, 2, 128], F32)
            P.op("pool", lambda: nc.gpsimd.memset(maskA[:], 0.0), writes=[b_const])
            P.op("pool", lambda: nc.gpsimd.memset(maskB[:], 0.0), writes=[b_const])
            P.op("pool", lambda: nc.gpsimd.affine_select(out=maskA[:, 0, 256:384], in_=maskA[:, 0, 256:384], compare_op=ALU.is_ge,
                                                         fill=NEG, base=0, pattern=[[1, 128]], channel_multiplier=-1), writes=[b_const])
            P.op("pool", lambda: nc.gpsimd.memset(maskA[:, 1, 256:384], NEG), writes=[b_const])
            P.op("pool", lambda: nc.gpsimd.affine_select(out=maskB[:, 0, :], in_=maskB[:, 0, :], compare_op=ALU.is_ge,
                                                         fill=NEG, base=0, pattern=[[-1, 128]], channel_multiplier=1), writes=[b_const])
            P.op("pool", lambda: nc.gpsimd.memset(maskB[:, 1, :], NEG), writes=[b_const])
            rot_b = sb("rot_b", [128, 128], BF16)
            rot_f = sb("rot_f", [128, 128], F32)
            b_rot = P.buf("rot")
            P.dma("sp", rot_f[:], rot_d[:, :], writes=[b_rot])
            P.op("dve", lambda: nc.vector.tensor_copy(out=rot_b[:], in_=rot_f[:]), reads=[b_rot], writes=[b_rot])

        cvec = sb("cvec", [128, NDC, 2], F32)
        sc_bf = sb("sc_bf", [128, NDC, 2], BF16)
        modsb = sb("modsb", [128, NL, NMOD * NDC, 2], F32)
        adab = sb("adab", [128, NL, NMOD * NDC], F32)
        normg = sb("normg", [128, NL, 3, NDC], F32)
        sc1 = sb("sc1", [128, NL, 3, NDC, 2], F32)
        gate = sb("gate", [128, NL, 3, NDC, 2], F32)
        fing = sb("fing", [128, NDC], F32)
        b_mod = P.buf("mod")
        b_small = P.buf("small")
        P.dma("sp", cvec[:], cvec_d[:, :, :], writes=[b_small])
        P.dma("sp", fing[:], fing_d[:, :], writes=[b_small])
        for li in range(NL):
            P.dma("sp", adab[:, li, :], W[li]["ada_b"][:, :], writes=[b_small])
            P.dma("sp", normg[:, li, :, :], W[li]["norm_g"][:, :, :], writes=[b_small])
        b_sc = P.buf("sc")
        P.op("act", lambda: nc.scalar.activation(out=sc_bf[:], in_=cvec[:], func=AF.Silu), reads=[b_small], writes=[b_sc])

        ps = [E(nc.psum_tensor("ps%d" % i, [128, 512], F32)) for i in range(8)]
        pb = [P.buf("psb%d" % i) for i in range(8)]

        with ExitStack() as es2:
            adat = [es2.enter_context(nc.sbuf_tensor("sb_adat%d" % i, [128, NDC, 512], BF16)) for i in range(3)]
            aring = Ring(P, adat)
            for li in range(NL):
                aw = W[li]["ada_w"].rearrange("(dc p) e -> p dc e", p=128)
                for et in range(NMOD * D // 512):
                    t, tb = aring.next()
                    P.dma("pool", t[:], aw[:, :, et * 512:(et + 1) * 512], writes=[tb])
                    mms = []
                    for j in range(4):
                        ec = et * 4 + j
                        for dc in range(NDC):
                            mms.append(lambda j=j, ec=ec, dc=dc, t=t: nc.tensor.matmul(
                                ps[0][:, ec * 2:ec * 2 + 2], lhsT=t[:, dc, j * 128:(j + 1) * 128], rhs=sc_bf[:, dc, :],
                                start=(dc == 0), stop=(dc == NDC - 1)))
                    P.mm_group(mms, reads=[tb, b_sc], writes=[pb[0]])
                P.op("dve", lambda li=li: nc.vector.tensor_tensor(
                    out=modsb[:, li, :, :], in0=ps[0][:, 0:NMOD * NDC * 2].rearrange("p (e c) -> p e c", c=2),
                    in1=adab[:, li, :].unsqueeze(2).to_broadcast([128, NMOD * NDC, 2]), op=ALU.add),
                    reads=[pb[0], b_small], writes=[b_mod])
                for k in range(3):
                    P.op("dve", lambda li=li, k=k: nc.vector.tensor_scalar(
                        out=sc1[:, li, k, :, :], in0=modsb[:, li, (3 * k + 1) * NDC:(3 * k + 2) * NDC, :],
                        scalar1=1.0, scalar2=None, op0=ALU.add), reads=[b_mod], writes=[b_mod])
                    P.op("dve", lambda li=li, k=k: nc.vector.tensor_tensor(
                        out=sc1[:, li, k, :, :], in0=sc1[:, li, k, :, :],
                        in1=normg[:, li, k, :].unsqueeze(2).to_broadcast([128, NDC, 2]), op=ALU.mult),
                        reads=[b_mod, b_small], writes=[b_mod])
                    P.op("dve", lambda li=li, k=k: nc.vector.tensor_scalar(
                        out=gate[:, li, k, :, :], in0=modsb[:, li, (3 * k + 2) * NDC:(3 * k + 3) * NDC, :],
                        scalar1=(1.0 if k == 1 else 0.5), scalar2=None, op0=ALU.mult), reads=[b_mod], writes=[b_mod])

            P.barrier()

        def shift_ap(li, k, dc, cond):
            return modsb[:, li, 3 * k * NDC + dc, cond:cond + 1]

        hs = sb("hs", [128, NDC, 512], F32)
        b_hs = P.buf("hs")
        sq = sb("sq", [128, NDC, 528], BF16)
        b_sq = P.buf("sq")
        rstd = sb("rstd", [128, 512], F32)
        b_rstd = P.buf("rstd")
        tmpr = Ring(P, [sb("tmp%d" % i, [128, 512], F32) for i in range(3)])
        hreg = {}

        def hT_buf(tok0):
            if tok0 not in hreg:
                hreg[tok0] = P.buf("hT%d" % tok0)
            return hreg[tok0]

        es_x = ExitStack()
        xin = Ring(P, [es_x.enter_context(nc.sbuf_tensor("sb_xin%d" % i, [128, D], F32)) for i in range(2)])

        def load_transpose(src, tok0, n):
            for tb in range(n // 128):
                t, tbuf = xin.next()
                P.dma("sp", t[:], src[tb * 128:(tb + 1) * 128, :], writes=[tbuf])
                for q in range(4):
                    bank = 1 + (q % 2)
                    mms = [lambda dc=dc, t=t, bank=bank: nc.tensor.transpose(
                        out=ps[bank][:, (dc % 4) * 128:(dc % 4 + 1) * 128], in_=t[:, dc * 128:(dc + 1) * 128],
                        identity=ident_f[:]) for dc in range(q * 4, q * 4 + 4)]
                    P.mm_group(mms, reads=[tbuf, b_const], writes=[pb[bank]])
                    P.op("dve" if q % 2 == 0 else "act",
                         (lambda q=q, bank=bank, tb=tb: nc.vector.tensor_copy(
                             out=hs[:, q * 4:q * 4 + 4, tb * 128:(tb + 1) * 128],
                             in_=ps[bank][:, :].rearrange("p (c t) -> p c t", c=4))) if q % 2 == 0 else
                         (lambda q=q, bank=bank, tb=tb: nc.scalar.copy(
                             out=hs[:, q * 4:q * 4 + 4, tb * 128:(tb + 1) * 128],
                             in_=ps[bank][:, :].rearrange("p (c t) -> p c t", c=4))),
                         reads=[pb[bank]], writes=[b_hs])
            P.dma("sp", hT_v[:, :, tok0:tok0 + n], hs[:, :, 0:n], reads=[b_hs], writes=[hT_buf(tok0)])

        for (tok0, n, cond, isc) in lat_groups:
            load_transpose(x_d[tok0:tok0 + n, :], tok0, n)
        load_transpose(ctx_d, L, CTX)
        P.barrier()
        es_x.close()

        def rms_stats(n):
            P.op("act", lambda: nc.scalar.activation(out=sq[:, :, 0:n], in_=hs[:, :, 0:n], func=AF.Square),
                 reads=[b_hs], writes=[b_sq])
            mms = [lambda dc=dc: nc.tensor.matmul(ps[0][:, 0:n], lhsT=ones_bf[:], rhs=sq[:, dc, 0:n],
                                                  start=(dc == 0), stop=(dc == NDC - 1)) for dc in range(NDC)]
            P.mm_group(mms, reads=[b_sq, b_const], writes=[pb[0]])
            P.op("act", lambda: nc.scalar.activation(out=rstd[:, 0:n], in_=ps[0][:, 0:n], func=AF.Sqrt,
                                                     bias=EPS, scale=1.0 / D), reads=[pb[0]], writes=[b_rstd])
            P.op("dve", lambda: nc.vector.reciprocal(out=rstd[:, 0:n], in_=rstd[:, 0:n]), reads=[b_rstd], writes=[b_rstd])

        def modulate(dst, b_dst, n, li, k, cond):
            for dc in range(NDC):
                t, tb = tmpr.next()
                P.op("dve", lambda dc=dc, t=t: nc.vector.scalar_tensor_tensor(
                    out=t[:, 0:n], in0=hs[:, dc, 0:n], scalar=sc1[:, li, k, dc, cond:cond + 1], in1=rstd[:, 0:n],
                    op0=ALU.mult, op1=ALU.mult), reads=[b_hs, b_rstd, b_mod], writes=[tb])
                P.op("act", lambda dc=dc, t=t: nc.scalar.activation(
                    out=dst[:, dc, 0:n], in_=t[:, 0:n], func=AF.Identity, bias=shift_ap(li, k, dc, cond), scale=1.0),
                    reads=[tb, b_mod], writes=[b_dst])

        b_out = P.buf("out")
        hT2_d = nc.dram_tensor("hT2", [D, NT], F32, kind="Internal").ap()
        hT2_v = hT2_d.rearrange("(dc p) t -> p dc t", p=128)
        b_h2 = P.buf("hT2")
        ffn_scr = {}

        def ffn_convert(li, k):
            if (li, k) in ffn_scr:
                return
            w1s = nc.dram_tensor("w1s_%d_%d" % (li, k), [NFC, 128, NDC, 2, 128], BF16, kind="Internal").ap()
            w2s = nc.dram_tensor("w2s_%d_%d" % (li, k), [NDC, 128, NFC, 128], BF16, kind="Internal").ap()
            b1 = [P.buf() for _ in range(NFC)]
            b2 = [P.buf() for _ in range(NDC)]
            w_in = W[li]["ffn_w_in"][k].rearrange("(dc p) f -> p dc f", p=128)
            w_out = W[li]["ffn_w_out"][k].rearrange("(fc p) d -> p fc d", p=128)
            for fc in range(NFC):
                P.dma("pool", w1s[fc, :, :, 0, :], w_in[:, :, fc * 128:(fc + 1) * 128], writes=[b1[fc]])
                P.dma("pool", w1s[fc, :, :, 1, :], w_in[:, :, DFF + fc * 128:DFF + (fc + 1) * 128], writes=[b1[fc]])
            for dc in range(NDC):
                P.dma("pool", w2s[dc, :, :, :], w_out[:, :, dc * 128:(dc + 1) * 128], writes=[b2[dc]])
            ffn_scr[(li, k)] = (w1s, w2s, b1, b2)

        ffn_order = []

        def ffn_prefetch_next(li, k):
            i = ffn_order.index((li, k))
            if i + 1 < len(ffn_order):
                ffn_convert(*ffn_order[i + 1])

        def ffn_phase(li, k, groups):
            kk = 0 if k == 0 else 2
            ffn_convert(li, k)
            w1s, w2s, b1s, b2s = ffn_scr[(li, k)]
            ffn_prefetch_next(li, k)
            with ExitStack() as es3:
                S = lambda name, shape, dt: es3.enter_context(nc.sbuf_tensor("sb_%s_%d" % (name, P.uid()), list(shape), dt))
                aT = S("aT", [128, NDC, 512], BF16)
                b_aT = P.buf("aT")
                hid = S("hid", [128, NFC, 512], BF16)
                b_hid = P.buf("hid")
                w1r = Ring(P, [S("w1_%d" % i, [128, NDC, 2, 128], BF16) for i in range(4)])
                w2r = Ring(P, [S("w2_%d" % i, [128, NFC, 128], BF16) for i in range(3)])
                sgr = Ring(P, [S("sg%d" % i, [128, 512], F32) for i in range(2)])
                w1u = {}
                for (tok0, n, cond, isc) in groups:
                    P.dma("sp", hs[:, :, 0:n], hT_v[:, :, tok0:tok0 + n], reads=[hT_buf(tok0)], writes=[b_hs])
                    rms_stats(n)
                    modulate(aT, b_aT, n, li, kk, cond)
                    for fc in range(NFC):
                        wt, wb = w1r.next()
                        wb2 = w1u.setdefault(id(wb), P.buf())
                        P.dma("sp", wt[:], w1s[fc], reads=[b1s[fc]], writes=[wb, wb2])
                        bg = 1 + (fc % 2)
                        bu = 3 if fc % 2 == 0 else 7
                        P.mm_group([lambda dc=dc, wt=wt, bg=bg: nc.tensor.matmul(
                            ps[bg][:, 0:n], lhsT=wt[:, dc, 0, :], rhs=aT[:, dc, 0:n], start=(dc == 0), stop=(dc == NDC - 1))
                            for dc in range(NDC)], reads=[wb, b_aT], writes=[pb[bg]])
                        P.mm_group([lambda dc=dc, wt=wt, bu=bu: nc.tensor.matmul(
                            ps[bu][:, 0:n], lhsT=wt[:, dc, 1, :], rhs=aT[:, dc, 0:n], start=(dc == 0), stop=(dc == NDC - 1))
                            for dc in range(NDC)], reads=[wb2, b_aT], writes=[pb[bu]])
                        sg, sgb = sgr.next()
                        P.op("act", lambda sg=sg, bg=bg: nc.scalar.activation(out=sg[:, 0:n], in_=ps[bg][:, 0:n], func=AF.Silu),
                             reads=[pb[bg]], writes=[sgb])
                        P.op("dve", lambda sg=sg, bu=bu, fc=fc: nc.vector.tensor_tensor(
                            out=hid[:, fc, 0:n], in0=sg[:, 0:n], in1=ps[bu][:, 0:n], op=ALU.mult),
                            reads=[sgb, pb[bu]], writes=[b_hid])
                    for dc in range(NDC):
                        wt, wb = w2r.next()
                        P.dma("sp", wt[:], w2s[dc], reads=[b2s[dc]], writes=[wb])
                        by = 5 + (dc % 2)
                        P.mm_group([lambda fc=fc, wt=wt, by=by: nc.tensor.matmul(
                            ps[by][:, 0:n], lhsT=wt[:, fc, :], rhs=hid[:, fc, 0:n], start=(fc == 0), stop=(fc == NFC - 1))
                            for fc in range(NFC)], reads=[wb, b_hid], writes=[pb[by]])
                        P.op("dve", lambda dc=dc, by=by: nc.vector.scalar_tensor_tensor(
                            out=hs[:, dc, 0:n], in0=ps[by][:, 0:n], scalar=gate[:, li, kk, dc, cond:cond + 1],
                            in1=hs[:, dc, 0:n], op0=ALU.mult, op1=ALU.add), reads=[pb[by], b_hs, b_mod], writes=[b_hs])
                    P.dma("sp", hT_v[:, :, tok0:tok0 + n], hs[:, :, 0:n], reads=[b_hs], writes=[hT_buf(tok0)])
                if debug and li == 0 and k == lay_dbg_k:
                    dbg_aT = nc.dram_tensor("dbg_aT", [128, NDC * 512], BF16, kind="ExternalOutput").ap()
                    P.dma("sp", dbg_aT[:, :], aT[:].rearrange("p a b -> p (a b)"), reads=[b_aT], writes=[b_out])
                    dbg_hid = nc.dram_tensor("dbg_hid", [128, NFC * 512], BF16, kind="ExternalOutput").ap()
                    P.dma("sp", dbg_hid[:, :], hid[:].rearrange("p a b -> p (a b)"), reads=[b_hid], writes=[b_out])
                P.barrier()

        def pool_phase(li, groups):
            HALO = 8
            pw_d = W[li]["pool_w"]
            with ExitStack() as es3:
                S = lambda name, shape, dt: es3.enter_context(nc.sbuf_tensor("sb_%s_%d" % (name, P.uid()), list(shape), dt))
                pw = S("pw", [128, 4, 4, 512], BF16)
                b_pw = P.buf("pw")
                psc = S("psc", [128, NDC], F32)
                gsc = S("gsc", [128, NDC, 2], F32)
                for g in range(4):
                    P.dma("pool", pw[:, g, :, :], pw_d[g].rearrange("(cc p) e -> p cc e", p=128), writes=[b_pw])
                P.dma("sp", psc[:], W[li]["pool_scale"][:, :], writes=[b_pw])
                P.op("dve", lambda: nc.vector.tensor_tensor(out=gsc[:], in0=gate[:, li, 1, :, :],
                                                            in1=psc[:].unsqueeze(2).to_broadcast([128, NDC, 2]), op=ALU.mult),
                     reads=[b_pw, b_mod], writes=[b_pw])
                NW = 512 + 2 * HALO
                hh = S("hh", [128, NDC, NW], F32)
                b_hh = P.buf("hh")
                a32 = S("a32", [128, NDC, NW], F32)
                b_a = P.buf("a32")
                s1 = S("s1", [128, 4, NW], F32)
                s2 = S("s2", [128, 4, NW], F32)
                b_s = P.buf("s12")
                pooled = S("pooled", [128, NDC, 512], BF16)
                b_pl = P.buf("pooled")
                rsth = S("rsth", [128, NW], F32)
                b_rh = P.buf("rsth")
                sqh, b_sqh = sq, b_sq
                cnt = S("cnt", [128, 4, 512], F32)
                b_cnt = P.buf("cnt")
                for (tok0, n, cond, isc) in groups:
                    seq0 = L if isc else 0
                    seqn = CTX if isc else L
                    lo = max(tok0 - HALO, seq0)
                    hi = min(tok0 + n + HALO, seq0 + seqn)
                    c0 = lo - (tok0 - HALO)
                    c1 = hi - (tok0 - HALO)
                    nw = n + 2 * HALO
                    if c0 > 0 or c1 < nw:
                        P.op("pool", lambda: nc.gpsimd.memset(hh[:, :, 0:nw], 1.0), writes=[b_hh])
                    rd = [hT_buf(t) for t in hreg if (t < hi and t + 512 > lo)]
                    P.dma("sp", hh[:, :, c0:c1], hT_v[:, :, lo:hi], reads=rd, writes=[b_hh])
                    P.op("act", lambda: nc.scalar.activation(out=sqh[:, :, 0:nw], in_=hh[:, :, 0:nw], func=AF.Square),
                         reads=[b_hh], writes=[b_sqh])
                    for (a, b, bank) in ((0, min(512, nw), 0), (512, nw, 7)):
                        if b <= a:
                            continue
                        P.mm_group([lambda dc=dc, a=a, b=b, bank=bank: nc.tensor.matmul(
                            ps[bank][:, 0:b - a], lhsT=ones_bf[:], rhs=sqh[:, dc, a:b], start=(dc == 0), stop=(dc == NDC - 1))
                            for dc in range(NDC)], reads=[b_sqh, b_const], writes=[pb[bank]])
                        P.op("act", lambda a=a, b=b, bank=bank: nc.scalar.activation(
                            out=rsth[:, a:b], in_=ps[bank][:, 0:b - a], func=AF.Sqrt, bias=EPS, scale=1.0 / D),
                            reads=[pb[bank]], writes=[b_rh])
                    P.op("dve", lambda: nc.vector.reciprocal(out=rsth[:, 0:nw], in_=rsth[:, 0:nw]), reads=[b_rh], writes=[b_rh])
                    for dc in range(NDC):
                        t, tb = tmpr.next()
                        P.op("dve", lambda dc=dc: nc.vector.scalar_tensor_tensor(
                            out=a32[:, dc, 0:nw], in0=hh[:, dc, 0:nw], scalar=sc1[:, li, 1, dc, cond:cond + 1], in1=rsth[:, 0:nw],
                            op0=ALU.mult, op1=ALU.mult), reads=[b_hh, b_rh, b_mod], writes=[b_a])
                        P.op("act", lambda dc=dc: nc.scalar.activation(
                            out=a32[:, dc, 0:nw], in_=a32[:, dc, 0:nw], func=AF.Identity, bias=shift_ap(li, 1, dc, cond), scale=1.0),
                            reads=[b_a, b_mod], writes=[b_a])
                    if c0 > 0:
                        P.op("pool", lambda: nc.gpsimd.memset(a32[:, :, 0:c0], 0.0), reads=[b_a], writes=[b_a])
                    if c1 < nw:
                        P.op("pool", lambda: nc.gpsimd.memset(a32[:, :, c1:nw], 0.0), reads=[b_a], writes=[b_a])
                    for g, w in enumerate((2, 4, 8, 16)):
                        src = a32[:, g * 4:(g + 1) * 4, :]
                        width = 1
                        cur = src
                        dst_cycle = [s1, s2]
                        di = 0
                        while width < w:
                            dst = dst_cycle[di]
                            di ^= 1
                            P.op("dve", lambda cur=cur, dst=dst, width=width: nc.vector.tensor_tensor(
                                out=dst[:, :, width:nw], in0=cur[:, :, width:nw], in1=cur[:, :, 0:nw - width], op=ALU.add),
                                reads=[b_a, b_s], writes=[b_s])
                            cur = dst
                            width *= 2
                        off = HALO + w // 2 - 1
                        P.op("pool", lambda g=g, w=w: nc.gpsimd.memset(cnt[:, g, 0:n], float(w)), writes=[b_cnt])
                        for tl in list(range(0, min(n, 8))) + list(range(max(n - 8, 8), n)):
                            tg = tok0 + tl - seq0
                            lo_ = min(max(tg - w // 2, 0), seqn)
                            hi_ = min(max(tg - w // 2 + w, 0), seqn)
                            if hi_ - lo_ != w:
                                P.op("pool", lambda g=g, tl=tl, v=float(hi_ - lo_): nc.gpsimd.memset(cnt[:, g, tl:tl + 1], v),
                                     reads=[b_cnt], writes=[b_cnt])
                        P.op("dve", lambda g=g: nc.vector.reciprocal(out=cnt[:, g, 0:n], in_=cnt[:, g, 0:n]),
                             reads=[b_cnt], writes=[b_cnt])
                        P.op("dve", lambda cur=cur, off=off, g=g: nc.vector.tensor_tensor(
                            out=s1[:, :, 0:n] if cur is not s1 else s2[:, :, 0:n], in0=cur[:, :, off:off + n],
                            in1=cnt[:, g, 0:n].unsqueeze(1).to_broadcast([128, 4, n]), op=ALU.mult),
                            reads=[b_s, b_cnt], writes=[b_s])
                        q = s1 if cur is not s1 else s2
                        P.op("dve", lambda q=q, g=g: nc.vector.tensor_tensor(
                            out=pooled[:, g * 4:(g + 1) * 4, 0:n], in0=q[:, :, 0:n], in1=a32[:, g * 4:(g + 1) * 4, HALO:HALO + n],
                            op=ALU.subtract), reads=[b_s, b_a], writes=[b_pl])
                    P.op("dve", lambda: nc.vector.tensor_copy(out=hs[:, :, 0:n], in_=hh[:, :, HALO:HALO + n]),
                         reads=[b_hh], writes=[b_hs])
                    for g in range(4):
                        for ec in range(4):
                            dc = g * 4 + ec
                            bank = 5 + (dc % 2)
                            P.mm_group([lambda cc=cc, g=g, ec=ec, bank=bank: nc.tensor.matmul(
                                ps[bank][:, 0:n], lhsT=pw[:, g, cc, ec * 128:(ec + 1) * 128], rhs=pooled[:, g * 4 + cc, 0:n],
                                start=(cc == 0), stop=(cc == 3)) for cc in range(4)], reads=[b_pw, b_pl], writes=[pb[bank]])
                            P.op("dve", lambda dc=dc, bank=bank: nc.vector.scalar_tensor_tensor(
                                out=hs[:, dc, 0:n], in0=ps[bank][:, 0:n], scalar=gsc[:, dc, cond:cond + 1],
                                in1=hs[:, dc, 0:n], op0=ALU.mult, op1=ALU.add), reads=[pb[bank], b_hs, b_pw], writes=[b_hs])
                    P.dma("sp", hT2_v[:, :, tok0:tok0 + n], hs[:, :, 0:n], reads=[b_hs], writes=[b_h2])
                for (tok0, n, cond, isc) in groups:
                    P.dma("sp", hT_d[:, tok0:tok0 + n], hT2_d[:, tok0:tok0 + n], reads=[b_h2], writes=[hT_buf(tok0)])
                P.barrier()


        def ssm_phase(li, groups, need_ctx):
            Wl = W[li]
            w_in = Wl["s_w_in"].rearrange("(dc p) e -> p dc e", p=128)
            w_out = Wl["s_w_out"].rearrange("(fc p) d -> p fc d", p=128)
            b_xbc = P.buf(); b_dtr = P.buf(); b_z = P.buf(); b_xs = P.buf(); b_BT = P.buf(); b_CT = P.buf()
            b_Btm = P.buf(); b_dttm = P.buf(); b_dtAtm = P.buf(); b_acs = P.buf(); b_yf = P.buf(); b_uT = P.buf()
            hs_flat = hs[:].rearrange("p a b -> p (a b)")
            sq_flat = sq[:].rearrange("p a b -> p (a b)")
            with ExitStack() as es3:
                S = lambda name, shape, dt: es3.enter_context(nc.sbuf_tensor("sb_%s_%d" % (name, P.uid()), list(shape), dt))
                aT = S("aT", [128, NDC, 512], BF16)
                b_aT = P.buf("aT")
                wr = Ring(P, [S("wi%d" % i, [128, NDC, 128], BF16) for i in range(4)])
                wzr = Ring(P, [S("wz%d" % i, [128, NDC, 512], BF16) for i in range(2)])
                stg = Ring(P, [S("stg%d" % i, [128, 4, 512], F32) for i in range(2)])
                zst = Ring(P, [S("zst%d" % i, [128, 512], F32) for i in range(3)])
                for (tok0, n, cond, isc) in groups:
                    P.dma("sp", hs[:, :, 0:n], hT_v[:, :, tok0:tok0 + n], reads=[hT_buf(tok0)], writes=[b_hs])
                    rms_stats(n)
                    modulate(aT, b_aT, n, li, 1, cond)
                    for j in range(49):
                        col = (4096 + j * 128) if j < 48 else 10240
                        wt, wb = wr.next()
                        P.dma("pool", wt[:], w_in[:, :, col:col + 128], writes=[wb])
                        bk = 1 + (j % 2)
                        P.mm_group([lambda dc=dc, wt=wt, bk=bk: nc.tensor.matmul(
                            ps[bk][:, 0:n], lhsT=wt[:, dc, :], rhs=aT[:, dc, 0:n], start=(dc == 0), stop=(dc == NDC - 1))
                            for dc in range(NDC)], reads=[wb, b_aT], writes=[pb[bk]])
                        if j % 4 == 0:
                            st, stb = stg.next()
                        if j % 2 == 0:
                            P.op("dve", lambda st=st, j=j, bk=bk: nc.vector.tensor_copy(out=st[:, j % 4, 0:n], in_=ps[bk][:, 0:n]),
                                 reads=[pb[bk]], writes=[stb])
                        else:
                            P.op("act", lambda st=st, j=j, bk=bk: nc.scalar.activation(out=st[:, j % 4, 0:n], in_=ps[bk][:, 0:n], func=AF.Identity),
                                 reads=[pb[bk]], writes=[stb])
                        if j < 48 and j % 4 == 3:
                            P.dma("sp", xbc_d[:, j - 3:j + 1, tok0:tok0 + n], st[:, :, 0:n], reads=[stb], writes=[b_xbc])
                        if j == 48:
                            P.dma("sp", dtr_d[:, tok0:tok0 + n], st[:, 0, 0:n], reads=[stb], writes=[b_dtr])
                    for cbk in range(8):
                        wz, wzb = wzr.next()
                        P.dma("pool", wz[:], w_in[:, :, cbk * 512:(cbk + 1) * 512], writes=[wzb])
                        for tb in range(n // 128):
                            bk = 5 + (tb % 2)
                            P.mm_group([lambda dc=dc, tb=tb, bk=bk, wz=wz: nc.tensor.matmul(
                                ps[bk][:, :], lhsT=aT[:, dc, tb * 128:(tb + 1) * 128], rhs=wz[:, dc, :], start=(dc == 0), stop=(dc == NDC - 1))
                                for dc in range(NDC)], reads=[wzb, b_aT], writes=[pb[bk]])
                            zt, ztb = zst.next()
                            if tb % 2 == 0:
                                P.op("dve", lambda zt=zt, bk=bk: nc.vector.tensor_copy(out=zt[:], in_=ps[bk][:, :]), reads=[pb[bk]], writes=[ztb])
                            else:
                                P.op("act", lambda zt=zt, bk=bk: nc.scalar.activation(out=zt[:], in_=ps[bk][:, :], func=AF.Identity),
                                     reads=[pb[bk]], writes=[ztb])
                            P.dma("sp", z_d[tok0 + tb * 128:tok0 + (tb + 1) * 128, cbk * 512:(cbk + 1) * 512], zt[:], reads=[ztb], writes=[b_z])
                P.barrier()
            with ExitStack() as es3:
                S = lambda name, shape, dt: es3.enter_context(nc.sbuf_tensor("sb_%s_%d" % (name, P.uid()), list(shape), dt))
                cw = S("cw", [128, 48, 7], F32)
                cbias = S("cbias", [128, 48], F32)
                dtb = S("dtb", [128, 1], F32)
                avec = S("avec", [128, 1], F32)
                b_cw = P.buf("cw")
                P.dma("sp", cw[:], Wl["s_cw"][:, :, :], writes=[b_cw])
                P.dma("sp", cbias[:], Wl["s_cb"][:, :], writes=[b_cw])
                P.dma("sp", dtb[:], Wl["s_dtb"][:, :], writes=[b_cw])
                P.dma("sp", avec[:], Wl["s_alog"][:, :], writes=[b_cw])
                P.op("act", lambda: nc.scalar.activation(out=avec[:], in_=avec[:], func=AF.Exp), reads=[b_cw], writes=[b_cw])
                P.op("dve", lambda: nc.vector.tensor_scalar(out=avec[:], in0=avec[:], scalar1=-1.0, scalar2=None, op0=ALU.mult),
                     reads=[b_cw], writes=[b_cw])
                xin = S("xin", [128, 8, 518], F32)
                b_xin = P.buf("xin")
                accr = Ring(P, [S("acc%d" % i, [128, 512], F32) for i in range(2)])
                xc = S("xc", [128, 8, 512], BF16)
                b_xc = P.buf("xc")
                trs = Ring(P, [S("trs%d" % i, [128, 1024], BF16) for i in range(2)])
                ptp = ps[4][:, :].bitcast(BF16)
                d1 = S("d1", [128, 512], F32)
                d2 = S("d2", [128, 512], F32)
                d3 = S("d3", [128, 512], F32)
                dA = S("dA", [128, 512], F32)
                b_d = P.buf("dwork")
                ttm = S("ttm", [128, 2, 128], F32)
                b_ttm = P.buf("ttm")
                acsg = S("acsg", [128, 512], F32)
                b_acsg = P.buf("acsg")
                for (tok0, n, cond, isc) in groups:
                    seq0 = L if isc else 0
                    seqn = CTX if isc else L
                    lo = max(tok0 - 3, seq0)
                    hi = min(tok0 + n + 3, seq0 + seqn)
                    c0 = lo - (tok0 - 3)
                    c1 = hi - (tok0 - 3)
                    for j8 in range(6):
                        if c0 > 0 or c1 < n + 6:
                            P.op("pool", lambda: nc.gpsimd.memset(xin[:, :, 0:n + 6], 0.0), writes=[b_xin])
                        P.dma("sp", xin[:, :, c0:c1], xbc_d[:, j8 * 8:(j8 + 1) * 8, lo:hi], reads=[b_xbc], writes=[b_xin])
                        for jj in range(8):
                            j = j8 * 8 + jj
                            ac, acb = accr.next()
                            P.op("act", lambda ac=ac, jj=jj, j=j: nc.scalar.activation(
                                out=ac[:, 0:n], in_=xin[:, jj, 0:n], func=AF.Identity, bias=cbias[:, j:j + 1], scale=cw[:, j, 0:1]),
                                reads=[b_xin, b_cw], writes=[acb])
                            for k in range(1, 7):
                                P.op("dve", lambda ac=ac, jj=jj, j=j, k=k: nc.vector.scalar_tensor_tensor(
                                    out=ac[:, 0:n], in0=xin[:, jj, k:k + n], scalar=cw[:, j, k:k + 1], in1=ac[:, 0:n],
                                    op0=ALU.mult, op1=ALU.add), reads=[b_xin, b_cw, acb], writes=[acb])
                            P.op("act", lambda ac=ac, jj=jj: nc.scalar.activation(out=xc[:, jj, 0:n], in_=ac[:, 0:n], func=AF.Silu),
                                 reads=[acb], writes=[b_xc])
                        if j8 == 4:
                            P.dma("sp", BT_d[:, :, tok0:tok0 + n], xc[:, :, 0:n], reads=[b_xc], writes=[b_BT])
                        if j8 == 5:
                            P.dma("sp", CT_d[:, :, tok0:tok0 + n], xc[:, :, 0:n], reads=[b_xc], writes=[b_CT])
                        if j8 <= 4:
                            for tb in range(n // 128):
                                P.mm_group([lambda jj=jj, tb=tb: nc.tensor.transpose(
                                    out=ptp[:, jj * 128:(jj + 1) * 128], in_=xc[:, jj, tb * 128:(tb + 1) * 128], identity=ident_b[:])
                                    for jj in range(8)], reads=[b_xc, b_const], writes=[pb[4]])
                                tr, trb = trs.next()
                                P.op("act", lambda tr=tr: nc.scalar.activation(out=tr[:], in_=ptp[:, :], func=AF.Identity),
                                     reads=[pb[4]], writes=[trb])
                                r0 = tok0 + tb * 128
                                if j8 < 4:
                                    P.dma("sp", xs_d[r0:r0 + 128, j8 * 1024:(j8 + 1) * 1024], tr[:], reads=[trb], writes=[b_xs])
                                else:
                                    P.dma("sp", Btm_d[r0:r0 + 128, :], tr[:], reads=[trb], writes=[b_Btm])
                    P.dma("sp", d1[:, 0:n], dtr_d[:, tok0:tok0 + n], reads=[b_dtr], writes=[b_d])
                    P.op("dve", lambda: nc.vector.tensor_scalar(out=d1[:, 0:n], in0=d1[:, 0:n], scalar1=dtb[:, 0:1], scalar2=None, op0=ALU.add),
                         reads=[b_d, b_cw], writes=[b_d])
                    P.op("act", lambda: nc.scalar.activation(out=d2[:, 0:n], in_=d1[:, 0:n], func=AF.Abs), reads=[b_d], writes=[b_d])
                    P.op("act", lambda: nc.scalar.activation(out=d2[:, 0:n], in_=d2[:, 0:n], func=AF.Exp, scale=-1.0), reads=[b_d], writes=[b_d])
                    P.op("act", lambda: nc.scalar.activation(out=d2[:, 0:n], in_=d2[:, 0:n], func=AF.Ln, bias=1.0, scale=1.0), reads=[b_d], writes=[b_d])
                    P.op("dve", lambda: nc.vector.tensor_scalar(out=d3[:, 0:n], in0=d1[:, 0:n], scalar1=0.0, scalar2=None, op0=ALU.max),
                         reads=[b_d], writes=[b_d])
                    P.op("dve", lambda: nc.vector.tensor_tensor(out=d3[:, 0:n], in0=d3[:, 0:n], in1=d2[:, 0:n], op=ALU.add), reads=[b_d], writes=[b_d])
                    P.op("dve", lambda: nc.vector.tensor_scalar(out=dA[:, 0:n], in0=d3[:, 0:n], scalar1=avec[:, 0:1], scalar2=None, op0=ALU.mult),
                         reads=[b_d, b_cw], writes=[b_d])
                    for tb in range(n // 128):
                        r0 = tok0 + tb * 128
                        P.mm_group([lambda tb=tb: nc.tensor.transpose(out=ps[1][:, 0:128], in_=d3[:, tb * 128:(tb + 1) * 128], identity=ident_f[:]),
                                    lambda tb=tb: nc.tensor.transpose(out=ps[1][:, 128:256], in_=dA[:, tb * 128:(tb + 1) * 128], identity=ident_f[:])],
                                   reads=[b_d, b_const], writes=[pb[1]])
                        P.op("dve", lambda: nc.vector.tensor_copy(out=ttm[:], in_=ps[1][:, 0:256].rearrange("p (a b) -> p a b", a=2)),
                             reads=[pb[1]], writes=[b_ttm])
                        P.dma("sp", dttm_d[r0:r0 + 128, :], ttm[:, 0, :], reads=[b_ttm], writes=[b_dttm])
                        P.dma("sp", dtAtm_d[r0:r0 + 128, :], ttm[:, 1, :], reads=[b_ttm], writes=[b_dtAtm])
                        P.mm_group([lambda: nc.tensor.matmul(ps[2][0:64, 0:128], lhsT=ttm[:, 1, 0:64], rhs=triF[:], start=True, stop=True),
                                    lambda: nc.tensor.matmul(ps[2][64:128, 0:128], lhsT=ttm[:, 1, 64:128], rhs=triB[:], start=True, stop=True)],
                                   reads=[b_ttm, b_const], writes=[pb[2]])
                        P.op("dve", lambda tb=tb: nc.vector.tensor_copy(out=acsg[:, tb * 128:(tb + 1) * 128], in_=ps[2][:, 0:128]),
                             reads=[pb[2]], writes=[b_acsg])
                    P.dma("sp", acsfm_d[:, tok0:tok0 + n], acsg[:, 0:n], reads=[b_acsg], writes=[b_acs])
                P.barrier()
            nlat = L // 128
            chunks_f = [(L, True), (L + 128, True)] + [(c * 128, False) for c in range(nlat)]
            chunks_b = [(L + 128, True), (L, True)] + [(c * 128, False) for c in range(nlat - 1, -1, -1)]
            with ExitStack() as es3:
                S = lambda name, shape, dt: es3.enter_context(nc.sbuf_tensor("sb_%s_%d" % (name, P.uid()), list(shape), dt))
                y_acc = hs_flat[:, 0:4096]
                zz = hs_flat[:, 4096:8192]
                xd = sq_flat[:, 0:4096]
                xdd = sq_flat[:, 4096:8192]
                b_y = b_hs
                b_zz = P.buf("zz")
                b_xd = b_sq
                xs_t = S("xs_t", [128, SI], BF16)
                b_xst = P.buf("xs_t")
                acs_rows = [S("acs_row%d" % i, [128, SH // 2, 128], F32) for i in range(2)]
                b_rows = [P.buf("acs_row%d" % i) for i in range(2)]
                Bfm = S("Bfm", [128, SG, 128], BF16)
                Cfm = S("Cfm", [128, SG, 128], BF16)
                Btm = S("Btm", [128, SG * SN], BF16)
                b_bc = P.buf("bc")
                dtt = S("dtt", [128, 2, 128], F32)
                b_dtt = P.buf("dtt")
                sm_ = S("ssmall", [128, 6, SH], F32)
                b_sm = P.buf("ssmall")
                M_all = S("M_all", [128, SH, 128], BF16)
                b_M = [P.buf() for _ in range(SG)]
                cb_all = S("cb_all", [128, SG, 128], F32)
                b_cb = P.buf("cb_all")
                b_Stg = [P.buf() for _ in range(SG)]
                b_Sbg = [P.buf() for _ in range(SG)]
                tr_ = Ring(P, [S("tt%d" % i, [128, 512], F32) for i in range(2)])
                St = S("St", [128, SG, 512], F32)
                Sb = S("Sb", [128, SG, 512], BF16)
                b_St = P.buf("St")
                b_Sb = P.buf("Sb")
                dvec = S("dvec", [128, SH], F32)
                gnb = S("gnb", [128, SI], F32)
                b_gn = P.buf("gn")
                P.dma("sp", dvec[:], Wl["s_dvec"][:, :], writes=[b_gn])
                P.dma("sp", gnb[:], Wl["s_gn"][:, :], writes=[b_gn])
                ubf = S("ubf", [128, SI], BF16)
                b_ubf = P.buf("ubf")
                uTt = S("uTt", [128, 32, 128], BF16)
                b_uTt = P.buf("uTt")
                ptp = ps[4][:, :].bitcast(BF16)
                for d, chunks in ((0, chunks_f), (1, chunks_b)):
                    mb = mbF if d == 0 else mbB
                    tri = triF if d == 0 else triB
                    P.op("pool", lambda: nc.gpsimd.memset(St[:], 0.0), writes=b_Stg)
                    P.op("pool", lambda: nc.gpsimd.memset(Sb[:], 0.0), writes=b_Sbg)
                    for (r0, isc) in chunks:
                        want_y = (not isc) or need_ctx
                        P.dma("sp", xs_t[:], xs_d[r0:r0 + 128, :], reads=[b_xs], writes=[b_xst])
                        P.dma("sp", dtt[:, 0, :], dttm_d[r0:r0 + 128, :], reads=[b_dttm], writes=[b_dtt])
                        P.dma("sp", dtt[:, 1, :], dtAtm_d[r0:r0 + 128, :], reads=[b_dtAtm], writes=[b_dtt])
                        P.dma("sp", Bfm[:], BT_d[:, :, r0:r0 + 128], reads=[b_BT], writes=[b_bc])
                        P.dma("sp", Cfm[:], CT_d[:, :, r0:r0 + 128], reads=[b_CT], writes=[b_bc])
                        P.dma("sp", Btm[:], Btm_d[r0:r0 + 128, :], reads=[b_Btm], writes=[b_bc])
                        if want_y:
                            for hf in range(2):
                                P.dma("sp", acs_rows[hf][:], acsfm_d[d * 64 + hf * 32:d * 64 + (hf + 1) * 32, r0:r0 + 128].partition_broadcast(128),
                                      reads=[b_acs], writes=[b_rows[hf]])
                        dtd = dtt[:, 0, d * 64:(d + 1) * 64]
                        dtAd = dtt[:, 1, d * 64:(d + 1) * 64]
                        P.mm_group([lambda: nc.tensor.matmul(ps[0][:, 0:64], lhsT=tri[:], rhs=dtAd, start=True, stop=True),
                                    lambda: nc.tensor.matmul(ps[0][:, 64:128], lhsT=ones_f[:], rhs=dtAd, start=True, stop=True)],
                                   reads=[b_dtt, b_const], writes=[pb[0]])
                        P.op("dve", lambda: nc.vector.tensor_copy(out=sm_[:, 0, :], in_=ps[0][:, 0:64]), reads=[pb[0]], writes=[b_sm])
                        P.op("dve", lambda: nc.vector.tensor_copy(out=sm_[:, 2, :], in_=ps[0][:, 64:128]), reads=[pb[0]], writes=[b_sm])
                        P.op("act", lambda: nc.scalar.activation(out=sm_[:, 1, :], in_=sm_[:, 0, :], func=AF.Exp), reads=[b_sm], writes=[b_sm])
                        P.op("act", lambda: nc.scalar.activation(out=sm_[:, 5, :], in_=sm_[:, 2, :], func=AF.Exp), reads=[b_sm], writes=[b_sm])
                        P.op("dve", lambda: nc.vector.tensor_tensor(out=sm_[:, 3, :], in0=sm_[:, 2, :], in1=sm_[:, 0, :], op=ALU.subtract),
                             reads=[b_sm], writes=[b_sm])
                        P.op("act", lambda: nc.scalar.activation(out=sm_[:, 3, :], in_=sm_[:, 3, :], func=AF.Exp), reads=[b_sm], writes=[b_sm])
                        P.op("dve", lambda: nc.vector.tensor_tensor(out=sm_[:, 4, :], in0=sm_[:, 3, :], in1=dtd, op=ALU.mult),
                             reads=[b_sm, b_dtt], writes=[b_sm])
                        xs3 = xs_t[:].rearrange("p (h q) -> p h q", q=SP_)
                        P.op("dve", lambda: nc.vector.tensor_tensor(out=xd.rearrange("p (h q) -> p h q", q=SP_), in0=xs3,
                                                                    in1=dtd.unsqueeze(2).to_broadcast([128, SH, SP_]), op=ALU.mult),
                             reads=[b_xst, b_dtt], writes=[b_xd])
                        P.op("dve", lambda: nc.vector.tensor_tensor(out=xdd.rearrange("p (h q) -> p h q", q=SP_), in0=xs3,
                                                                    in1=sm_[:, 4, :].unsqueeze(2).to_broadcast([128, SH, SP_]), op=ALU.mult),
                             reads=[b_xst, b_sm], writes=[b_xd])
                        if want_y:
                            P.mm_group([lambda g=g: nc.tensor.matmul(ps[1][:, g * 128:(g + 1) * 128], lhsT=Bfm[:, g, :], rhs=Cfm[:, g, :], start=True, stop=True)
                                        for g in range(4)], reads=[b_bc], writes=[pb[1]])
                            P.mm_group([lambda g=g: nc.tensor.matmul(ps[2][:, (g - 4) * 128:(g - 3) * 128], lhsT=Bfm[:, g, :], rhs=Cfm[:, g, :], start=True, stop=True)
                                        for g in range(4, 8)], reads=[b_bc], writes=[pb[2]])
                            P.op("act", lambda: nc.scalar.activation(out=cb_all[:, 0:4, :], in_=ps[1][:, :].rearrange("p (g l) -> p g l", g=4), func=AF.Identity),
                                 reads=[pb[1]], writes=[b_cb])
                            P.op("act", lambda: nc.scalar.activation(out=cb_all[:, 4:8, :], in_=ps[2][:, :].rearrange("p (g l) -> p g l", g=4), func=AF.Identity),
                                 reads=[pb[2]], writes=[b_cb])
                            for hf in range(2):
                                ar, arb = acs_rows[hf], b_rows[hf]
                                P.op("dve", lambda ar=ar, hf=hf: nc.vector.tensor_tensor(
                                    out=ar[:], in0=ar[:], in1=sm_[:, 0, hf * 32:(hf + 1) * 32].unsqueeze(2).to_broadcast([128, 32, 128]), op=ALU.subtract),
                                    reads=[arb, b_sm], writes=[arb])
                                P.op("dve", lambda ar=ar: nc.vector.tensor_tensor(
                                    out=ar[:], in0=ar[:], in1=mb[:].unsqueeze(1).to_broadcast([128, 32, 128]), op=ALU.add),
                                    reads=[arb, b_const], writes=[arb])
                                P.op("act", lambda ar=ar: nc.scalar.activation(out=ar[:], in_=ar[:], func=AF.Exp), reads=[arb], writes=[arb])
                                for g in range(hf * 4, hf * 4 + 4):
                                    P.op("pool", lambda g=g, ar=ar: nc.gpsimd.tensor_tensor(
                                        out=M_all[:, g * 8:(g + 1) * 8, :], in0=ar[:, (g % 4) * 8:(g % 4 + 1) * 8, :],
                                        in1=cb_all[:, g, :].unsqueeze(1).to_broadcast([128, 8, 128]), op=ALU.mult),
                                        reads=[arb, b_cb], writes=[b_M[g]])
                        for g in range(SG):
                            if want_y:
                                byd = 3 if g % 2 == 0 else 5
                                bz = 6 if g % 2 == 0 else 7
                                P.mm_group([lambda hh=hh, g=g, byd=byd: nc.tensor.matmul(
                                    ps[byd][:, hh * 64:(hh + 1) * 64], lhsT=M_all[:, g * 8 + hh, :], rhs=xd[:, (g * 8 + hh) * 64:(g * 8 + hh + 1) * 64],
                                    start=True, stop=True) for hh in range(8)], reads=[b_M[g], b_xd], writes=[pb[byd]])
                                P.mm_group([lambda g=g, bz=bz: nc.tensor.matmul(ps[bz][:, :], lhsT=Cfm[:, g, :], rhs=Sb[:, g, :], start=True, stop=True)],
                                           reads=[b_bc, b_Sbg[g]], writes=[pb[bz]])
                                tt, ttb = tr_.next()
                                P.op("dve", lambda tt=tt, g=g, bz=bz: nc.vector.tensor_tensor(
                                    out=tt[:].rearrange("p (h q) -> p h q", q=SP_), in0=ps[bz][:, :].rearrange("p (h q) -> p h q", q=SP_),
                                    in1=sm_[:, 1, g * 8:(g + 1) * 8].unsqueeze(2).to_broadcast([128, 8, SP_]), op=ALU.mult),
                                    reads=[pb[bz], b_sm], writes=[ttb])
                                P.op("dve", lambda tt=tt, g=g, byd=byd: nc.vector.tensor_tensor(
                                    out=y_acc[:, g * 512:(g + 1) * 512], in0=tt[:], in1=ps[byd][:, :], op=ALU.add),
                                    reads=[ttb, pb[byd]], writes=[b_y])
                            P.mm_group([lambda g=g: nc.tensor.matmul(ps[4][:, :], lhsT=Btm[:, g * 128:(g + 1) * 128], rhs=xdd[:, g * 512:(g + 1) * 512],
                                                                     start=True, stop=True)], reads=[b_bc, b_xd], writes=[pb[4]])
                            P.op("pool", lambda g=g: nc.gpsimd.tensor_tensor(
                                out=St[:, g, :].rearrange("p (h q) -> p h q", q=SP_), in0=St[:, g, :].rearrange("p (h q) -> p h q", q=SP_),
                                in1=sm_[:, 5, g * 8:(g + 1) * 8].unsqueeze(2).to_broadcast([128, 8, SP_]), op=ALU.mult),
                                reads=[b_Stg[g], b_sm], writes=[b_Stg[g]])
                            P.op("dve", lambda g=g: nc.vector.tensor_tensor(out=St[:, g, :], in0=St[:, g, :], in1=ps[4][:, :], op=ALU.add),
                                 reads=[b_Stg[g], pb[4]], writes=[b_Stg[g]])
                            P.op("act", lambda g=g: nc.scalar.activation(out=Sb[:, g, :], in_=St[:, g, :], func=AF.Identity),
                                 reads=[b_Stg[g]], writes=[b_Sbg[g]])
                        if not want_y:
                            continue
                        if d == 0:
                            P.dma("sp", yf_d[r0:r0 + 128, :], y_acc, reads=[b_y], writes=[b_yf])
                            continue
                        P.dma("sp", zz, yf_d[r0:r0 + 128, :], reads=[b_yf], writes=[b_zz])
                        P.op("dve", lambda: nc.vector.tensor_tensor(out=y_acc, in0=y_acc, in1=zz, op=ALU.add), reads=[b_y, b_zz], writes=[b_y])
                        P.op("dve", lambda: nc.vector.tensor_tensor(out=zz.rearrange("p (h q) -> p h q", q=SP_), in0=xs3,
                                                                    in1=dvec[:].unsqueeze(2).to_broadcast([128, SH, SP_]), op=ALU.mult),
                             reads=[b_xst, b_gn, b_zz], writes=[b_zz])
                        P.op("dve", lambda: nc.vector.tensor_tensor(out=y_acc, in0=y_acc, in1=zz, op=ALU.add), reads=[b_y, b_zz], writes=[b_y])
                        P.dma("sp", zz, z_d[r0:r0 + 128, :], reads=[b_z, b_zz], writes=[b_zz])
                        P.op("act", lambda: nc.scalar.activation(out=zz, in_=zz, func=AF.Silu), reads=[b_zz], writes=[b_zz])
                        P.op("dve", lambda: nc.vector.tensor_tensor(out=y_acc, in0=y_acc, in1=zz, op=ALU.mult), reads=[b_y, b_zz], writes=[b_y])
                        for g in range(SG):
                            P.op("act", lambda g=g: nc.scalar.activation(out=zz[:, g * 512:(g + 1) * 512], in_=y_acc[:, g * 512:(g + 1) * 512],
                                                                         func=AF.Square, accum_out=sm_[:, 3, g:g + 1]),
                                 reads=[b_y, b_zz], writes=[b_zz, b_sm])
                        P.op("act", lambda: nc.scalar.activation(out=sm_[:, 3, 8:16], in_=sm_[:, 3, 0:8], func=AF.Sqrt, bias=EPS, scale=1.0 / 512.0),
                             reads=[b_sm], writes=[b_sm])
                        P.op("dve", lambda: nc.vector.reciprocal(out=sm_[:, 3, 16:24], in_=sm_[:, 3, 8:16]), reads=[b_sm], writes=[b_sm])
                        P.op("dve", lambda: nc.vector.tensor_tensor(
                            out=y_acc.rearrange("p (g q) -> p g q", q=512), in0=y_acc.rearrange("p (g q) -> p g q", q=512),
                            in1=sm_[:, 3, 16:24].unsqueeze(2).to_broadcast([128, 8, 512]), op=ALU.mult), reads=[b_y, b_sm], writes=[b_y])
                        P.op("dve", lambda: nc.vector.tensor_tensor(out=ubf[:], in0=y_acc, in1=gnb[:], op=ALU.mult), reads=[b_y, b_gn], writes=[b_ubf])
                        for q4 in range(4):
                            P.mm_group([lambda q4=q4, jj=jj: nc.tensor.transpose(
                                out=ptp[:, jj * 128:(jj + 1) * 128], in_=ubf[:, (q4 * 8 + jj) * 128:(q4 * 8 + jj + 1) * 128], identity=ident_b[:])
                                for jj in range(8)], reads=[b_ubf, b_const], writes=[pb[4]])
                            P.op("act", lambda q4=q4: nc.scalar.activation(out=uTt[:, q4 * 8:(q4 + 1) * 8, :],
                                                                            in_=ptp[:, :].rearrange("p (a b) -> p a b", a=8), func=AF.Identity),
                                 reads=[pb[4]], writes=[b_uTt])
                        P.dma("sp", uT_d[:, :, r0:r0 + 128], uTt[:], reads=[b_uTt], writes=[b_uT])
                P.barrier()
            with ExitStack() as es3:
                S = lambda name, shape, dt: es3.enter_context(nc.sbuf_tensor("sb_%s_%d" % (name, P.uid()), list(shape), dt))
                ug = S("ug", [128, 32, 512], BF16)
                b_ug = P.buf("ug")
                wor = Ring(P, [S("wso%d" % i, [128, 32, 128], BF16) for i in range(3)])
                for (tok0, n, cond, isc) in groups:
                    if isc and not need_ctx:
                        continue
                    P.dma("sp", hs[:, :, 0:n], hT_v[:, :, tok0:tok0 + n], reads=[hT_buf(tok0)], writes=[b_hs])
                    P.dma("sp", ug[:, :, 0:n], uT_d[:, :, tok0:tok0 + n], reads=[b_uT], writes=[b_ug])
                    for dc in range(NDC):
                        wt, wb = wor.next()
                        P.dma("pool", wt[:], w_out[:, :, dc * 128:(dc + 1) * 128], writes=[wb])
                        by = 5 + (dc % 2)
                        P.mm_group([lambda fc=fc, wt=wt, by=by: nc.tensor.matmul(
                            ps[by][:, 0:n], lhsT=wt[:, fc, :], rhs=ug[:, fc, 0:n], start=(fc == 0), stop=(fc == 31))
                            for fc in range(32)], reads=[wb, b_ug], writes=[pb[by]])
                        P.op("dve", lambda dc=dc, by=by: nc.vector.scalar_tensor_tensor(
                            out=hs[:, dc, 0:n], in0=ps[by][:, 0:n], scalar=gate[:, li, 1, dc, cond:cond + 1],
                            in1=hs[:, dc, 0:n], op0=ALU.mult, op1=ALU.add), reads=[pb[by], b_hs, b_mod], writes=[b_hs])
                    P.dma("sp", hT_v[:, :, tok0:tok0 + n], hs[:, :, 0:n], reads=[b_hs], writes=[hT_buf(tok0)])
                P.barrier()

        def attn_phase(li, groups, need_ctx):
            wq = W[li]["w_qkv"].rearrange("(dc p) e -> p dc e", p=128)
            wo = W[li]["w_o"].rearrange("(hc p) e -> p hc e", p=128)
            b_q = P.buf("qT_d")
            b_k = P.buf("kT_d")
            b_v = P.buf("v_d")
            import os
            if os.environ.get("ATT_STOP") == "0":
                return
            with ExitStack() as es3:
                S = lambda name, shape, dt: es3.enter_context(nc.sbuf_tensor("sb_%s_%d" % (name, P.uid()), list(shape), dt))
                aT = S("aT", [128, NDC, 512], BF16)
                b_aT = P.buf("aT")
                wv = S("wv", [128, NDC, 512], BF16)
                b_wv = P.buf("wv")
                P.dma("pool", wv[:], wq[:, :, 2560:3072], writes=[b_wv])
                wr = Ring(P, [S("wqk%d" % i, [128, NDC, 128], BF16) for i in range(4)])
                cs = S("cs", [128, 512], F32)
                sn = S("sn", [128, 512], F32)
                b_cs = P.buf("cs")
                xbr = Ring(P, [S("xb%d" % i, [128, 512], BF16) for i in range(2)])
                t1r = Ring(P, [S("t1%d" % i, [128, 512], F32) for i in range(2)])
                qo = S("qo", [128, NH + NKV, 512], BF16)
                b_qo = P.buf("qo")
                vo = Ring(P, [S("vo%d" % i, [128, 512], BF16) for i in range(2)])
                for (tok0, n, cond, isc) in groups:
                    P.dma("sp", hs[:, :, 0:n], hT_v[:, :, tok0:tok0 + n], reads=[hT_buf(tok0)], writes=[b_hs])
                    rms_stats(n)
                    modulate(aT, b_aT, n, li, 1, cond)
                    if not isc:
                        P.dma("sp", cs[:, 0:n], cosT_d[:, tok0:tok0 + n], writes=[b_cs])
                        P.dma("sp", sn[:, 0:n], sinT_d[:, tok0:tok0 + n], writes=[b_cs])
                    for hc in range(0 if os.environ.get("ATT_NOQK") else NH + NKV):
                        wt, wb = wr.next()
                        P.dma("pool", wt[:], wq[:, :, hc * 128:(hc + 1) * 128], writes=[wb])
                        bk = 1 + (hc % 2)
                        P.mm_group([lambda dc=dc, wt=wt, bk=bk: nc.tensor.matmul(
                            ps[bk][:, 0:n], lhsT=wt[:, dc, :], rhs=aT[:, dc, 0:n], start=(dc == 0), stop=(dc == NDC - 1))
                            for dc in range(NDC)], reads=[wb, b_aT], writes=[pb[bk]])
                        if isc or os.environ.get("ATT_NOROPE"):
                            P.op("act", lambda hc=hc, bk=bk: nc.scalar.activation(out=qo[:, hc, 0:n], in_=ps[bk][:, 0:n], func=AF.Identity),
                                 reads=[pb[bk]], writes=[b_qo])
                        else:
                            xb, xbb = xbr.next()
                            P.op("act", lambda xb=xb, bk=bk: nc.scalar.activation(out=xb[:, 0:n], in_=ps[bk][:, 0:n], func=AF.Identity),
                                 reads=[pb[bk]], writes=[xbb])
                            RM = os.environ.get("ATT_RM", "")
                            if "a" not in RM:
                                lw = ident_b if "i" in RM else rot_b
                                P.mm_group([lambda xb=xb, lw=lw: nc.tensor.matmul(ps[3][:, 0:n], lhsT=lw[:], rhs=xb[:, 0:n], start=True, stop=True)],
                                           reads=[xbb, b_const, b_rot], writes=[pb[3]])
                            t1, t1b = t1r.next()
                            if "b" not in RM:
                                P.op("dve", lambda t1=t1, bk=bk: nc.vector.tensor_tensor(out=t1[:, 0:n], in0=ps[bk][:, 0:n], in1=cs[:, 0:n], op=ALU.mult),
                                     reads=[pb[bk], b_cs, xbb], writes=[t1b])
                            t2, t2b = tmpr.next()
                            if "c" not in RM:
                                P.op("dve", lambda t2=t2: nc.vector.tensor_tensor(out=t2[:, 0:n], in0=ps[3][:, 0:n], in1=sn[:, 0:n], op=ALU.mult),
                                     reads=[pb[3], b_cs], writes=[t2b])
                            if "d" not in RM:
                                P.op("dve", lambda t1=t1, t2=t2, hc=hc: nc.vector.tensor_tensor(out=qo[:, hc, 0:n], in0=t1[:, 0:n], in1=t2[:, 0:n], op=ALU.add),
                                     reads=[t1b, t2b], writes=[b_qo])
                    if os.environ.get("ATT_STOP") != "2":
                        P.dma("sp", qT_d[:, :, tok0:tok0 + n], qo[:, 0:NH, 0:n], reads=[b_qo], writes=[b_q])
                        P.dma("sp", kT_d[:, :, tok0:tok0 + n], qo[:, NH:NH + NKV, 0:n], reads=[b_qo], writes=[b_k])
                    for tb in range(0 if os.environ.get("ATT_NOV") else n // 128):
                        bk = 5 + (tb % 2)
                        P.mm_group([lambda dc=dc, tb=tb, bk=bk: nc.tensor.matmul(
                            ps[bk][:, :], lhsT=aT[:, dc, tb * 128:(tb + 1) * 128], rhs=wv[:, dc, :], start=(dc == 0), stop=(dc == NDC - 1))
                            for dc in range(NDC)], reads=[b_wv, b_aT], writes=[pb[bk]])
                        vt, vtb = vo.next()
                        P.op("act", lambda vt=vt, bk=bk: nc.scalar.activation(out=vt[:], in_=ps[bk][:, :], func=AF.Identity), reads=[pb[bk]], writes=[vtb])
                        if os.environ.get("ATT_STOP") != "2":
                            P.dma("sp", v_d[tok0 + tb * 128:tok0 + (tb + 1) * 128, :], vt[:], reads=[vtb], writes=[b_v])
                P.barrier()
            import os
            if os.environ.get("ATT_STOP") in ("1", "2"):
                return
            with ExitStack() as es3:
                S = lambda name, shape, dt: es3.enter_context(nc.sbuf_tensor("sb_%s_%d" % (name, P.uid()), list(shape), dt))
                qg = S("qg", [128, NH, 512], BF16)
                b_qg = P.buf("qg")
                kband = S("kband", [128, NKV, 768], BF16)
                vband = S("vband", [128, 6, 512], BF16)
                b_band = P.buf("band")
                kctx = S("kctx", [128, NKV, CTX], BF16)
                vctx = S("vctx", [128, 2, 512], BF16)
                b_kc = P.buf("kctx")
                sinkb = S("sinkb", [128, NH], F32)
                P.dma("sp", sinkb[:], W[li]["sinkb"][:, :], writes=[b_kc])
                P.dma("sp", kctx[:], kT_d[:, :, L:L + CTX], reads=[b_k], writes=[b_kc])
                P.dma("sp", vctx[:], v_d[L:L + CTX, :].rearrange("(b p) e -> p b e", p=128), reads=[b_v], writes=[b_kc])
                oT = S("oT", [128, NH, 512], BF16)
                b_oT = P.buf("oT")
                smr = Ring(P, [S("sm%d" % i, [128, 640], F32) for i in range(2)])
                pfr = Ring(P, [S("pf%d" % i, [128, 640], F32) for i in range(2)])
                pnr = Ring(P, [S("pn%d" % i, [128, 640], BF16) for i in range(2)])
                ptr_ = Ring(P, [S("pt%d" % i, [128, 640], BF16) for i in range(2)])
                smallr = Ring(P, [S("sml%d" % i, [128, 8], F32) for i in range(4)])
                wor = Ring(P, [S("wo%d" % i, [128, NH, 128], BF16) for i in range(3)])
                ptp = ps[4][:, :].bitcast(BF16)
                for (tok0, n, cond, isc) in groups:
                    if isc and not need_ctx:
                        continue
                    nqb = n // 128
                    P.dma("sp", hs[:, :, 0:n], hT_v[:, :, tok0:tok0 + n], reads=[hT_buf(tok0)], writes=[b_hs])
                    P.dma("sp", qg[:, :, 0:n], qT_d[:, :, tok0:tok0 + n], reads=[b_q], writes=[b_qg])
                    if not isc:
                        lo = max(tok0 - 128, 0)
                        hi = min(tok0 + n + 128, L)
                        c0 = lo - (tok0 - 128)
                        c1 = hi - (tok0 - 128)
                        if c0 > 0 or c1 < n + 256:
                            P.op("pool", lambda: nc.gpsimd.memset(kband[:], 0.0), writes=[b_band])
                            P.op("pool", lambda: nc.gpsimd.memset(vband[:], 0.0), writes=[b_band])
                        P.dma("sp", kband[:, :, c0:c1], kT_d[:, :, lo:hi], reads=[b_k], writes=[b_band])
                        P.dma("sp", vband[:, c0 // 128:c1 // 128, :], v_d[lo:hi, :].rearrange("(b p) e -> p b e", p=128),
                              reads=[b_v], writes=[b_band])
                    for h in range(NH):
                        kv = h // 4
                        bo = 5 + (h % 2)
                        for qb in range(nqb):
                            qblk = qg[:, h, qb * 128:(qb + 1) * 128]
                            ba = 1 + ((h * nqb + qb) % 2)
                            nk = 256 if isc else 640
                            mms = [lambda qblk=qblk, ba=ba, kv=kv: nc.tensor.matmul(ps[ba][:, 0:256], lhsT=qblk, rhs=kctx[:, kv, :], start=True, stop=True)]
                            if not isc:
                                mms.append(lambda qblk=qblk, ba=ba, kv=kv, qb=qb: nc.tensor.matmul(
                                    ps[ba][:, 256:512], lhsT=qblk, rhs=kband[:, kv, qb * 128:(qb + 2) * 128], start=True, stop=True))
                            P.mm_group(mms, reads=[b_qg, b_kc, b_band], writes=[pb[ba]])
                            if not isc:
                                P.mm_group([lambda qblk=qblk, kv=kv, qb=qb: nc.tensor.matmul(
                                    ps[3][:, 0:128], lhsT=qblk, rhs=kband[:, kv, (qb + 2) * 128:(qb + 3) * 128], start=True, stop=True)],
                                    reads=[b_qg, b_band], writes=[pb[3]])
                            sm, smb = smr.next()
                            if isc:
                                P.op("dve", lambda sm=sm, ba=ba: nc.vector.tensor_scalar(
                                    out=sm[:, 0:256], in0=ps[ba][:, 0:256], scalar1=ATT_SCALE, scalar2=None, op0=ALU.mult),
                                    reads=[pb[ba]], writes=[smb])
                            else:
                                first = (tok0 + qb * 128 == 0)
                                lastb = (tok0 + (qb + 1) * 128 == L)
                                P.op("dve", lambda sm=sm, ba=ba, first=first: nc.vector.scalar_tensor_tensor(
                                    out=sm[:, 0:512], in0=ps[ba][:, :], scalar=ATT_SCALE, in1=maskA[:, 1 if first else 0, :],
                                    op0=ALU.mult, op1=ALU.add), reads=[pb[ba], b_const], writes=[smb])
                                P.op("dve", lambda sm=sm, lastb=lastb: nc.vector.scalar_tensor_tensor(
                                    out=sm[:, 512:640], in0=ps[3][:, 0:128], scalar=ATT_SCALE, in1=maskB[:, 1 if lastb else 0, :],
                                    op0=ALU.mult, op1=ALU.add), reads=[pb[3], b_const], writes=[smb])
                            sl, slb = smallr.next()
                            P.op("dve", lambda sl=sl, sm=sm, nk=nk: nc.vector.reduce_max(out=sl[:, 0:1], in_=sm[:, 0:nk], axis=AX.X),
                                 reads=[smb], writes=[slb])
                            P.op("dve", lambda sl=sl, h=h: nc.vector.tensor_tensor(out=sl[:, 1:2], in0=sl[:, 0:1], in1=sinkb[:, h:h + 1], op=ALU.max),
                                 reads=[slb, b_kc], writes=[slb])
                            P.op("dve", lambda sl=sl: nc.vector.tensor_scalar(out=sl[:, 2:3], in0=sl[:, 1:2], scalar1=-1.0, scalar2=None, op0=ALU.mult),
                                 reads=[slb], writes=[slb])
                            pf, pfb = pfr.next()
                            P.op("act", lambda pf=pf, sm=sm, sl=sl, nk=nk: nc.scalar.activation(
                                out=pf[:, 0:nk], in_=sm[:, 0:nk], func=AF.Exp, bias=sl[:, 2:3], scale=1.0, accum_out=sl[:, 3:4]),
                                reads=[smb, slb], writes=[pfb, slb])
                            P.op("act", lambda sl=sl, h=h: nc.scalar.activation(
                                out=sl[:, 4:5], in_=sinkb[:, h:h + 1], func=AF.Exp, bias=sl[:, 2:3], scale=1.0),
                                reads=[slb, b_kc], writes=[slb])
                            P.op("dve", lambda sl=sl: nc.vector.tensor_tensor(out=sl[:, 5:6], in0=sl[:, 3:4], in1=sl[:, 4:5], op=ALU.add),
                                 reads=[slb], writes=[slb])
                            P.op("dve", lambda sl=sl: nc.vector.reciprocal(out=sl[:, 6:7], in_=sl[:, 5:6]), reads=[slb], writes=[slb])
                            pn, pnb = pnr.next()
                            P.op("dve", lambda pn=pn, pf=pf, sl=sl, nk=nk: nc.vector.tensor_scalar(
                                out=pn[:, 0:nk], in0=pf[:, 0:nk], scalar1=sl[:, 6:7], scalar2=None, op0=ALU.mult),
                                reads=[pfb, slb], writes=[pnb])
                            nj = nk // 128
                            P.mm_group([lambda j=j, pn=pn: nc.tensor.transpose(
                                out=ptp[:, j * 128:(j + 1) * 128], in_=pn[:, j * 128:(j + 1) * 128], identity=ident_b[:]) for j in range(nj)],
                                reads=[pnb, b_const], writes=[pb[4]])
                            pt, ptb = ptr_.next()
                            P.op("act", lambda pt=pt, nk=nk: nc.scalar.activation(out=pt[:, 0:nk], in_=ptp[:, 0:nk], func=AF.Identity), reads=[pb[4]], writes=[ptb])
                            mms = []
                            for j in range(nj):
                                if j < 2:
                                    vblk = vctx[:, j, kv * 128:(kv + 1) * 128]
                                else:
                                    vblk = vband[:, qb + j - 2, kv * 128:(kv + 1) * 128]
                                mms.append(lambda j=j, vblk=vblk, pt=pt, bo=bo, qb=qb, nj=nj: nc.tensor.matmul(
                                    ps[bo][:, qb * 128:(qb + 1) * 128], lhsT=vblk, rhs=pt[:, j * 128:(j + 1) * 128],
                                    start=(j == 0), stop=(j == nj - 1)))
                            P.mm_group(mms, reads=[ptb, b_kc, b_band], writes=[pb[bo]])
                        P.op("act", lambda h=h, bo=bo: nc.scalar.activation(out=oT[:, h, 0:n], in_=ps[bo][:, 0:n], func=AF.Identity), reads=[pb[bo]], writes=[b_oT])
                    for dc in range(NDC):
                        wt, wb = wor.next()
                        P.dma("pool", wt[:], wo[:, :, dc * 128:(dc + 1) * 128], writes=[wb])
                        by = 0 if dc % 2 == 0 else 7
                        P.mm_group([lambda hc=hc, wt=wt, by=by: nc.tensor.matmul(
                            ps[by][:, 0:n], lhsT=wt[:, hc, :], rhs=oT[:, hc, 0:n], start=(hc == 0), stop=(hc == NH - 1))
                            for hc in range(NH)], reads=[wb, b_oT], writes=[pb[by]])
                        P.op("dve", lambda dc=dc, by=by: nc.vector.scalar_tensor_tensor(
                            out=hs[:, dc, 0:n], in0=ps[by][:, 0:n], scalar=gate[:, li, 1, dc, cond:cond + 1],
                            in1=hs[:, dc, 0:n], op0=ALU.mult, op1=ALU.add), reads=[pb[by], b_hs, b_mod], writes=[b_hs])
                    P.dma("sp", hT_v[:, :, tok0:tok0 + n], hs[:, :, 0:n], reads=[b_hs], writes=[hT_buf(tok0)])
                P.barrier()

        def final_phase():
            with ExitStack() as es3:
                S = lambda name, shape, dt: es3.enter_context(nc.sbuf_tensor("sb_%s_%d" % (name, P.uid()), list(shape), dt))
                nrm = S("nrm", [128, NDC, 512], F32)
                b_nrm = P.buf("nrm")
                otr = Ring(P, [S("ot%d" % i, [128, D], F32) for i in range(2)])
                for (tok0, n, cond, isc) in lat_groups:
                    P.dma("sp", hs[:, :, 0:n], hT_v[:, :, tok0:tok0 + n], reads=[hT_buf(tok0)], writes=[b_hs])
                    rms_stats(n)
                    for dc in range(NDC):
                        P.op("dve", lambda dc=dc: nc.vector.scalar_tensor_tensor(
                            out=nrm[:, dc, 0:n], in0=hs[:, dc, 0:n], scalar=fing[:, dc:dc + 1], in1=rstd[:, 0:n],
                            op0=ALU.mult, op1=ALU.mult), reads=[b_hs, b_rstd, b_small], writes=[b_nrm])
                    for tb in range(n // 128):
                        ot, otb = otr.next()
                        for q in range(4):
                            bank = 1 + (q % 2)
                            P.mm_group([lambda dc=dc, bank=bank, tb=tb: nc.tensor.transpose(
                                out=ps[bank][:, (dc % 4) * 128:(dc % 4 + 1) * 128], in_=nrm[:, dc, tb * 128:(tb + 1) * 128],
                                identity=ident_f[:]) for dc in range(q * 4, q * 4 + 4)], reads=[b_nrm, b_const], writes=[pb[bank]])
                            if q % 2 == 0:
                                P.op("dve", lambda q=q, bank=bank, ot=ot: nc.vector.tensor_copy(
                                    out=ot[:, q * 512:(q + 1) * 512], in_=ps[bank][:, :]), reads=[pb[bank]], writes=[otb])
                            else:
                                P.op("act", lambda q=q, bank=bank, ot=ot: nc.scalar.copy(
                                    out=ot[:, q * 512:(q + 1) * 512], in_=ps[bank][:, :]), reads=[pb[bank]], writes=[otb])
                        P.dma("sp", out_d[tok0 + tb * 128:tok0 + (tb + 1) * 128, :], ot[:], reads=[otb], writes=[b_out])
                P.barrier()

        for li, lay in enumerate(layers):
            if lay.get("skip_ffn"):
                continue
            if not lay.get("ffn2only"):
                ffn_order.append((li, 0))
            if not lay.get("ffn1only"):
                ffn_order.append((li, 1))
        for li, lay in enumerate(layers):
            kind, last = lay["kind"], lay["last"]
            ctx_live = (not last) or kind != 1
            if lay.get("skip_ffn"):
                continue
            if not lay.get("ffn2only"):
                ffn_phase(li, 0, lat_groups + ([ctx_group] if ctx_live else []))
            if lay.get("ffn1only"):
                continue
            if lay.get("skip_mixer"):
                pass
            elif kind == 1:
                pool_phase(li, lat_groups + ([] if last else [ctx_group]))
            elif kind == 2:
                attn_phase(li, lat_groups + [ctx_group], not last)
            elif kind == 0:
                ssm_phase(li, lat_groups + [ctx_group], not last)
            ffn_phase(li, 1, lat_groups + ([] if last else [ctx_group]))
        final_phase()
        if debug:
            dbg_mod = nc.dram_tensor("dbg_mod", [128, NL * NMOD * NDC * 2], F32, kind="ExternalOutput").ap()
            P.dma("sp", dbg_mod[:, :], modsb[:].rearrange("p l e c -> p (l e c)"), reads=[b_mod], writes=[b_out])
            dbg_sc1 = nc.dram_tensor("dbg_sc1", [128, NL * 3 * NDC * 2], F32, kind="ExternalOutput").ap()
            P.dma("sp", dbg_sc1[:, :], sc1[:].rearrange("p l k e c -> p (l k e c)"), reads=[b_mod], writes=[b_out])
            dbg_gate = nc.dram_tensor("dbg_gate", [128, NL * 3 * NDC * 2], F32, kind="ExternalOutput").ap()
            P.dma("sp", dbg_gate[:, :], gate[:].rearrange("p l k e c -> p (l k e c)"), reads=[b_mod], writes=[b_out])
            dbg_rstd = nc.dram_tensor("dbg_rstd", [128, 512], F32, kind="ExternalOutput").ap()
            P.dma("sp", dbg_rstd[:, :], rstd[:], reads=[b_rstd], writes=[b_out])
        fin = [b_out] + list(hreg.values()) if debug else [b_out]
        P.finish(fin)
    return nc


def _pl(v):
    v = np.asarray(v, np.float32)
    lead = v.shape[:-1]
    n = v.shape[-1] // 128
    v = v.reshape(lead + (n, 128))
    v = np.moveaxis(v, -1, 0)
    return np.ascontiguousarray(v)


def _rope_consts(L):
    t = np.arange(L)
    row = (t // GRID_W).astype(np.float32)
    col = (t % GRID_W).astype(np.float32)
    inv_freq = (10000.0 ** (-np.arange(0, 64, 2, dtype=np.float32) / 64.0)).astype(np.float32)
    ang = np.concatenate([row[:, None] * inv_freq, col[:, None] * inv_freq], axis=-1).astype(np.float32)
    cosT = np.repeat(np.cos(ang).T, 2, axis=0).astype(np.float32)
    sinT = np.repeat(np.sin(ang).T, 2, axis=0).astype(np.float32)
    rot = np.zeros((128, 128), np.float32)
    for i in range(64):
        rot[2 * i + 1, 2 * i] = -1.0
        rot[2 * i, 2 * i + 1] = 1.0
    return {"cosT": np.ascontiguousarray(cosT), "sinT": np.ascontiguousarray(sinT), "rotT": rot}


def make_in_maps(inputs, L, layer_ids, batch_ids):
    depth = inputs["ada_w"].shape[0]
    shared = {}
    shared["final_g"] = _pl(inputs["final_g"])
    for li, i in enumerate(layer_ids):
        kind, j = i % 3, i // 3
        shared["ada_w%d" % li] = inputs["ada_w"][i]
        shared["ada_b%d" % li] = _pl(inputs["ada_b"][i])
        shared["norm_g%d" % li] = _pl(inputs["norm_g"][i])
        for k in range(2):
            shared["ffn_w_in%d_%d" % (li, k)] = inputs["ffn_w_in"][i, k]
            shared["ffn_w_out%d_%d" % (li, k)] = inputs["ffn_w_out"][i, k]
        if kind == 0:
            shared["ssm_w_in%d" % li] = inputs["ssm_w_in"][j]
            cwv = np.asarray(inputs["ssm_conv_w"][j], np.float32)
            shared["ssm_cw%d" % li] = np.ascontiguousarray(cwv.reshape(7, 48, 128).transpose(2, 1, 0))
            shared["ssm_cb%d" % li] = _pl(inputs["ssm_conv_b"][j])
            shared["ssm_dtb%d" % li] = np.ascontiguousarray(np.asarray(inputs["ssm_dt_bias"][j], np.float32).reshape(128, 1))
            shared["ssm_alog%d" % li] = np.ascontiguousarray(np.asarray(inputs["ssm_a_log"][j], np.float32).reshape(128, 1))
            shared["ssm_dvec%d" % li] = np.ascontiguousarray(np.broadcast_to(np.asarray(inputs["ssm_d"][j], np.float32)[None, :], (128, SH)))
            shared["ssm_gn%d" % li] = np.ascontiguousarray(np.broadcast_to(np.asarray(inputs["ssm_norm_g"][j], np.float32)[None, :], (128, SI)))
            shared["ssm_w_out%d" % li] = inputs["ssm_w_out"][j]
        if kind == 2:
            shared["attn_w_qkv%d" % li] = inputs["attn_w_qkv"][j]
            shared["attn_w_o%d" % li] = inputs["attn_w_o"][j]
            shared["attn_sinkb%d" % li] = np.ascontiguousarray(np.broadcast_to(np.asarray(inputs["attn_sink"][j], np.float32)[None, :], (128, NH)))
            shared.update(_rope_consts(L))
        if kind == 1:
            shared["pool_w%d" % li] = inputs["pool_w"][j]
            shared["pool_scale%d" % li] = _pl(inputs["pool_scale"][j])
    maps = []
    for b in batch_ids:
        m = dict(shared)
        m["x"] = np.ascontiguousarray(inputs["x"][b, :L])
        m["ctx"] = np.ascontiguousarray(inputs["ctx"][b])
        cv = np.stack([inputs["c"][b], inputs["c_ctx"]], axis=-1)
        m["cvec"] = np.ascontiguousarray(cv.reshape(NDC, 128, 2).transpose(1, 0, 2))
        maps.append(m)
    return maps


def kernel(**inputs):
    B, L = inputs["x"].shape[:2]
    depth = inputs["ada_w"].shape[0]
    layers = [{"idx": i, "kind": i % 3, "last": i == depth - 1} for i in range(depth)]
    nc = build_program(L, layers)
    maps = make_in_maps(inputs, L, list(range(depth)), list(range(B)))
    res = run_bass_kernel_spmd(nc, maps, core_ids=list(range(B)))
    return np.stack([r["out"] for r in res.results], axis=0).astype(np.float32)
```

```python
import numpy as np
from contextlib import ExitStack
import concourse.bass as bass
import concourse.mybir as mybir
from concourse.bass_utils import run_bass_kernel_spmd

F32 = mybir.dt.float32
BF16 = mybir.dt.bfloat16
AF = mybir.ActivationFunctionType
ALU = mybir.AluOpType
AX = mybir.AxisListType

D = 2048
NDC = 16
DFF = 5632
NFC = 44
CTX = 256
NMOD = 9
EPS = 1e-6
GRID_W = 64
SI = 4096
SH = 64
SP_ = 64
SG = 8
SN = 128
SCONV = 6144
SPROJ = 10368
HD = 128
NH = 16
NKV = 4
ATT_SCALE = HD ** -0.5
NEG = -30000.0


class Buf:
    __slots__ = ("name", "w", "rs")

    def __init__(self, name):
        self.name = name
        self.w = None
        self.rs = {}


class Prog:
    def __init__(self, nc, es):
        self.nc = nc
        self.es = es
        self.eng = {"pe": nc.tensor, "act": nc.scalar, "dve": nc.vector, "pool": nc.gpsimd, "sp": nc.sync}
        self.sem = {}
        self.cnt = {}
        for e in ("pe", "act", "dve", "pool"):
            self.sem[e] = es.enter_context(nc.semaphore("s_" + e))
            self.cnt[e] = 0
        self.waited = {e: {} for e in self.eng}
        self.dq = {}
        for q in ("sp", "pool", "act"):
            sems = [es.enter_context(nc.semaphore("d_%s%d" % (q, i))) for i in range(8)]
            self.dq[q] = {"sems": sems, "cnt": [0] * 8, "i": 0}
        self.nbuf = 0

    def uid(self):
        self.nbuf += 1
        return self.nbuf

    def buf(self, name=None):
        self.nbuf += 1
        return Buf(name or ("b%d" % self.nbuf))

    def _wait(self, e, tok):
        if tok is None:
            return
        sem, val, owner = tok
        if owner == e and e == "pe":
            return
        key = id(sem)
        if self.waited[e].get(key, 0) >= val:
            return
        self.waited[e][key] = val
        self.eng[e].wait_ge(sem, val)

    def _deps(self, e, reads, writes, same_ok=False):
        def w(t):
            if same_ok and t is not None and t[2] == e:
                return
            self._wait(e, t)
        for b in reads:
            w(b.w)
        for b in writes:
            w(b.w)
            for t in list(b.rs.values()):
                w(t)

    def _commit(self, tok, reads, writes):
        for b in writes:
            b.w = tok
            b.rs = {}
        for b in reads:
            key = id(tok[0])
            old = b.rs.get(key)
            if old is None or old[1] < tok[1]:
                b.rs[key] = tok

    def op(self, e, fn, reads=(), writes=(), mark=True, same_ok=False):
        self._deps(e, reads, writes, same_ok)
        ins = fn()
        if mark:
            self.cnt[e] += 1
            ins.then_inc(self.sem[e], 1)
            tok = (self.sem[e], self.cnt[e], e)
            self._commit(tok, reads, writes)
        return ins

    def mm_group(self, mms, reads, writes):
        self._deps("pe", reads, writes)
        ins = None
        for f in mms:
            ins = f()
        self.cnt["pe"] += 1
        ins.then_inc(self.sem["pe"], 1)
        tok = (self.sem["pe"], self.cnt["pe"], "pe")
        self._commit(tok, reads, writes)

    def dma(self, q, out, in_, reads=(), writes=(), **kw):
        self._deps(q, reads, writes)
        d = self.dq[q]
        i = d["i"]
        d["i"] = (i + 1) % 8
        sem = d["sems"][i]
        if d["cnt"][i] > 0:
            self._wait(q, (sem, d["cnt"][i], "dma"))
        d["cnt"][i] += 16
        self.eng[q].dma_start(out=out, in_=in_, **kw).then_inc(sem, 16)
        tok = (sem, d["cnt"][i], "dma")
        self._commit(tok, reads, writes)
        return tok

    def barrier(self):
        toks = [(self.sem[e], self.cnt[e], e) for e in self.cnt if self.cnt[e] > 0]
        for q, d in self.dq.items():
            for sem, c in zip(d["sems"], d["cnt"]):
                if c > 0:
                    toks.append((sem, c, "dma"))
        for e in self.eng:
            for t in toks:
                self._wait(e, t)

    def finish(self, bufs):
        for b in bufs:
            self._wait("sp", b.w)


class Ring:
    def __init__(self, P, tiles):
        self.tiles = tiles
        self.bufs = [P.buf() for _ in tiles]
        self.i = 0

    def next(self):
        i = self.i
        self.i = (i + 1) % len(self.tiles)
        return self.tiles[i], self.bufs[i]


def build_program(L, layers, debug=False):
    nc = bass.Bass("TRN2", target_bir_lowering=False)
    NT = L + CTX
    NL = len(layers)

    def din(name, shape, dt=F32):
        return nc.dram_tensor(name, list(shape), dt, kind="ExternalInput").ap()

    x_d = din("x", [L, D])
    ctx_d = din("ctx", [CTX, D])
    cvec_d = din("cvec", [128, NDC, 2])
    fing_d = din("final_g", [128, NDC])
    has_attn = any(l["kind"] == 2 for l in layers)
    if has_attn:
        cosT_d = din("cosT", [128, L])
        sinT_d = din("sinT", [128, L])
        rot_d = din("rotT", [128, 128])
        qT_d = nc.dram_tensor("qT_s", [128, NH, NT], BF16, kind="Internal").ap()
        kT_d = nc.dram_tensor("kT_s", [128, NKV, NT], BF16, kind="Internal").ap()
        v_d = nc.dram_tensor("v_s", [NT, NKV * HD], BF16, kind="Internal").ap()
    has_ssm = any(l["kind"] == 0 for l in layers)
    if has_ssm:
        def dscr(name, shape, dt):
            return nc.dram_tensor(name, list(shape), dt, kind="Internal").ap()
        xbc_d = dscr("m_xbc", [128, 48, NT], F32)
        dtr_d = dscr("m_dtr", [128, NT], F32)
        z_d = dscr("m_z", [NT, SI], F32)
        xs_d = dscr("m_xs", [NT, SI], BF16)
        BT_d = dscr("m_BT", [128, SG, NT], BF16)
        CT_d = dscr("m_CT", [128, SG, NT], BF16)
        Btm_d = dscr("m_Btm", [NT, SG * SN], BF16)
        dttm_d = dscr("m_dttm", [NT, 128], F32)
        dtAtm_d = dscr("m_dtAtm", [NT, 128], F32)
        acsfm_d = dscr("m_acsfm", [128, NT], F32)
        yf_d = dscr("m_yf", [NT, SI], F32)
        uT_d = dscr("m_uT", [128, 32, NT], BF16)
    out_d = nc.dram_tensor("out", [L, D], F32, kind="ExternalOutput").ap()
    hT_d = nc.dram_tensor("hT", [D, NT], F32, kind=("ExternalOutput" if debug else "Internal")).ap()
    hT_v = hT_d.rearrange("(dc p) t -> p dc t", p=128)
    W = []
    for li, lay in enumerate(layers):
        w = {}
        w["ada_w"] = din("ada_w%d" % li, [D, NMOD * D])
        w["ada_b"] = din("ada_b%d" % li, [128, NMOD * NDC])
        w["norm_g"] = din("norm_g%d" % li, [128, 3, NDC])
        w["ffn_w_in"] = [din("ffn_w_in%d_%d" % (li, k), [D, 2 * DFF]) for k in range(2)]
        w["ffn_w_out"] = [din("ffn_w_out%d_%d" % (li, k), [DFF, D]) for k in range(2)]
        if lay["kind"] == 0:
            w["s_w_in"] = din("ssm_w_in%d" % li, [D, SPROJ])
            w["s_cw"] = din("ssm_cw%d" % li, [128, 48, 7])
            w["s_cb"] = din("ssm_cb%d" % li, [128, 48])
            w["s_dtb"] = din("ssm_dtb%d" % li, [128, 1])
            w["s_alog"] = din("ssm_alog%d" % li, [128, 1])
            w["s_dvec"] = din("ssm_dvec%d" % li, [128, SH])
            w["s_gn"] = din("ssm_gn%d" % li, [128, SI])
            w["s_w_out"] = din("ssm_w_out%d" % li, [SI, D])
        if lay["kind"] == 2:
            w["w_qkv"] = din("attn_w_qkv%d" % li, [D, 3072])
            w["w_o"] = din("attn_w_o%d" % li, [D, D])
            w["sinkb"] = din("attn_sinkb%d" % li, [128, NH])
        if lay["kind"] == 1:
            w["pool_w"] = din("pool_w%d" % li, [4, 512, 512])
            w["pool_scale"] = din("pool_scale%d" % li, [128, NDC])
        W.append(w)

    lay_dbg_k = 1 if layers[0].get("ffn2only") else 0
    lat_groups = [(g * 512, 512, 0, False) for g in range(L // 512)]
    ctx_group = (L, CTX, 1, True)

    es = ExitStack()
    with es:
        P = Prog(nc, es)
        E = es.enter_context

        def sb(name, shape, dt):
            return E(nc.sbuf_tensor("sb_" + name, list(shape), dt))

        ones_bf = sb("ones_bf", [128, 128], BF16)
        ident_f = sb("ident_f", [128, 128], F32)
        b_const = P.buf("const")
        P.op("pool", lambda: nc.gpsimd.memset(ones_bf[:], 1.0), writes=[b_const])
        P.op("pool", lambda: nc.gpsimd.memset(ident_f[:], 0.0), writes=[b_const])
        P.op("pool", lambda: nc.gpsimd.affine_select(out=ident_f[:], in_=ident_f[:], compare_op=ALU.not_equal, fill=1.0,
                                                     base=0, pattern=[[-1, 128]], channel_multiplier=1), writes=[b_const])

        ident_b = sb("ident_b", [128, 128], BF16)
        P.op("pool", lambda: nc.gpsimd.tensor_copy(out=ident_b[:], in_=ident_f[:]), reads=[b_const], writes=[b_const])
        if has_ssm:
            triF = sb("triF", [128, 128], F32)
            triB = sb("triB", [128, 128], F32)
            mbF = sb("mbF", [128, 128], F32)
            mbB = sb("mbB", [128, 128], F32)
            ones_f = sb("ones_f", [128, 128], F32)
            P.op("pool", lambda: nc.gpsimd.memset(ones_f[:], 1.0), writes=[b_const])
            for (t_, v_) in ((triF, 1.0), (triB, 1.0), (mbF, 0.0), (mbB, 0.0)):
                P.op("pool", lambda t_=t_, v_=v_: nc.gpsimd.memset(t_[:], v_), writes=[b_const])
            P.op("pool", lambda: nc.gpsimd.affine_select(out=triF[:], in_=triF[:], compare_op=ALU.is_ge, fill=0.0, base=0,
                                                         pattern=[[1, 128]], channel_multiplier=-1), writes=[b_const])
            P.op("pool", lambda: nc.gpsimd.affine_select(out=triB[:], in_=triB[:], compare_op=ALU.is_ge, fill=0.0, base=0,
                                                         pattern=[[-1, 128]], channel_multiplier=1), writes=[b_const])
            P.op("pool", lambda: nc.gpsimd.affine_select(out=mbF[:], in_=mbF[:], compare_op=ALU.is_ge, fill=NEG, base=0,
                                                         pattern=[[1, 128]], channel_multiplier=-1), writes=[b_const])
            P.op("pool", lambda: nc.gpsimd.affine_select(out=mbB[:], in_=mbB[:], compare_op=ALU.is_ge, fill=NEG, base=0,
                                                         pattern=[[-1, 128]], channel_multiplier=1), writes=[b_const])
        if has_attn:
            maskA = sb("maskA", [128, 2, 512], F32)
            maskB = sb("maskB", [128, 2, 128], F32)
            P.op("pool", lambda: nc.gpsimd.memset(maskA[:], 0.0), writes=[b_const])
            P.op("pool", lambda: nc.gpsimd.memset(maskB[:], 0.0), writes=[b_const])
            P.op("pool", lambda: nc.gpsimd.affine_select(out=maskA[:, 0, 256:384], in_=maskA[:, 0, 256:384], compare_op=ALU.is_ge,
                                                         fill=NEG, base=0, pattern=[[1, 128]], channel_multiplier=-1), writes=[b_const])
            P.op("pool", lambda: nc.gpsimd.memset(maskA[:, 1, 256:384], NEG), writes=[b_const])
            P.op("pool", lambda: nc.gpsimd.affine_select(out=maskB[:, 0, :], in_=maskB[:, 0, :], compare_op=ALU.is_ge,
                                                         fill=NEG, base=0, pattern=[[-1, 128]], channel_multiplier=1), writes=[b_const])
            P.op("pool", lambda: nc.gpsimd.memset(maskB[:, 1, :], NEG), writes=[b_const])
            rot_b = sb("rot_b", [128, 128], BF16)
            rot_f = sb("rot_f", [128, 128], F32)
            b_rot = P.buf("rot")
            P.dma("sp", rot_f[:], rot_d[:, :], writes=[b_rot])
            P.op("dve", lambda: nc.vector.tensor_copy(out=rot_b[:], in_=rot_f[:]), reads=[b_rot], writes=[b_rot])

        cvec = sb("cvec", [128, NDC, 2], F32)
        sc_bf = sb("sc_bf", [128, NDC, 2], BF16)
        modsb = sb("modsb", [128, NL, NMOD * NDC, 2], F32)
        adab = sb("adab", [128, NL, NMOD * NDC], F32)
        normg = sb("normg", [128, NL, 3, NDC], F32)
        sc1 = sb("sc1", [128, NL, 3, NDC, 2], F32)
        gate = sb("gate", [128, NL, 3, NDC, 2], F32)
        fing = sb("fing", [128, NDC], F32)
        b_mod = P.buf("mod")
        b_small = P.buf("small")
        P.dma("sp", cvec[:], cvec_d[:, :, :], writes=[b_small])
        P.dma("sp", fing[:], fing_d[:, :], writes=[b_small])
        for li in range(NL):
            P.dma("sp", adab[:, li, :], W[li]["ada_b"][:, :], writes=[b_small])
            P.dma("sp", normg[:, li, :, :], W[li]["norm_g"][:, :, :], writes=[b_small])
        b_sc = P.buf("sc")
        P.op("act", lambda: nc.scalar.activation(out=sc_bf[:], in_=cvec[:], func=AF.Silu), reads=[b_small], writes=[b_sc])

        ps = [E(nc.psum_tensor("ps%d" % i, [128, 512], F32)) for i in range(8)]
        pb = [P.buf("psb%d" % i) for i in range(8)]

        with ExitStack() as es2:
            adat = [es2.enter_context(nc.sbuf_tensor("sb_adat%d" % i, [128, NDC, 512], BF16)) for i in range(3)]
            aring = Ring(P, adat)
            for li in range(NL):
                aw = W[li]["ada_w"].rearrange("(dc p) e -> p dc e", p=128)
                for et in range(NMOD * D // 512):
                    t, tb = aring.next()
                    P.dma("pool", t[:], aw[:, :, et * 512:(et + 1) * 512], writes=[tb])
                    mms = []
                    for j in range(4):
                        ec = et * 4 + j
                        for dc in range(NDC):
                            mms.append(lambda j=j, ec=ec, dc=dc, t=t: nc.tensor.matmul(
                                ps[0][:, ec * 2:ec * 2 + 2], lhsT=t[:, dc, j * 128:(j + 1) * 128], rhs=sc_bf[:, dc, :],
                                start=(dc == 0), stop=(dc == NDC - 1)))
                    P.mm_group(mms, reads=[tb, b_sc], writes=[pb[0]])
                P.op("dve", lambda li=li: nc.vector.tensor_tensor(
                    out=modsb[:, li, :, :], in0=ps[0][:, 0:NMOD * NDC * 2].rearrange("p (e c) -> p e c", c=2),
                    in1=adab[:, li, :].unsqueeze(2).to_broadcast([128, NMOD * NDC, 2]), op=ALU.add),
                    reads=[pb[0], b_small], writes=[b_mod])
                for k in range(3):
                    P.op("dve", lambda li=li, k=k: nc.vector.tensor_scalar(
                        out=sc1[:, li, k, :, :], in0=modsb[:, li, (3 * k + 1) * NDC:(3 * k + 2) * NDC, :],
                        scalar1=1.0, scalar2=None, op0=ALU.add), reads=[b_mod], writes=[b_mod])
                    P.op("dve", lambda li=li, k=k: nc.vector.tensor_tensor(
                        out=sc1[:, li, k, :, :], in0=sc1[:, li, k, :, :],
                        in1=normg[:, li, k, :].unsqueeze(2).to_broadcast([128, NDC, 2]), op=ALU.mult),
                        reads=[b_mod, b_small], writes=[b_mod])
                    P.op("dve", lambda li=li, k=k: nc.vector.tensor_scalar(
                        out=gate[:, li, k, :, :], in0=modsb[:, li, (3 * k + 2) * NDC:(3 * k + 3) * NDC, :],
                        scalar1=(1.0 if k == 1 else 0.5), scalar2=None, op0=ALU.mult), reads=[b_mod], writes=[b_mod])

            P.barrier()

        def shift_ap(li, k, dc, cond):
            return modsb[:, li, 3 * k * NDC + dc, cond:cond + 1]

        hs = sb("hs", [128, NDC, 512], F32)
        b_hs = P.buf("hs")
        sq = sb("sq", [128, NDC, 528], BF16)
        b_sq = P.buf("sq")
        rstd = sb("rstd", [128, 512], F32)
        b_rstd = P.buf("rstd")
        tmpr = Ring(P, [sb("tmp%d" % i, [128, 512], F32) for i in range(3)])
        hreg = {}

        def hT_buf(tok0):
            if tok0 not in hreg:
                hreg[tok0] = P.buf("hT%d" % tok0)
            return hreg[tok0]

        es_x = ExitStack()
        xin = Ring(P, [es_x.enter_context(nc.sbuf_tensor("sb_xin%d" % i, [128, D], F32)) for i in range(2)])

        def load_transpose(src, tok0, n):
            for tb in range(n // 128):
                t, tbuf = xin.next()
                P.dma("sp", t[:], src[tb * 128:(tb + 1) * 128, :], writes=[tbuf])
                for q in range(4):
                    bank = 1 + (q % 2)
                    mms = [lambda dc=dc, t=t, bank=bank: nc.tensor.transpose(
                        out=ps[bank][:, (dc % 4) * 128:(dc % 4 + 1) * 128], in_=t[:, dc * 128:(dc + 1) * 128],
                        identity=ident_f[:]) for dc in range(q * 4, q * 4 + 4)]
                    P.mm_group(mms, reads=[tbuf, b_const], writes=[pb[bank]])
                    P.op("dve" if q % 2 == 0 else "act",
                         (lambda q=q, bank=bank, tb=tb: nc.vector.tensor_copy(
                             out=hs[:, q * 4:q * 4 + 4, tb * 128:(tb + 1) * 128],
                             in_=ps[bank][:, :].rearrange("p (c t) -> p c t", c=4))) if q % 2 == 0 else
                         (lambda q=q, bank=bank, tb=tb: nc.scalar.copy(
                             out=hs[:, q * 4:q * 4 + 4, tb * 128:(tb + 1) * 128],
                             in_=ps[bank][:, :].rearrange("p (c t) -> p c t", c=4))),
                         reads=[pb[bank]], writes=[b_hs])
            P.dma("sp", hT_v[:, :, tok0:tok0 + n], hs[:, :, 0:n], reads=[b_hs], writes=[hT_buf(tok0)])

        for (tok0, n, cond, isc) in lat_groups:
            load_transpose(x_d[tok0:tok0 + n, :], tok0, n)
        load_transpose(ctx_d, L, CTX)
        P.barrier()
        es_x.close()

        def rms_stats(n):
            P.op("act", lambda: nc.scalar.activation(out=sq[:, :, 0:n], in_=hs[:, :, 0:n], func=AF.Square),
                 reads=[b_hs], writes=[b_sq])
            mms = [lambda dc=dc: nc.tensor.matmul(ps[0][:, 0:n], lhsT=ones_bf[:], rhs=sq[:, dc, 0:n],
                                                  start=(dc == 0), stop=(dc == NDC - 1)) for dc in range(NDC)]
            P.mm_group(mms, reads=[b_sq, b_const], writes=[pb[0]])
            P.op("act", lambda: nc.scalar.activation(out=rstd[:, 0:n], in_=ps[0][:, 0:n], func=AF.Sqrt,
                                                     bias=EPS, scale=1.0 / D), reads=[pb[0]], writes=[b_rstd])
            P.op("dve", lambda: nc.vector.reciprocal(out=rstd[:, 0:n], in_=rstd[:, 0:n]), reads=[b_rstd], writes=[b_rstd])

        def modulate(dst, b_dst, n, li, k, cond):
            for dc in range(NDC):
                t, tb = tmpr.next()
                P.op("dve", lambda dc=dc, t=t: nc.vector.scalar_tensor_tensor(
                    out=t[:, 0:n], in0=hs[:, dc, 0:n], scalar=sc1[:, li, k, dc, cond:cond + 1], in1=rstd[:, 0:n],
                    op0=ALU.mult, op1=ALU.mult), reads=[b_hs, b_rstd, b_mod], writes=[tb])
                P.op("act", lambda dc=dc, t=t: nc.scalar.activation(
                    out=dst[:, dc, 0:n], in_=t[:, 0:n], func=AF.Identity, bias=shift_ap(li, k, dc, cond), scale=1.0),
                    reads=[tb, b_mod], writes=[b_dst])

        b_out = P.buf("out")
        hT2_d = nc.dram_tensor("hT2", [D, NT], F32, kind="Internal").ap()
        hT2_v = hT2_d.rearrange("(dc p) t -> p dc t", p=128)
        b_h2 = P.buf("hT2")
        ffn_scr = {}

        def ffn_convert(li, k):
            if (li, k) in ffn_scr:
                return
            w1s = nc.dram_tensor("w1s_%d_%d" % (li, k), [NFC, 128, NDC, 2, 128], BF16, kind="Internal").ap()
            w2s = nc.dram_tensor("w2s_%d_%d" % (li, k), [NDC, 128, NFC, 128], BF16, kind="Internal").ap()
            b1 = [P.buf() for _ in range(NFC)]
            b2 = [P.buf() for _ in range(NDC)]
            w_in = W[li]["ffn_w_in"][k].rearrange("(dc p) f -> p dc f", p=128)
            w_out = W[li]["ffn_w_out"][k].rearrange("(fc p) d -> p fc d", p=128)
            for fc in range(NFC):
                P.dma("pool", w1s[fc, :, :, 0, :], w_in[:, :, fc * 128:(fc + 1) * 128], writes=[b1[fc]])
                P.dma("pool", w1s[fc, :, :, 1, :], w_in[:, :, DFF + fc * 128:DFF + (fc + 1) * 128], writes=[b1[fc]])
            for dc in range(NDC):
                P.dma("pool", w2s[dc, :, :, :], w_out[:, :, dc * 128:(dc + 1) * 128], writes=[b2[dc]])
            ffn_scr[(li, k)] = (w1s, w2s, b1, b2)

        ffn_order = []

        def ffn_prefetch_next(li, k):
            i = ffn_order.index((li, k))
            if i + 1 < len(ffn_order):
                ffn_convert(*ffn_order[i + 1])

        def ffn_phase(li, k, groups):
            kk = 0 if k == 0 else 2
            ffn_convert(li, k)
            w1s, w2s, b1s, b2s = ffn_scr[(li, k)]
            ffn_prefetch_next(li, k)
            with ExitStack() as es3:
                S = lambda name, shape, dt: es3.enter_context(nc.sbuf_tensor("sb_%s_%d" % (name, P.uid()), list(shape), dt))
                aT = S("aT", [128, NDC, 512], BF16)
                b_aT = P.buf("aT")
                hid = S("hid", [128, NFC, 512], BF16)
                b_hid = P.buf("hid")
                w1r = Ring(P, [S("w1_%d" % i, [128, NDC, 2, 128], BF16) for i in range(4)])
                w2r = Ring(P, [S("w2_%d" % i, [128, NFC, 128], BF16) for i in range(3)])
                sgr = Ring(P, [S("sg%d" % i, [128, 512], F32) for i in range(2)])
                w1u = {}
                for (tok0, n, cond, isc) in groups:
                    P.dma("sp", hs[:, :, 0:n], hT_v[:, :, tok0:tok0 + n], reads=[hT_buf(tok0)], writes=[b_hs])
                    rms_stats(n)
                    modulate(aT, b_aT, n, li, kk, cond)
                    for fc in range(NFC):
                        wt, wb = w1r.next()
                        wb2 = w1u.setdefault(id(wb), P.buf())
                        P.dma("sp", wt[:], w1s[fc], reads=[b1s[fc]], writes=[wb, wb2])
                        bg = 1 + (fc % 2)
                        bu = 3 if fc % 2 == 0 else 7
                        P.mm_group([lambda dc=dc, wt=wt, bg=bg: nc.tensor.matmul(
                            ps[bg][:, 0:n], lhsT=wt[:, dc, 0, :], rhs=aT[:, dc, 0:n], start=(dc == 0), stop=(dc == NDC - 1))
                            for dc in range(NDC)], reads=[wb, b_aT], writes=[pb[bg]])
                        P.mm_group([lambda dc=dc, wt=wt, bu=bu: nc.tensor.matmul(
                            ps[bu][:, 0:n], lhsT=wt[:, dc, 1, :], rhs=aT[:, dc, 0:n], start=(dc == 0), stop=(dc == NDC - 1))
                            for dc in range(NDC)], reads=[wb2, b_aT], writes=[pb[bu]])
                        sg, sgb = sgr.next()
                        P.op("act", lambda sg=sg, bg=bg: nc.scalar.activation(out=sg[:, 0:n], in_=ps[bg][:, 0:n], func=AF.Silu),
                             reads=[pb[bg]], writes=[sgb])
                        P.op("dve", lambda sg=sg, bu=bu, fc=fc: nc.vector.tensor_tensor(
                            out=hid[:, fc, 0:n], in0=sg[:, 0:n], in1=ps[bu][:, 0:n], op=ALU.mult),
                            reads=[sgb, pb[bu]], writes=[b_hid])
                    for dc in range(NDC):
                        wt, wb = w2r.next()
                        P.dma("sp", wt[:], w2s[dc], reads=[b2s[dc]], writes=[wb])
                        by = 5 + (dc % 2)
                        P.mm_group([lambda fc=fc, wt=wt, by=by: nc.tensor.matmul(
                            ps[by][:, 0:n], lhsT=wt[:, fc, :], rhs=hid[:, fc, 0:n], start=(fc == 0), stop=(fc == NFC - 1))
                            for fc in range(NFC)], reads=[wb, b_hid], writes=[pb[by]])
                        P.op("dve", lambda dc=dc, by=by: nc.vector.scalar_tensor_tensor(
                            out=hs[:, dc, 0:n], in0=ps[by][:, 0:n], scalar=gate[:, li, kk, dc, cond:cond + 1],
                            in1=hs[:, dc, 0:n], op0=ALU.mult, op1=ALU.add), reads=[pb[by], b_hs, b_mod], writes=[b_hs])
                    P.dma("sp", hT_v[:, :, tok0:tok0 + n], hs[:, :, 0:n], reads=[b_hs], writes=[hT_buf(tok0)])
                if debug and li == 0 and k == lay_dbg_k:
                    dbg_aT = nc.dram_tensor("dbg_aT", [128, NDC * 512], BF16, kind="ExternalOutput").ap()
                    P.dma("sp", dbg_aT[:, :], aT[:].rearrange("p a b -> p (a b)"), reads=[b_aT], writes=[b_out])
                    dbg_hid = nc.dram_tensor("dbg_hid", [128, NFC * 512], BF16, kind="ExternalOutput").ap()
                    P.dma("sp", dbg_hid[:, :], hid[:].rearrange("p a b -> p (a b)"), reads=[b_hid], writes=[b_out])
                P.barrier()

        def pool_phase(li, groups):
            HALO = 8
            pw_d = W[li]["pool_w"]
            with ExitStack() as es3:
                S = lambda name, shape, dt: es3.enter_context(nc.sbuf_tensor("sb_%s_%d" % (name, P.uid()), list(shape), dt))
                pw = S("pw", [128, 4, 4, 512], BF16)
                b_pw = P.buf("pw")
                psc = S("psc", [128, NDC], F32)
                gsc = S("gsc", [128, NDC, 2], F32)
                for g in range(4):
                    P.dma("pool", pw[:, g, :, :], pw_d[g].rearrange("(cc p) e -> p cc e", p=128), writes=[b_pw])
                P.dma("sp", psc[:], W[li]["pool_scale"][:, :], writes=[b_pw])
                P.op("dve", lambda: nc.vector.tensor_tensor(out=gsc[:], in0=gate[:, li, 1, :, :],
                                                            in1=psc[:].unsqueeze(2).to_broadcast([128, NDC, 2]), op=ALU.mult),
                     reads=[b_pw, b_mod], writes=[b_pw])
                NW = 512 + 2 * HALO
                hh = S("hh", [128, NDC, NW], F32)
                b_hh = P.buf("hh")
                a32 = S("a32", [128, NDC, NW], F32)
                b_a = P.buf("a32")
                s1 = S("s1", [128, 4, NW], F32)
                s2 = S("s2", [128, 4, NW], F32)
                b_s = P.buf("s12")
                pooled = S("pooled", [128, NDC, 512], BF16)
                b_pl = P.buf("pooled")
                rsth = S("rsth", [128, NW], F32)
                b_rh = P.buf("rsth")
                sqh, b_sqh = sq, b_sq
                cnt = S("cnt", [128, 4, 512], F32)
                b_cnt = P.buf("cnt")
                for (tok0, n, cond, isc) in groups:
                    seq0 = L if isc else 0
                    seqn = CTX if isc else L
                    lo = max(tok0 - HALO, seq0)
                    hi = min(tok0 + n + HALO, seq0 + seqn)
                    c0 = lo - (tok0 - HALO)
                    c1 = hi - (tok0 - HALO)
                    nw = n + 2 * HALO
                    if c0 > 0 or c1 < nw:
                        P.op("pool", lambda: nc.gpsimd.memset(hh[:, :, 0:nw], 1.0), writes=[b_hh])
                    rd = [hT_buf(t) for t in hreg if (t < hi and t + 512 > lo)]
                    P.dma("sp", hh[:, :, c0:c1], hT_v[:, :, lo:hi], reads=rd, writes=[b_hh])
                    P.op("act", lambda: nc.scalar.activation(out=sqh[:, :, 0:nw], in_=hh[:, :, 0:nw], func=AF.Square),
                         reads=[b_hh], writes=[b_sqh])
                    for (a, b, bank) in ((0, min(512, nw), 0), (512, nw, 7)):
                        if b <= a:
                            continue
                        P.mm_group([lambda dc=dc, a=a, b=b, bank=bank: nc.tensor.matmul(
                            ps[bank][:, 0:b - a], lhsT=ones_bf[:], rhs=sqh[:, dc, a:b], start=(dc == 0), stop=(dc == NDC - 1))
                            for dc in range(NDC)], reads=[b_sqh, b_const], writes=[pb[bank]])
                        P.op("act", lambda a=a, b=b, bank=bank: nc.scalar.activation(
                            out=rsth[:, a:b], in_=ps[bank][:, 0:b - a], func=AF.Sqrt, bias=EPS, scale=1.0 / D),
                            reads=[pb[bank]], writes=[b_rh])
                    P.op("dve", lambda: nc.vector.reciprocal(out=rsth[:, 0:nw], in_=rsth[:, 0:nw]), reads=[b_rh], writes=[b_rh])
                    for dc in range(NDC):
                        t, tb = tmpr.next()
                        P.op("dve", lambda dc=dc: nc.vector.scalar_tensor_tensor(
                            out=a32[:, dc, 0:nw], in0=hh[:, dc, 0:nw], scalar=sc1[:, li, 1, dc, cond:cond + 1], in1=rsth[:, 0:nw],
                            op0=ALU.mult, op1=ALU.mult), reads=[b_hh, b_rh, b_mod], writes=[b_a])
                        P.op("act", lambda dc=dc: nc.scalar.activation(
                            out=a32[:, dc, 0:nw], in_=a32[:, dc, 0:nw], func=AF.Identity, bias=shift_ap(li, 1, dc, cond), scale=1.0),
                            reads=[b_a, b_mod], writes=[b_a])
                    if c0 > 0:
                        P.op("pool", lambda: nc.gpsimd.memset(a32[:, :, 0:c0], 0.0), reads=[b_a], writes=[b_a])
                    if c1 < nw:
                        P.op("pool", lambda: nc.gpsimd.memset(a32[:, :, c1:nw], 0.0), reads=[b_a], writes=[b_a])
                    for g, w in enumerate((2, 4, 8, 16)):
                        src = a32[:, g * 4:(g + 1) * 4, :]
                        width = 1
                        cur = src
                        dst_cycle = [s1, s2]
                        di = 0
                        while width < w:
                            dst = dst_cycle[di]
                            di ^= 1
                            P.op("dve", lambda cur=cur, dst=dst, width=width: nc.vector.tensor_tensor(
                                out=dst[:, :, width:nw], in0=cur[:, :, width:nw], in1=cur[:, :, 0:nw - width], op=ALU.add),
                                reads=[b_a, b_s], writes=[b_s])
                            cur = dst
                            width *= 2
                        off = HALO + w // 2 - 1
                        P.op("pool", lambda g=g, w=w: nc.gpsimd.memset(cnt[:, g, 0:n], float(w)), writes=[b_cnt])
                        for tl in list(range(0, min(n, 8))) + list(range(max(n - 8, 8), n)):
                            tg = tok0 + tl - seq0
                            lo_ = min(max(tg - w // 2, 0), seqn)
                            hi_ = min(max(tg - w // 2 + w, 0), seqn)
                            if hi_ - lo_ != w:
                                P.op("pool", lambda g=g, tl=tl, v=float(hi_ - lo_): nc.gpsimd.memset(cnt[:, g, tl:tl + 1], v),
                                     reads=[b_cnt], writes=[b_cnt])
                        P.op("dve", lambda g=g: nc.vector.reciprocal(out=cnt[:, g, 0:n], in_=cnt[:, g, 0:n]),
                             reads=[b_cnt], writes=[b_cnt])
                        P.op("dve", lambda cur=cur, off=off, g=g: nc.vector.tensor_tensor(
                            out=s1[:, :, 0:n] if cur is not s1 else s2[:, :, 0:n], in0=cur[:, :, off:off + n],
                            in1=cnt[:, g, 0:n].unsqueeze(1).to_broadcast([128, 4, n]), op=ALU.mult),
                            reads=[b_s, b_cnt], writes=[b_s])
                        q = s1 if cur is not s1 else s2
                        P.op("dve", lambda q=q, g=g: nc.vector.tensor_tensor(
                            out=pooled[:, g * 4:(g + 1) * 4, 0:n], in0=q[:, :, 0:n], in1=a32[:, g * 4:(g + 1) * 4, HALO:HALO + n],
                            op=ALU.subtract), reads=[b_s, b_a], writes=[b_pl])
                    P.op("dve", lambda: nc.vector.tensor_copy(out=hs[:, :, 0:n], in_=hh[:, :, HALO:HALO + n]),
                         reads=[b_hh], writes=[b_hs])
                    for g in range(4):
                        for ec in range(4):
                            dc = g * 4 + ec
                            bank = 5 + (dc % 2)
                            P.mm_group([lambda cc=cc, g=g, ec=ec, bank=bank: nc.tensor.matmul(
                                ps[bank][:, 0:n], lhsT=pw[:, g, cc, ec * 128:(ec + 1) * 128], rhs=pooled[:, g * 4 + cc, 0:n],
                                start=(cc == 0), stop=(cc == 3)) for cc in range(4)], reads=[b_pw, b_pl], writes=[pb[bank]])
                            P.op("dve", lambda dc=dc, bank=bank: nc.vector.scalar_tensor_tensor(
                                out=hs[:, dc, 0:n], in0=ps[bank][:, 0:n], scalar=gsc[:, dc, cond:cond + 1],
                                in1=hs[:, dc, 0:n], op0=ALU.mult, op1=ALU.add), reads=[pb[bank], b_hs, b_pw], writes=[b_hs])
                    P.dma("sp", hT2_v[:, :, tok0:tok0 + n], hs[:, :, 0:n], reads=[b_hs], writes=[b_h2])
                for (tok0, n, cond, isc) in groups:
                    P.dma("sp", hT_d[:, tok0:tok0 + n], hT2_d[:, tok0:tok0 + n], reads=[b_h2], writes=[hT_buf(tok0)])
                P.barrier()


        def ssm_phase(li, groups, need_ctx):
            Wl = W[li]
            w_in = Wl["s_w_in"].rearrange("(dc p) e -> p dc e", p=128)
            w_out = Wl["s_w_out"].rearrange("(fc p) d -> p fc d", p=128)
            b_xbc = P.buf(); b_dtr = P.buf(); b_z = P.buf(); b_xs = P.buf(); b_BT = P.buf(); b_CT = P.buf()
            b_Btm = P.buf(); b_dttm = P.buf(); b_dtAtm = P.buf(); b_acs = P.buf(); b_yf = P.buf(); b_uT = P.buf()
            hs_flat = hs[:].rearrange("p a b -> p (a b)")
            sq_flat = sq[:].rearrange("p a b -> p (a b)")
            with ExitStack() as es3:
                S = lambda name, shape, dt: es3.enter_context(nc.sbuf_tensor("sb_%s_%d" % (name, P.uid()), list(shape), dt))
                aT = S("aT", [128, NDC, 512], BF16)
                b_aT = P.buf("aT")
                wr = Ring(P, [S("wi%d" % i, [128, NDC, 128], BF16) for i in range(4)])
                wzr = Ring(P, [S("wz%d" % i, [128, NDC, 512], BF16) for i in range(2)])
                stg = Ring(P, [S("stg%d" % i, [128, 4, 512], F32) for i in range(2)])
                zst = Ring(P, [S("zst%d" % i, [128, 512], F32) for i in range(3)])
                for (tok0, n, cond, isc) in groups:
                    P.dma("sp", hs[:, :, 0:n], hT_v[:, :, tok0:tok0 + n], reads=[hT_buf(tok0)], writes=[b_hs])
                    rms_stats(n)
                    modulate(aT, b_aT, n, li, 1, cond)
                    for j in range(49):
                        col = (4096 + j * 128) if j < 48 else 10240
                        wt, wb = wr.next()
                        P.dma("pool", wt[:], w_in[:, :, col:col + 128], writes=[wb])
                        bk = 1 + (j % 2)
                        P.mm_group([lambda dc=dc, wt=wt, bk=bk: nc.tensor.matmul(
                            ps[bk][:, 0:n], lhsT=wt[:, dc, :], rhs=aT[:, dc, 0:n], start=(dc == 0), stop=(dc == NDC - 1))
                            for dc in range(NDC)], reads=[wb, b_aT], writes=[pb[bk]])
                        if j % 4 == 0:
                            st, stb = stg.next()
                        if j % 2 == 0:
                            P.op("dve", lambda st=st, j=j, bk=bk: nc.vector.tensor_copy(out=st[:, j % 4, 0:n], in_=ps[bk][:, 0:n]),
                                 reads=[pb[bk]], writes=[stb])
                        else:
                            P.op("act", lambda st=st, j=j, bk=bk: nc.scalar.activation(out=st[:, j % 4, 0:n], in_=ps[bk][:, 0:n], func=AF.Identity),
                                 reads=[pb[bk]], writes=[stb])
                        if j < 48 and j % 4 == 3:
                            P.dma("sp", xbc_d[:, j - 3:j + 1, tok0:tok0 + n], st[:, :, 0:n], reads=[stb], writes=[b_xbc])
                        if j == 48:
                            P.dma("sp", dtr_d[:, tok0:tok0 + n], st[:, 0, 0:n], reads=[stb], writes=[b_dtr])
                    for cbk in range(8):
                        wz, wzb = wzr.next()
                        P.dma("pool", wz[:], w_in[:, :, cbk * 512:(cbk + 1) * 512], writes=[wzb])
                        for tb in range(n // 128):
                            bk = 5 + (tb % 2)
                            P.mm_group([lambda dc=dc, tb=tb, bk=bk, wz=wz: nc.tensor.matmul(
                                ps[bk][:, :], lhsT=aT[:, dc, tb * 128:(tb + 1) * 128], rhs=wz[:, dc, :], start=(dc == 0), stop=(dc == NDC - 1))
                                for dc in range(NDC)], reads=[wzb, b_aT], writes=[pb[bk]])
                            zt, ztb = zst.next()
                            if tb % 2 == 0:
                                P.op("dve", lambda zt=zt, bk=bk: nc.vector.tensor_copy(out=zt[:], in_=ps[bk][:, :]), reads=[pb[bk]], writes=[ztb])
                            else:
                                P.op("act", lambda zt=zt, bk=bk: nc.scalar.activation(out=zt[:], in_=ps[bk][:, :], func=AF.Identity),
                                     reads=[pb[bk]], writes=[ztb])
                            P.dma("sp", z_d[tok0 + tb * 128:tok0 + (tb + 1) * 128, cbk * 512:(cbk + 1) * 512], zt[:], reads=[ztb], writes=[b_z])
                P.barrier()
            with ExitStack() as es3:
                S = lambda name, shape, dt: es3.enter_context(nc.sbuf_tensor("sb_%s_%d" % (name, P.uid()), list(shape), dt))
                cw = S("cw", [128, 48, 7], F32)
                cbias = S("cbias", [128, 48], F32)
                dtb = S("dtb", [128, 1], F32)
                avec = S("avec", [128, 1], F32)
                b_cw = P.buf("cw")
                P.dma("sp", cw[:], Wl["s_cw"][:, :, :], writes=[b_cw])
                P.dma("sp", cbias[:], Wl["s_cb"][:, :], writes=[b_cw])
                P.dma("sp", dtb[:], Wl["s_dtb"][:, :], writes=[b_cw])
                P.dma("sp", avec[:], Wl["s_alog"][:, :], writes=[b_cw])
                P.op("act", lambda: nc.scalar.activation(out=avec[:], in_=avec[:], func=AF.Exp), reads=[b_cw], writes=[b_cw])
                P.op("dve", lambda: nc.vector.tensor_scalar(out=avec[:], in0=avec[:], scalar1=-1.0, scalar2=None, op0=ALU.mult),
                     reads=[b_cw], writes=[b_cw])
                xin = S("xin", [128, 8, 518], F32)
                b_xin = P.buf("xin")
                accr = Ring(P, [S("acc%d" % i, [128, 512], F32) for i in range(2)])
                xc = S("xc", [128, 8, 512], BF16)
                b_xc = P.buf("xc")
                trs = Ring(P, [S("trs%d" % i, [128, 1024], BF16) for i in range(2)])
                ptp = ps[4][:, :].bitcast(BF16)
                d1 = S("d1", [128, 512], F32)
                d2 = S("d2", [128, 512], F32)
                d3 = S("d3", [128, 512], F32)
                dA = S("dA", [128, 512], F32)
                b_d = P.buf("dwork")
                ttm = S("ttm", [128, 2, 128], F32)
                b_ttm = P.buf("ttm")
                acsg = S("acsg", [128, 512], F32)
                b_acsg = P.buf("acsg")
                for (tok0, n, cond, isc) in groups:
                    seq0 = L if isc else 0
                    seqn = CTX if isc else L
                    lo = max(tok0 - 3, seq0)
                    hi = min(tok0 + n + 3, seq0 + seqn)
                    c0 = lo - (tok0 - 3)
                    c1 = hi - (tok0 - 3)
                    for j8 in range(6):
                        if c0 > 0 or c1 < n + 6:
                            P.op("pool", lambda: nc.gpsimd.memset(xin[:, :, 0:n + 6], 0.0), writes=[b_xin])
                        P.dma("sp", xin[:, :, c0:c1], xbc_d[:, j8 * 8:(j8 + 1) * 8, lo:hi], reads=[b_xbc], writes=[b_xin])
                        for jj in range(8):
                            j = j8 * 8 + jj
                            ac, acb = accr.next()
                            P.op("act", lambda ac=ac, jj=jj, j=j: nc.scalar.activation(
                                out=ac[:, 0:n], in_=xin[:, jj, 0:n], func=AF.Identity, bias=cbias[:, j:j + 1], scale=cw[:, j, 0:1]),
                                reads=[b_xin, b_cw], writes=[acb])
                            for k in range(1, 7):
                                P.op("dve", lambda ac=ac, jj=jj, j=j, k=k: nc.vector.scalar_tensor_tensor(
                                    out=ac[:, 0:n], in0=xin[:, jj, k:k + n], scalar=cw[:, j, k:k + 1], in1=ac[:, 0:n],
                                    op0=ALU.mult, op1=ALU.add), reads=[b_xin, b_cw, acb], writes=[acb], same_ok=(k > 1))
                            P.op("act", lambda ac=ac, jj=jj: nc.scalar.activation(out=xc[:, jj, 0:n], in_=ac[:, 0:n], func=AF.Silu),
                                 reads=[acb], writes=[b_xc])
                        if j8 == 4:
                            P.dma("sp", BT_d[:, :, tok0:tok0 + n], xc[:, :, 0:n], reads=[b_xc], writes=[b_BT])
                        if j8 == 5:
                            P.dma("sp", CT_d[:, :, tok0:tok0 + n], xc[:, :, 0:n], reads=[b_xc], writes=[b_CT])
                        if j8 <= 4:
                            for tb in range(n // 128):
                                P.mm_group([lambda jj=jj, tb=tb: nc.tensor.transpose(
                                    out=ptp[:, jj * 128:(jj + 1) * 128], in_=xc[:, jj, tb * 128:(tb + 1) * 128], identity=ident_b[:])
                                    for jj in range(8)], reads=[b_xc, b_const], writes=[pb[4]])
                                tr, trb = trs.next()
                                P.op("act", lambda tr=tr: nc.scalar.activation(out=tr[:], in_=ptp[:, :], func=AF.Identity),
                                     reads=[pb[4]], writes=[trb])
                                r0 = tok0 + tb * 128
                                if j8 < 4:
                                    P.dma("sp", xs_d[r0:r0 + 128, j8 * 1024:(j8 + 1) * 1024], tr[:], reads=[trb], writes=[b_xs])
                                else:
                                    P.dma("sp", Btm_d[r0:r0 + 128, :], tr[:], reads=[trb], writes=[b_Btm])
                    P.dma("sp", d1[:, 0:n], dtr_d[:, tok0:tok0 + n], reads=[b_dtr], writes=[b_d])
                    P.op("dve", lambda: nc.vector.tensor_scalar(out=d1[:, 0:n], in0=d1[:, 0:n], scalar1=dtb[:, 0:1], scalar2=None, op0=ALU.add),
                         reads=[b_d, b_cw], writes=[b_d])
                    P.op("act", lambda: nc.scalar.activation(out=d2[:, 0:n], in_=d1[:, 0:n], func=AF.Abs), reads=[b_d], writes=[b_d])
                    P.op("act", lambda: nc.scalar.activation(out=d2[:, 0:n], in_=d2[:, 0:n], func=AF.Exp, scale=-1.0), reads=[b_d], writes=[b_d])
                    P.op("act", lambda: nc.scalar.activation(out=d2[:, 0:n], in_=d2[:, 0:n], func=AF.Ln, bias=1.0, scale=1.0), reads=[b_d], writes=[b_d])
                    P.op("dve", lambda: nc.vector.tensor_scalar(out=d3[:, 0:n], in0=d1[:, 0:n], scalar1=0.0, scalar2=None, op0=ALU.max),
                         reads=[b_d], writes=[b_d])
                    P.op("dve", lambda: nc.vector.tensor_tensor(out=d3[:, 0:n], in0=d3[:, 0:n], in1=d2[:, 0:n], op=ALU.add), reads=[b_d], writes=[b_d])
                    P.op("dve", lambda: nc.vector.tensor_scalar(out=dA[:, 0:n], in0=d3[:, 0:n], scalar1=avec[:, 0:1], scalar2=None, op0=ALU.mult),
                         reads=[b_d, b_cw], writes=[b_d])
                    for tb in range(n // 128):
                        r0 = tok0 + tb * 128
                        P.mm_group([lambda tb=tb: nc.tensor.transpose(out=ps[1][:, 0:128], in_=d3[:, tb * 128:(tb + 1) * 128], identity=ident_f[:]),
                                    lambda tb=tb: nc.tensor.transpose(out=ps[1][:, 128:256], in_=dA[:, tb * 128:(tb + 1) * 128], identity=ident_f[:])],
                                   reads=[b_d, b_const], writes=[pb[1]])
                        P.op("dve", lambda: nc.vector.tensor_copy(out=ttm[:], in_=ps[1][:, 0:256].rearrange("p (a b) -> p a b", a=2)),
                             reads=[pb[1]], writes=[b_ttm])
                        P.dma("sp", dttm_d[r0:r0 + 128, :], ttm[:, 0, :], reads=[b_ttm], writes=[b_dttm])
                        P.dma("sp", dtAtm_d[r0:r0 + 128, :], ttm[:, 1, :], reads=[b_ttm], writes=[b_dtAtm])
                        P.mm_group([lambda: nc.tensor.matmul(ps[2][0:64, 0:128], lhsT=ttm[:, 1, 0:64], rhs=triF[:], start=True, stop=True),
                                    lambda: nc.tensor.matmul(ps[2][64:128, 0:128], lhsT=ttm[:, 1, 64:128], rhs=triB[:], start=True, stop=True)],
                                   reads=[b_ttm, b_const], writes=[pb[2]])
                        P.op("dve", lambda tb=tb: nc.vector.tensor_copy(out=acsg[:, tb * 128:(tb + 1) * 128], in_=ps[2][:, 0:128]),
                             reads=[pb[2]], writes=[b_acsg])
                    P.dma("sp", acsfm_d[:, tok0:tok0 + n], acsg[:, 0:n], reads=[b_acsg], writes=[b_acs])
                P.barrier()
            nlat = L // 128
            chunks_f = [(L, True), (L + 128, True)] + [(c * 128, False) for c in range(nlat)]
            chunks_b = [(L + 128, True), (L, True)] + [(c * 128, False) for c in range(nlat - 1, -1, -1)]
            with ExitStack() as es3:
                S = lambda name, shape, dt: es3.enter_context(nc.sbuf_tensor("sb_%s_%d" % (name, P.uid()), list(shape), dt))
                y_acc = hs_flat[:, 0:4096]
                zz = hs_flat[:, 4096:8192]
                xd = sq_flat[:, 0:4096]
                xdd = sq_flat[:, 4096:8192]
                b_y = b_hs
                b_zz = P.buf("zz")
                b_xd = b_sq
                xs_t = S("xs_t", [128, SI], BF16)
                b_xst = P.buf("xs_t")
                acs_rows = [S("acs_row%d" % i, [128, SH // 2, 128], F32) for i in range(2)]
                b_rows = [P.buf("acs_row%d" % i) for i in range(2)]
                Bfm = S("Bfm", [128, SG, 128], BF16)
                Cfm = S("Cfm", [128, SG, 128], BF16)
                Btm = S("Btm", [128, SG * SN], BF16)
                b_bc = P.buf("bc")
                dtt = S("dtt", [128, 2, 128], F32)
                b_dtt = P.buf("dtt")
                sm_ = S("ssmall", [128, 6, SH], F32)
                b_sm = P.buf("ssmall")
                M_all = S("M_all", [128, SH, 128], BF16)
                b_M = [P.buf() for _ in range(SG)]
                cb_all = S("cb_all", [128, SG, 128], F32)
                b_cb = P.buf("cb_all")
                b_Stg = [P.buf() for _ in range(SG)]
                b_Sbg = [P.buf() for _ in range(SG)]
                tr_ = Ring(P, [S("tt%d" % i, [128, 512], F32) for i in range(2)])
                St = S("St", [128, SG, 512], F32)
                Sb = S("Sb", [128, SG, 512], BF16)
                b_St = P.buf("St")
                b_Sb = P.buf("Sb")
                dvec = S("dvec", [128, SH], F32)
                gnb = S("gnb", [128, SI], F32)
                b_gn = P.buf("gn")
                P.dma("sp", dvec[:], Wl["s_dvec"][:, :], writes=[b_gn])
                P.dma("sp", gnb[:], Wl["s_gn"][:, :], writes=[b_gn])
                ubf = S("ubf", [128, SI], BF16)
                b_ubf = P.buf("ubf")
                uTt = S("uTt", [128, 32, 128], BF16)
                b_uTt = P.buf("uTt")
                ptp = ps[4][:, :].bitcast(BF16)
                for d, chunks in ((0, chunks_f), (1, chunks_b)):
                    mb = mbF if d == 0 else mbB
                    tri = triF if d == 0 else triB
                    P.op("pool", lambda: nc.gpsimd.memset(St[:], 0.0), writes=b_Stg)
                    P.op("pool", lambda: nc.gpsimd.memset(Sb[:], 0.0), writes=b_Sbg)
                    for (r0, isc) in chunks:
                        want_y = (not isc) or need_ctx
                        P.dma("sp", xs_t[:], xs_d[r0:r0 + 128, :], reads=[b_xs], writes=[b_xst])
                        P.dma("sp", dtt[:, 0, :], dttm_d[r0:r0 + 128, :], reads=[b_dttm], writes=[b_dtt])
                        P.dma("sp", dtt[:, 1, :], dtAtm_d[r0:r0 + 128, :], reads=[b_dtAtm], writes=[b_dtt])
                        P.dma("sp", Bfm[:], BT_d[:, :, r0:r0 + 128], reads=[b_BT], writes=[b_bc])
                        P.dma("sp", Cfm[:], CT_d[:, :, r0:r0 + 128], reads=[b_CT], writes=[b_bc])
                        P.dma("sp", Btm[:], Btm_d[r0:r0 + 128, :], reads=[b_Btm], writes=[b_bc])
                        if want_y:
                            for hf in range(2):
                                P.dma("sp", acs_rows[hf][:], acsfm_d[d * 64 + hf * 32:d * 64 + (hf + 1) * 32, r0:r0 + 128].partition_broadcast(128),
                                      reads=[b_acs], writes=[b_rows[hf]])
                        dtd = dtt[:, 0, d * 64:(d + 1) * 64]
                        dtAd = dtt[:, 1, d * 64:(d + 1) * 64]
                        P.mm_group([lambda: nc.tensor.matmul(ps[0][:, 0:64], lhsT=tri[:], rhs=dtAd, start=True, stop=True),
                                    lambda: nc.tensor.matmul(ps[0][:, 64:128], lhsT=ones_f[:], rhs=dtAd, start=True, stop=True)],
                                   reads=[b_dtt, b_const], writes=[pb[0]])
                        P.op("dve", lambda: nc.vector.tensor_copy(out=sm_[:, 0, :], in_=ps[0][:, 0:64]), reads=[pb[0]], writes=[b_sm])
                        P.op("dve", lambda: nc.vector.tensor_copy(out=sm_[:, 2, :], in_=ps[0][:, 64:128]), reads=[pb[0]], writes=[b_sm])
                        P.op("act", lambda: nc.scalar.activation(out=sm_[:, 1, :], in_=sm_[:, 0, :], func=AF.Exp), reads=[b_sm], writes=[b_sm])
                        P.op("act", lambda: nc.scalar.activation(out=sm_[:, 5, :], in_=sm_[:, 2, :], func=AF.Exp), reads=[b_sm], writes=[b_sm])
                        P.op("dve", lambda: nc.vector.tensor_tensor(out=sm_[:, 3, :], in0=sm_[:, 2, :], in1=sm_[:, 0, :], op=ALU.subtract),
                             reads=[b_sm], writes=[b_sm])
                        P.op("act", lambda: nc.scalar.activation(out=sm_[:, 3, :], in_=sm_[:, 3, :], func=AF.Exp), reads=[b_sm], writes=[b_sm])
                        P.op("dve", lambda: nc.vector.tensor_tensor(out=sm_[:, 4, :], in0=sm_[:, 3, :], in1=dtd, op=ALU.mult),
                             reads=[b_sm, b_dtt], writes=[b_sm])
                        xs3 = xs_t[:].rearrange("p (h q) -> p h q", q=SP_)
                        P.op("dve", lambda: nc.vector.tensor_tensor(out=xd.rearrange("p (h q) -> p h q", q=SP_), in0=xs3,
                                                                    in1=dtd.unsqueeze(2).to_broadcast([128, SH, SP_]), op=ALU.mult),
                             reads=[b_xst, b_dtt], writes=[b_xd])
                        P.op("dve", lambda: nc.vector.tensor_tensor(out=xdd.rearrange("p (h q) -> p h q", q=SP_), in0=xs3,
                                                                    in1=sm_[:, 4, :].unsqueeze(2).to_broadcast([128, SH, SP_]), op=ALU.mult),
                             reads=[b_xst, b_sm], writes=[b_xd])
                        if want_y:
                            P.mm_group([lambda g=g: nc.tensor.matmul(ps[1][:, g * 128:(g + 1) * 128], lhsT=Bfm[:, g, :], rhs=Cfm[:, g, :], start=True, stop=True)
                                        for g in range(4)], reads=[b_bc], writes=[pb[1]])
                            P.mm_group([lambda g=g: nc.tensor.matmul(ps[2][:, (g - 4) * 128:(g - 3) * 128], lhsT=Bfm[:, g, :], rhs=Cfm[:, g, :], start=True, stop=True)
                                        for g in range(4, 8)], reads=[b_bc], writes=[pb[2]])
                            P.op("act", lambda: nc.scalar.activation(out=cb_all[:, 0:4, :], in_=ps[1][:, :].rearrange("p (g l) -> p g l", g=4), func=AF.Identity),
                                 reads=[pb[1]], writes=[b_cb])
                            P.op("act", lambda: nc.scalar.activation(out=cb_all[:, 4:8, :], in_=ps[2][:, :].rearrange("p (g l) -> p g l", g=4), func=AF.Identity),
                                 reads=[pb[2]], writes=[b_cb])
                            for hf in range(2):
                                ar, arb = acs_rows[hf], b_rows[hf]
                                P.op("dve", lambda ar=ar, hf=hf: nc.vector.tensor_tensor(
                                    out=ar[:], in0=ar[:], in1=sm_[:, 0, hf * 32:(hf + 1) * 32].unsqueeze(2).to_broadcast([128, 32, 128]), op=ALU.subtract),
                                    reads=[arb, b_sm], writes=[arb])
                                P.op("dve", lambda ar=ar: nc.vector.tensor_tensor(
                                    out=ar[:], in0=ar[:], in1=mb[:].unsqueeze(1).to_broadcast([128, 32, 128]), op=ALU.add),
                                    reads=[arb, b_const], writes=[arb], same_ok=True)
                                P.op("act", lambda ar=ar: nc.scalar.activation(out=ar[:], in_=ar[:], func=AF.Exp), reads=[arb], writes=[arb])
                                for g in range(hf * 4, hf * 4 + 4):
                                    P.op("pool", lambda g=g, ar=ar: nc.gpsimd.tensor_tensor(
                                        out=M_all[:, g * 8:(g + 1) * 8, :], in0=ar[:, (g % 4) * 8:(g % 4 + 1) * 8, :],
                                        in1=cb_all[:, g, :].unsqueeze(1).to_broadcast([128, 8, 128]), op=ALU.mult),
                                        reads=[arb, b_cb], writes=[b_M[g]])
                        for g in range(SG):
                            if want_y:
                                byd = 3 if g % 2 == 0 else 5
                                bz = 6 if g % 2 == 0 else 7
                                P.mm_group([lambda hh=hh, g=g, byd=byd: nc.tensor.matmul(
                                    ps[byd][:, hh * 64:(hh + 1) * 64], lhsT=M_all[:, g * 8 + hh, :], rhs=xd[:, (g * 8 + hh) * 64:(g * 8 + hh + 1) * 64],
                                    start=True, stop=True) for hh in range(8)], reads=[b_M[g], b_xd], writes=[pb[byd]])
                                P.mm_group([lambda g=g, bz=bz: nc.tensor.matmul(ps[bz][:, :], lhsT=Cfm[:, g, :], rhs=Sb[:, g, :], start=True, stop=True)],
                                           reads=[b_bc, b_Sbg[g]], writes=[pb[bz]])
                                tt, ttb = tr_.next()
                                P.op("dve", lambda tt=tt, g=g, bz=bz: nc.vector.tensor_tensor(
                                    out=tt[:].rearrange("p (h q) -> p h q", q=SP_), in0=ps[bz][:, :].rearrange("p (h q) -> p h q", q=SP_),
                                    in1=sm_[:, 1, g * 8:(g + 1) * 8].unsqueeze(2).to_broadcast([128, 8, SP_]), op=ALU.mult),
                                    reads=[pb[bz], b_sm], writes=[ttb])
                                P.op("dve", lambda tt=tt, g=g, byd=byd: nc.vector.tensor_tensor(
                                    out=y_acc[:, g * 512:(g + 1) * 512], in0=tt[:], in1=ps[byd][:, :], op=ALU.add),
                                    reads=[ttb, pb[byd]], writes=[b_y])
                            P.mm_group([lambda g=g: nc.tensor.matmul(ps[4][:, :], lhsT=Btm[:, g * 128:(g + 1) * 128], rhs=xdd[:, g * 512:(g + 1) * 512],
                                                                     start=True, stop=True)], reads=[b_bc, b_xd], writes=[pb[4]])
                            P.op("pool", lambda g=g: nc.gpsimd.tensor_tensor(
                                out=St[:, g, :].rearrange("p (h q) -> p h q", q=SP_), in0=St[:, g, :].rearrange("p (h q) -> p h q", q=SP_),
                                in1=sm_[:, 5, g * 8:(g + 1) * 8].unsqueeze(2).to_broadcast([128, 8, SP_]), op=ALU.mult),
                                reads=[b_Stg[g], b_sm], writes=[b_Stg[g]])
                            P.op("dve", lambda g=g: nc.vector.tensor_tensor(out=St[:, g, :], in0=St[:, g, :], in1=ps[4][:, :], op=ALU.add),
                                 reads=[b_Stg[g], pb[4]], writes=[b_Stg[g]])
                            P.op("act", lambda g=g: nc.scalar.activation(out=Sb[:, g, :], in_=St[:, g, :], func=AF.Identity),
                                 reads=[b_Stg[g]], writes=[b_Sbg[g]])
                        if not want_y:
                            continue
                        if d == 0:
                            P.dma("sp", yf_d[r0:r0 + 128, :], y_acc, reads=[b_y], writes=[b_yf])
                            continue
                        P.dma("sp", zz, yf_d[r0:r0 + 128, :], reads=[b_yf], writes=[b_zz])
                        P.op("dve", lambda: nc.vector.tensor_tensor(out=y_acc, in0=y_acc, in1=zz, op=ALU.add), reads=[b_y, b_zz], writes=[b_y])
                        P.op("dve", lambda: nc.vector.tensor_tensor(out=zz.rearrange("p (h q) -> p h q", q=SP_), in0=xs3,
                                                                    in1=dvec[:].unsqueeze(2).to_broadcast([128, SH, SP_]), op=ALU.mult),
                             reads=[b_xst, b_gn, b_zz], writes=[b_zz])
                        P.op("dve", lambda: nc.vector.tensor_tensor(out=y_acc, in0=y_acc, in1=zz, op=ALU.add), reads=[b_y, b_zz], writes=[b_y])
                        P.dma("sp", zz, z_d[r0:r0 + 128, :], reads=[b_z, b_zz], writes=[b_zz])
                        P.op("act", lambda: nc.scalar.activation(out=zz, in_=zz, func=AF.Silu), reads=[b_zz], writes=[b_zz])
                        P.op("dve", lambda: nc.vector.tensor_tensor(out=y_acc, in0=y_acc, in1=zz, op=ALU.mult), reads=[b_y, b_zz], writes=[b_y])
                        for g in range(SG):
                            P.op("act", lambda g=g: nc.scalar.activation(out=zz[:, g * 512:(g + 1) * 512], in_=y_acc[:, g * 512:(g + 1) * 512],
                                                                         func=AF.Square, accum_out=sm_[:, 3, g:g + 1]),
                                 reads=[b_y, b_zz], writes=[b_zz, b_sm])
                        P.op("act", lambda: nc.scalar.activation(out=sm_[:, 3, 8:16], in_=sm_[:, 3, 0:8], func=AF.Sqrt, bias=EPS, scale=1.0 / 512.0),
                             reads=[b_sm], writes=[b_sm])
                        P.op("dve", lambda: nc.vector.reciprocal(out=sm_[:, 3, 16:24], in_=sm_[:, 3, 8:16]), reads=[b_sm], writes=[b_sm])
                        P.op("dve", lambda: nc.vector.tensor_tensor(
                            out=y_acc.rearrange("p (g q) -> p g q", q=512), in0=y_acc.rearrange("p (g q) -> p g q", q=512),
                            in1=sm_[:, 3, 16:24].unsqueeze(2).to_broadcast([128, 8, 512]), op=ALU.mult), reads=[b_y, b_sm], writes=[b_y])
                        P.op("dve", lambda: nc.vector.tensor_tensor(out=ubf[:], in0=y_acc, in1=gnb[:], op=ALU.mult), reads=[b_y, b_gn], writes=[b_ubf])
                        for q4 in range(4):
                            P.mm_group([lambda q4=q4, jj=jj: nc.tensor.transpose(
                                out=ptp[:, jj * 128:(jj + 1) * 128], in_=ubf[:, (q4 * 8 + jj) * 128:(q4 * 8 + jj + 1) * 128], identity=ident_b[:])
                                for jj in range(8)], reads=[b_ubf, b_const], writes=[pb[4]])
                            P.op("act", lambda q4=q4: nc.scalar.activation(out=uTt[:, q4 * 8:(q4 + 1) * 8, :],
                                                                            in_=ptp[:, :].rearrange("p (a b) -> p a b", a=8), func=AF.Identity),
                                 reads=[pb[4]], writes=[b_uTt])
                        P.dma("sp", uT_d[:, :, r0:r0 + 128], uTt[:], reads=[b_uTt], writes=[b_uT])
                P.barrier()
            with ExitStack() as es3:
                S = lambda name, shape, dt: es3.enter_context(nc.sbuf_tensor("sb_%s_%d" % (name, P.uid()), list(shape), dt))
                ug = S("ug", [128, 32, 512], BF16)
                b_ug = P.buf("ug")
                wor = Ring(P, [S("wso%d" % i, [128, 32, 128], BF16) for i in range(3)])
                for (tok0, n, cond, isc) in groups:
                    if isc and not need_ctx:
                        continue
                    P.dma("sp", hs[:, :, 0:n], hT_v[:, :, tok0:tok0 + n], reads=[hT_buf(tok0)], writes=[b_hs])
                    P.dma("sp", ug[:, :, 0:n], uT_d[:, :, tok0:tok0 + n], reads=[b_uT], writes=[b_ug])
                    for dc in range(NDC):
                        wt, wb = wor.next()
                        P.dma("pool", wt[:], w_out[:, :, dc * 128:(dc + 1) * 128], writes=[wb])
                        by = 5 + (dc % 2)
                        P.mm_group([lambda fc=fc, wt=wt, by=by: nc.tensor.matmul(
                            ps[by][:, 0:n], lhsT=wt[:, fc, :], rhs=ug[:, fc, 0:n], start=(fc == 0), stop=(fc == 31))
                            for fc in range(32)], reads=[wb, b_ug], writes=[pb[by]])
                        P.op("dve", lambda dc=dc, by=by: nc.vector.scalar_tensor_tensor(
                            out=hs[:, dc, 0:n], in0=ps[by][:, 0:n], scalar=gate[:, li, 1, dc, cond:cond + 1],
                            in1=hs[:, dc, 0:n], op0=ALU.mult, op1=ALU.add), reads=[pb[by], b_hs, b_mod], writes=[b_hs])
                    P.dma("sp", hT_v[:, :, tok0:tok0 + n], hs[:, :, 0:n], reads=[b_hs], writes=[hT_buf(tok0)])
                P.barrier()

        def attn_phase(li, groups, need_ctx):
            wq = W[li]["w_qkv"].rearrange("(dc p) e -> p dc e", p=128)
            wo = W[li]["w_o"].rearrange("(hc p) e -> p hc e", p=128)
            b_q = P.buf("qT_d")
            b_k = P.buf("kT_d")
            b_v = P.buf("v_d")
            import os
            if os.environ.get("ATT_STOP") == "0":
                return
            with ExitStack() as es3:
                S = lambda name, shape, dt: es3.enter_context(nc.sbuf_tensor("sb_%s_%d" % (name, P.uid()), list(shape), dt))
                aT = S("aT", [128, NDC, 512], BF16)
                b_aT = P.buf("aT")
                wv = S("wv", [128, NDC, 512], BF16)
                b_wv = P.buf("wv")
                P.dma("pool", wv[:], wq[:, :, 2560:3072], writes=[b_wv])
                wr = Ring(P, [S("wqk%d" % i, [128, NDC, 128], BF16) for i in range(4)])
                cs = S("cs", [128, 512], F32)
                sn = S("sn", [128, 512], F32)
                b_cs = P.buf("cs")
                xbr = Ring(P, [S("xb%d" % i, [128, 512], BF16) for i in range(2)])
                t1r = Ring(P, [S("t1%d" % i, [128, 512], F32) for i in range(2)])
                qo = S("qo", [128, NH + NKV, 512], BF16)
                b_qo = P.buf("qo")
                vo = Ring(P, [S("vo%d" % i, [128, 512], BF16) for i in range(2)])
                for (tok0, n, cond, isc) in groups:
                    P.dma("sp", hs[:, :, 0:n], hT_v[:, :, tok0:tok0 + n], reads=[hT_buf(tok0)], writes=[b_hs])
                    rms_stats(n)
                    modulate(aT, b_aT, n, li, 1, cond)
                    if not isc:
                        P.dma("sp", cs[:, 0:n], cosT_d[:, tok0:tok0 + n], writes=[b_cs])
                        P.dma("sp", sn[:, 0:n], sinT_d[:, tok0:tok0 + n], writes=[b_cs])
                    for hc in range(0 if os.environ.get("ATT_NOQK") else NH + NKV):
                        wt, wb = wr.next()
                        P.dma("pool", wt[:], wq[:, :, hc * 128:(hc + 1) * 128], writes=[wb])
                        bk = 1 + (hc % 2)
                        P.mm_group([lambda dc=dc, wt=wt, bk=bk: nc.tensor.matmul(
                            ps[bk][:, 0:n], lhsT=wt[:, dc, :], rhs=aT[:, dc, 0:n], start=(dc == 0), stop=(dc == NDC - 1))
                            for dc in range(NDC)], reads=[wb, b_aT], writes=[pb[bk]])
                        if isc or os.environ.get("ATT_NOROPE"):
                            P.op("act", lambda hc=hc, bk=bk: nc.scalar.activation(out=qo[:, hc, 0:n], in_=ps[bk][:, 0:n], func=AF.Identity),
                                 reads=[pb[bk]], writes=[b_qo])
                        else:
                            xb, xbb = xbr.next()
                            P.op("act", lambda xb=xb, bk=bk: nc.scalar.activation(out=xb[:, 0:n], in_=ps[bk][:, 0:n], func=AF.Identity),
                                 reads=[pb[bk]], writes=[xbb])
                            RM = os.environ.get("ATT_RM", "")
                            if "a" not in RM:
                                lw = ident_b if "i" in RM else rot_b
                                P.mm_group([lambda xb=xb, lw=lw: nc.tensor.matmul(ps[3][:, 0:n], lhsT=lw[:], rhs=xb[:, 0:n], start=True, stop=True)],
                                           reads=[xbb, b_const, b_rot], writes=[pb[3]])
                            t1, t1b = t1r.next()
                            if "b" not in RM:
                                P.op("dve", lambda t1=t1, bk=bk: nc.vector.tensor_tensor(out=t1[:, 0:n], in0=ps[bk][:, 0:n], in1=cs[:, 0:n], op=ALU.mult),
                                     reads=[pb[bk], b_cs, xbb], writes=[t1b])
                            t2, t2b = tmpr.next()
                            if "c" not in RM:
                                P.op("dve", lambda t2=t2: nc.vector.tensor_tensor(out=t2[:, 0:n], in0=ps[3][:, 0:n], in1=sn[:, 0:n], op=ALU.mult),
                                     reads=[pb[3], b_cs], writes=[t2b])
                            if "d" not in RM:
                                P.op("dve", lambda t1=t1, t2=t2, hc=hc: nc.vector.tensor_tensor(out=qo[:, hc, 0:n], in0=t1[:, 0:n], in1=t2[:, 0:n], op=ALU.add),
                                     reads=[t1b, t2b], writes=[b_qo])
                    if os.environ.get("ATT_STOP") != "2":
                        P.dma("sp", qT_d[:, :, tok0:tok0 + n], qo[:, 0:NH, 0:n], reads=[b_qo], writes=[b_q])
                        P.dma("sp", kT_d[:, :, tok0:tok0 + n], qo[:, NH:NH + NKV, 0:n], reads=[b_qo], writes=[b_k])
                    for tb in range(0 if os.environ.get("ATT_NOV") else n // 128):
                        bk = 5 + (tb % 2)
                        P.mm_group([lambda dc=dc, tb=tb, bk=bk: nc.tensor.matmul(
                            ps[bk][:, :], lhsT=aT[:, dc, tb * 128:(tb + 1) * 128], rhs=wv[:, dc, :], start=(dc == 0), stop=(dc == NDC - 1))
                            for dc in range(NDC)], reads=[b_wv, b_aT], writes=[pb[bk]])
                        vt, vtb = vo.next()
                        P.op("act", lambda vt=vt, bk=bk: nc.scalar.activation(out=vt[:], in_=ps[bk][:, :], func=AF.Identity), reads=[pb[bk]], writes=[vtb])
                        if os.environ.get("ATT_STOP") != "2":
                            P.dma("sp", v_d[tok0 + tb * 128:tok0 + (tb + 1) * 128, :], vt[:], reads=[vtb], writes=[b_v])
                P.barrier()
            import os
            if os.environ.get("ATT_STOP") in ("1", "2"):
                return
            with ExitStack() as es3:
                S = lambda name, shape, dt: es3.enter_context(nc.sbuf_tensor("sb_%s_%d" % (name, P.uid()), list(shape), dt))
                qg = S("qg", [128, NH, 512], BF16)
                b_qg = P.buf("qg")
                kband = S("kband", [128, NKV, 768], BF16)
                vband = S("vband", [128, 6, 512], BF16)
                b_band = P.buf("band")
                kctx = S("kctx", [128, NKV, CTX], BF16)
                vctx = S("vctx", [128, 2, 512], BF16)
                b_kc = P.buf("kctx")
                sinkb = S("sinkb", [128, NH], F32)
                P.dma("sp", sinkb[:], W[li]["sinkb"][:, :], writes=[b_kc])
                P.dma("sp", kctx[:], kT_d[:, :, L:L + CTX], reads=[b_k], writes=[b_kc])
                P.dma("sp", vctx[:], v_d[L:L + CTX, :].rearrange("(b p) e -> p b e", p=128), reads=[b_v], writes=[b_kc])
                oT = S("oT", [128, NH, 512], BF16)
                b_oT = P.buf("oT")
                smr = Ring(P, [S("sm%d" % i, [128, 640], F32) for i in range(2)])
                pfr = Ring(P, [S("pf%d" % i, [128, 640], F32) for i in range(2)])
                pnr = Ring(P, [S("pn%d" % i, [128, 640], BF16) for i in range(2)])
                ptr_ = Ring(P, [S("pt%d" % i, [128, 640], BF16) for i in range(2)])
                smallr = Ring(P, [S("sml%d" % i, [128, 8], F32) for i in range(4)])
                wor = Ring(P, [S("wo%d" % i, [128, NH, 128], BF16) for i in range(3)])
                ptp = ps[4][:, :].bitcast(BF16)
                for (tok0, n, cond, isc) in groups:
                    if isc and not need_ctx:
                        continue
                    nqb = n // 128
                    P.dma("sp", hs[:, :, 0:n], hT_v[:, :, tok0:tok0 + n], reads=[hT_buf(tok0)], writes=[b_hs])
                    P.dma("sp", qg[:, :, 0:n], qT_d[:, :, tok0:tok0 + n], reads=[b_q], writes=[b_qg])
                    if not isc:
                        lo = max(tok0 - 128, 0)
                        hi = min(tok0 + n + 128, L)
                        c0 = lo - (tok0 - 128)
                        c1 = hi - (tok0 - 128)
                        if c0 > 0 or c1 < n + 256:
                            P.op("pool", lambda: nc.gpsimd.memset(kband[:], 0.0), writes=[b_band])
                            P.op("pool", lambda: nc.gpsimd.memset(vband[:], 0.0), writes=[b_band])
                        P.dma("sp", kband[:, :, c0:c1], kT_d[:, :, lo:hi], reads=[b_k], writes=[b_band])
                        P.dma("sp", vband[:, c0 // 128:c1 // 128, :], v_d[lo:hi, :].rearrange("(b p) e -> p b e", p=128),
                              reads=[b_v], writes=[b_band])
                    for h in range(NH):
                        kv = h // 4
                        bo = 5 + (h % 2)
                        for qb in range(nqb):
                            qblk = qg[:, h, qb * 128:(qb + 1) * 128]
                            ba = 1 + ((h * nqb + qb) % 2)
                            nk = 256 if isc else 640
                            mms = [lambda qblk=qblk, ba=ba, kv=kv: nc.tensor.matmul(ps[ba][:, 0:256], lhsT=qblk, rhs=kctx[:, kv, :], start=True, stop=True)]
                            if not isc:
                                mms.append(lambda qblk=qblk, ba=ba, kv=kv, qb=qb: nc.tensor.matmul(
                                    ps[ba][:, 256:512], lhsT=qblk, rhs=kband[:, kv, qb * 128:(qb + 2) * 128], start=True, stop=True))
                            P.mm_group(mms, reads=[b_qg, b_kc, b_band], writes=[pb[ba]])
                            if not isc:
                                P.mm_group([lambda qblk=qblk, kv=kv, qb=qb: nc.tensor.matmul(
                                    ps[3][:, 0:128], lhsT=qblk, rhs=kband[:, kv, (qb + 2) * 128:(qb + 3) * 128], start=True, stop=True)],
                                    reads=[b_qg, b_band], writes=[pb[3]])
                            sm, smb = smr.next()
                            if isc:
                                P.op("dve", lambda sm=sm, ba=ba: nc.vector.tensor_scalar(
                                    out=sm[:, 0:256], in0=ps[ba][:, 0:256], scalar1=ATT_SCALE, scalar2=None, op0=ALU.mult),
                                    reads=[pb[ba]], writes=[smb])
                            else:
                                first = (tok0 + qb * 128 == 0)
                                lastb = (tok0 + (qb + 1) * 128 == L)
                                P.op("dve", lambda sm=sm, ba=ba, first=first: nc.vector.scalar_tensor_tensor(
                                    out=sm[:, 0:512], in0=ps[ba][:, :], scalar=ATT_SCALE, in1=maskA[:, 1 if first else 0, :],
                                    op0=ALU.mult, op1=ALU.add), reads=[pb[ba], b_const], writes=[smb])
                                P.op("dve", lambda sm=sm, lastb=lastb: nc.vector.scalar_tensor_tensor(
                                    out=sm[:, 512:640], in0=ps[3][:, 0:128], scalar=ATT_SCALE, in1=maskB[:, 1 if lastb else 0, :],
                                    op0=ALU.mult, op1=ALU.add), reads=[pb[3], b_const], writes=[smb])
                            sl, slb = smallr.next()
                            P.op("dve", lambda sl=sl, sm=sm, nk=nk: nc.vector.reduce_max(out=sl[:, 0:1], in_=sm[:, 0:nk], axis=AX.X),
                                 reads=[smb], writes=[slb])
                            P.op("dve", lambda sl=sl, h=h: nc.vector.tensor_tensor(out=sl[:, 1:2], in0=sl[:, 0:1], in1=sinkb[:, h:h + 1], op=ALU.max),
                                 reads=[slb, b_kc], writes=[slb])
                            P.op("dve", lambda sl=sl: nc.vector.tensor_scalar(out=sl[:, 2:3], in0=sl[:, 1:2], scalar1=-1.0, scalar2=None, op0=ALU.mult),
                                 reads=[slb], writes=[slb])
                            pf, pfb = pfr.next()
                            P.op("act", lambda pf=pf, sm=sm, sl=sl, nk=nk: nc.scalar.activation(
                                out=pf[:, 0:nk], in_=sm[:, 0:nk], func=AF.Exp, bias=sl[:, 2:3], scale=1.0, accum_out=sl[:, 3:4]),
                                reads=[smb, slb], writes=[pfb, slb])
                            P.op("act", lambda sl=sl, h=h: nc.scalar.activation(
                                out=sl[:, 4:5], in_=sinkb[:, h:h + 1], func=AF.Exp, bias=sl[:, 2:3], scale=1.0),
                                reads=[slb, b_kc], writes=[slb])
                            P.op("dve", lambda sl=sl: nc.vector.tensor_tensor(out=sl[:, 5:6], in0=sl[:, 3:4], in1=sl[:, 4:5], op=ALU.add),
                                 reads=[slb], writes=[slb])
                            P.op("dve", lambda sl=sl: nc.vector.reciprocal(out=sl[:, 6:7], in_=sl[:, 5:6]), reads=[slb], writes=[slb])
                            pn, pnb = pnr.next()
                            P.op("dve", lambda pn=pn, pf=pf, sl=sl, nk=nk: nc.vector.tensor_scalar(
                                out=pn[:, 0:nk], in0=pf[:, 0:nk], scalar1=sl[:, 6:7], scalar2=None, op0=ALU.mult),
                                reads=[pfb, slb], writes=[pnb])
                            nj = nk // 128
                            P.mm_group([lambda j=j, pn=pn: nc.tensor.transpose(
                                out=ptp[:, j * 128:(j + 1) * 128], in_=pn[:, j * 128:(j + 1) * 128], identity=ident_b[:]) for j in range(nj)],
                                reads=[pnb, b_const], writes=[pb[4]])
                            pt, ptb = ptr_.next()
                            P.op("act", lambda pt=pt, nk=nk: nc.scalar.activation(out=pt[:, 0:nk], in_=ptp[:, 0:nk], func=AF.Identity), reads=[pb[4]], writes=[ptb])
                            mms = []
                            for j in range(nj):
                                if j < 2:
                                    vblk = vctx[:, j, kv * 128:(kv + 1) * 128]
                                else:
                                    vblk = vband[:, qb + j - 2, kv * 128:(kv + 1) * 128]
                                mms.append(lambda j=j, vblk=vblk, pt=pt, bo=bo, qb=qb, nj=nj: nc.tensor.matmul(
                                    ps[bo][:, qb * 128:(qb + 1) * 128], lhsT=vblk, rhs=pt[:, j * 128:(j + 1) * 128],
                                    start=(j == 0), stop=(j == nj - 1)))
                            P.mm_group(mms, reads=[ptb, b_kc, b_band], writes=[pb[bo]])
                        P.op("act", lambda h=h, bo=bo: nc.scalar.activation(out=oT[:, h, 0:n], in_=ps[bo][:, 0:n], func=AF.Identity), reads=[pb[bo]], writes=[b_oT])
                    for dc in range(NDC):
                        wt, wb = wor.next()
                        P.dma("pool", wt[:], wo[:, :, dc * 128:(dc + 1) * 128], writes=[wb])
                        by = 0 if dc % 2 == 0 else 7
                        P.mm_group([lambda hc=hc, wt=wt, by=by: nc.tensor.matmul(
                            ps[by][:, 0:n], lhsT=wt[:, hc, :], rhs=oT[:, hc, 0:n], start=(hc == 0), stop=(hc == NH - 1))
                            for hc in range(NH)], reads=[wb, b_oT], writes=[pb[by]])
                        P.op("dve", lambda dc=dc, by=by: nc.vector.scalar_tensor_tensor(
                            out=hs[:, dc, 0:n], in0=ps[by][:, 0:n], scalar=gate[:, li, 1, dc, cond:cond + 1],
                            in1=hs[:, dc, 0:n], op0=ALU.mult, op1=ALU.add), reads=[pb[by], b_hs, b_mod], writes=[b_hs])
                    P.dma("sp", hT_v[:, :, tok0:tok0 + n], hs[:, :, 0:n], reads=[b_hs], writes=[hT_buf(tok0)])
                P.barrier()

        def final_phase():
            with ExitStack() as es3:
                S = lambda name, shape, dt: es3.enter_context(nc.sbuf_tensor("sb_%s_%d" % (name, P.uid()), list(shape), dt))
                nrm = S("nrm", [128, NDC, 512], F32)
                b_nrm = P.buf("nrm")
                otr = Ring(P, [S("ot%d" % i, [128, D], F32) for i in range(2)])
                for (tok0, n, cond, isc) in lat_groups:
                    P.dma("sp", hs[:, :, 0:n], hT_v[:, :, tok0:tok0 + n], reads=[hT_buf(tok0)], writes=[b_hs])
                    rms_stats(n)
                    for dc in range(NDC):
                        P.op("dve", lambda dc=dc: nc.vector.scalar_tensor_tensor(
                            out=nrm[:, dc, 0:n], in0=hs[:, dc, 0:n], scalar=fing[:, dc:dc + 1], in1=rstd[:, 0:n],
                            op0=ALU.mult, op1=ALU.mult), reads=[b_hs, b_rstd, b_small], writes=[b_nrm])
                    for tb in range(n // 128):
                        ot, otb = otr.next()
                        for q in range(4):
                            bank = 1 + (q % 2)
                            P.mm_group([lambda dc=dc, bank=bank, tb=tb: nc.tensor.transpose(
                                out=ps[bank][:, (dc % 4) * 128:(dc % 4 + 1) * 128], in_=nrm[:, dc, tb * 128:(tb + 1) * 128],
                                identity=ident_f[:]) for dc in range(q * 4, q * 4 + 4)], reads=[b_nrm, b_const], writes=[pb[bank]])
                            if q % 2 == 0:
                                P.op("dve", lambda q=q, bank=bank, ot=ot: nc.vector.tensor_copy(
                                    out=ot[:, q * 512:(q + 1) * 512], in_=ps[bank][:, :]), reads=[pb[bank]], writes=[otb])
                            else:
                                P.op("act", lambda q=q, bank=bank, ot=ot: nc.scalar.copy(
                                    out=ot[:, q * 512:(q + 1) * 512], in_=ps[bank][:, :]), reads=[pb[bank]], writes=[otb])
                        P.dma("sp", out_d[tok0 + tb * 128:tok0 + (tb + 1) * 128, :], ot[:], reads=[otb], writes=[b_out])
                P.barrier()

        for li, lay in enumerate(layers):
            if lay.get("skip_ffn"):
                continue
            if not lay.get("ffn2only"):
                ffn_order.append((li, 0))
            if not lay.get("ffn1only"):
                ffn_order.append((li, 1))
        for li, lay in enumerate(layers):
            kind, last = lay["kind"], lay["last"]
            ctx_live = (not last) or kind != 1
            if lay.get("skip_ffn"):
                continue
            if not lay.get("ffn2only"):
                ffn_phase(li, 0, lat_groups + ([ctx_group] if ctx_live else []))
            if lay.get("ffn1only"):
                continue
            if lay.get("skip_mixer"):
                pass
            elif kind == 1:
                pool_phase(li, lat_groups + ([] if last else [ctx_group]))
            elif kind == 2:
                attn_phase(li, lat_groups + [ctx_group], not last)
            elif kind == 0:
                ssm_phase(li, lat_groups + [ctx_group], not last)
            ffn_phase(li, 1, lat_groups + ([] if last else [ctx_group]))
        final_phase()
        if debug:
            dbg_mod = nc.dram_tensor("dbg_mod", [128, NL * NMOD * NDC * 2], F32, kind="ExternalOutput").ap()
            P.dma("sp", dbg_mod[:, :], modsb[:].rearrange("p l e c -> p (l e c)"), reads=[b_mod], writes=[b_out])
            dbg_sc1 = nc.dram_tensor("dbg_sc1", [128, NL * 3 * NDC * 2], F32, kind="ExternalOutput").ap()
            P.dma("sp", dbg_sc1[:, :], sc1[:].rearrange("p l k e c -> p (l k e c)"), reads=[b_mod], writes=[b_out])
            dbg_gate = nc.dram_tensor("dbg_gate", [128, NL * 3 * NDC * 2], F32, kind="ExternalOutput").ap()
            P.dma("sp", dbg_gate[:, :], gate[:].rearrange("p l k e c -> p (l k e c)"), reads=[b_mod], writes=[b_out])
            dbg_rstd = nc.dram_tensor("dbg_rstd", [128, 512], F32, kind="ExternalOutput").ap()
            P.dma("sp", dbg_rstd[:, :], rstd[:], reads=[b_rstd], writes=[b_out])
        fin = [b_out] + list(hreg.values()) if debug else [b_out]
        P.finish(fin)
    return nc


def _pl(v):
    v = np.asarray(v, np.float32)
    lead = v.shape[:-1]
    n = v.shape[-1] // 128
    v = v.reshape(lead + (n, 128))
    v = np.moveaxis(v, -1, 0)
    return np.ascontiguousarray(v)


def _rope_consts(L):
    t = np.arange(L)
    row = (t // GRID_W).astype(np.float32)
    col = (t % GRID_W).astype(np.float32)
    inv_freq = (10000.0 ** (-np.arange(0, 64, 2, dtype=np.float32) / 64.0)).astype(np.float32)
    ang = np.concatenate([row[:, None] * inv_freq, col[:, None] * inv_freq], axis=-1).astype(np.float32)
    cosT = np.repeat(np.cos(ang).T, 2, axis=0).astype(np.float32)
    sinT = np.repeat(np.sin(ang).T, 2, axis=0).astype(np.float32)
    rot = np.zeros((128, 128), np.float32)
    for i in range(64):
        rot[2 * i + 1, 2 * i] = -1.0
        rot[2 * i, 2 * i + 1] = 1.0
    return {"cosT": np.ascontiguousarray(cosT), "sinT": np.ascontiguousarray(sinT), "rotT": rot}


def make_in_maps(inputs, L, layer_ids, batch_ids):
    depth = inputs["ada_w"].shape[0]
    shared = {}
    shared["final_g"] = _pl(inputs["final_g"])
    for li, i in enumerate(layer_ids):
        kind, j = i % 3, i // 3
        shared["ada_w%d" % li] = inputs["ada_w"][i]
        shared["ada_b%d" % li] = _pl(inputs["ada_b"][i])
        shared["norm_g%d" % li] = _pl(inputs["norm_g"][i])
        for k in range(2):
            shared["ffn_w_in%d_%d" % (li, k)] = inputs["ffn_w_in"][i, k]
            shared["ffn_w_out%d_%d" % (li, k)] = inputs["ffn_w_out"][i, k]
        if kind == 0:
            shared["ssm_w_in%d" % li] = inputs["ssm_w_in"][j]
            cwv = np.asarray(inputs["ssm_conv_w"][j], np.float32)
            shared["ssm_cw%d" % li] = np.ascontiguousarray(cwv.reshape(7, 48, 128).transpose(2, 1, 0))
            shared["ssm_cb%d" % li] = _pl(inputs["ssm_conv_b"][j])
            shared["ssm_dtb%d" % li] = np.ascontiguousarray(np.asarray(inputs["ssm_dt_bias"][j], np.float32).reshape(128, 1))
            shared["ssm_alog%d" % li] = np.ascontiguousarray(np.asarray(inputs["ssm_a_log"][j], np.float32).reshape(128, 1))
            shared["ssm_dvec%d" % li] = np.ascontiguousarray(np.broadcast_to(np.asarray(inputs["ssm_d"][j], np.float32)[None, :], (128, SH)))
            shared["ssm_gn%d" % li] = np.ascontiguousarray(np.broadcast_to(np.asarray(inputs["ssm_norm_g"][j], np.float32)[None, :], (128, SI)))
            shared["ssm_w_out%d" % li] = inputs["ssm_w_out"][j]
        if kind == 2:
            shared["attn_w_qkv%d" % li] = inputs["attn_w_qkv"][j]
            shared["attn_w_o%d" % li] = inputs["attn_w_o"][j]
            shared["attn_sinkb%d" % li] = np.ascontiguousarray(np.broadcast_to(np.asarray(inputs["attn_sink"][j], np.float32)[None, :], (128, NH)))
            shared.update(_rope_consts(L))
        if kind == 1:
            shared["pool_w%d" % li] = inputs["pool_w"][j]
            shared["pool_scale%d" % li] = _pl(inputs["pool_scale"][j])
    maps = []
    for b in batch_ids:
        m = dict(shared)
        m["x"] = np.ascontiguousarray(inputs["x"][b, :L])
        m["ctx"] = np.ascontiguousarray(inputs["ctx"][b])
        cv = np.stack([inputs["c"][b], inputs["c_ctx"]], axis=-1)
        m["cvec"] = np.ascontiguousarray(cv.reshape(NDC, 128, 2).transpose(1, 0, 2))
        maps.append(m)
    return maps


def kernel(**inputs):
    B, L = inputs["x"].shape[:2]
    depth = inputs["ada_w"].shape[0]
    layers = [{"idx": i, "kind": i % 3, "last": i == depth - 1} for i in range(depth)]
    nc = build_program(L, layers)
    maps = make_in_maps(inputs, L, list(range(depth)), list(range(B)))
    res = run_bass_kernel_spmd(nc, maps, core_ids=list(range(B)))
    return np.stack([r["out"] for r in res.results], axis=0).astype(np.float32)
```
